# Optimizing a Trainium2 kernel written in Bass

```python
import jax, jax.numpy as jnp
from jax import lax
import numpy as np

D_MODEL = 1024
BATCH = 8
SEQ = 4096
DEPTH = 2

LRU_WIDTH = D_MODEL // 2
LRU_BLOCKS = 8
LRU_BLOCK = LRU_WIDTH // LRU_BLOCKS
CONV_WIDTH = 4
LRU_C = 8.0
RET_HEADS = 8
RET_DIM = 64
RET_WIDTH = RET_HEADS * RET_DIM
RET_CHUNK = 128
RET_THETA = 10000.0
AB_IN = 2 * LRU_WIDTH + 4 * RET_WIDTH
AB_OUT = LRU_WIDTH + RET_WIDTH

NSA_HEADS = 16
NSA_KV_GROUPS = 4
NSA_HEAD_DIM = 64
NSA_REP = NSA_HEADS // NSA_KV_GROUPS
KV_WIDTH = NSA_KV_GROUPS * NSA_HEAD_DIM
CMP_LEN = 32
CMP_STRIDE = 16
SLC_LEN = 64
SLC_TOPK = 16
WINDOW = 512
NSA_QBLOCK = 32
ROPE_THETA = 500000.0
ROT_DIM = NSA_HEAD_DIM // 4
NSA_IN = NSA_HEADS * NSA_HEAD_DIM + 6 * KV_WIDTH + 3 * NSA_HEADS

FFN_HIDDEN = ((8 * D_MODEL // 3 + 255) // 256) * 256

EPS = 1e-6
NEG = -1e30
BIG = 1e30
N_EVEN = (DEPTH + 1) // 2
N_ODD = DEPTH // 2

kernel_name = 'hybrid_rglru_retention_nsa'


def rmsnorm(x, g):
    xf = x.astype(jnp.float32)
    y = xf * lax.rsqrt(jnp.mean(xf * xf, axis=-1, keepdims=True) + EPS)
    return (y * g.astype(jnp.float32)).astype(x.dtype)


def rotary(x, pos, rot_dim, theta):
    half = rot_dim // 2
    inv = theta ** (-jnp.arange(0, rot_dim, 2, dtype=jnp.float32) / rot_dim)
    ang = jnp.asarray(pos).astype(jnp.float32)[:, None] * inv
    cos = jnp.cos(ang)[:, None, :]
    sin = jnp.sin(ang)[:, None, :]
    x1 = x[..., :half].astype(jnp.float32)
    x2 = x[..., half:rot_dim].astype(jnp.float32)
    rot = jnp.concatenate([x1 * cos - x2 * sin, x2 * cos + x1 * sin], axis=-1).astype(x.dtype)
    return jnp.concatenate([rot, x[..., rot_dim:]], axis=-1)


def swiglu(x, w1, w3, w2):
    return (jax.nn.silu(x @ w1) * (x @ w3)) @ w2


def causal_depthwise_conv(x, w, b):
    S = x.shape[1]
    xp = jnp.pad(x, ((0, 0), (CONV_WIDTH - 1, 0), (0, 0)))
    out = b
    for k in range(CONV_WIDTH):
        out = out + xp[:, k:k + S] * w[k]
    return out


def block_diag_linear(x, w, b):
    B, S, _ = x.shape
    xb = x.reshape(B, S, LRU_BLOCKS, LRU_BLOCK)
    return jnp.einsum('bsni,nij->bsnj', xb, w).reshape(B, S, LRU_WIDTH) + b


def rg_lru(x, ga_w, ga_b, gx_w, gx_b, lam):
    r = jax.nn.sigmoid(block_diag_linear(x, ga_w, ga_b)).astype(jnp.float32)
    i = jax.nn.sigmoid(block_diag_linear(x, gx_w, gx_b))
    log_a = -LRU_C * r * jax.nn.softplus(-lam.astype(jnp.float32))
    a = jnp.exp(log_a)
    mult = jnp.sqrt(-jnp.expm1(2.0 * log_a))
    u = mult * (i * x).astype(jnp.float32)

    def combine(left, right):
        a1, b1 = left
        a2, b2 = right
        return a1 * a2, a2 * b1 + b2

    _, h = lax.associative_scan(combine, (a, u), axis=1)
    return h.astype(x.dtype)


def retention(q, k, v):
    B, S, H, d = q.shape
    C = RET_CHUNK
    N = S // C
    pos = jnp.arange(S)
    q = rotary(q, pos, d, RET_THETA)
    k = rotary(k, pos, d, RET_THETA) * (d ** -0.5)
    log_g = jnp.log1p(-(2.0 ** (-5.0 - jnp.arange(H, dtype=jnp.float32))))
    ci = jnp.arange(C, dtype=jnp.float32)
    diff = ci[:, None] - ci[None, :]
    inner_decay = jnp.where(diff >= 0, jnp.exp(jnp.maximum(diff, 0.0) * log_g[:, None, None]), 0.0)
    q_decay = jnp.exp((ci + 1.0) * log_g[:, None])
    k_decay = jnp.exp((C - 1.0 - ci) * log_g[:, None])
    chunk_decay = jnp.exp(C * log_g)

    def to_chunks(t):
        return t.reshape(B, N, C, H, d).transpose(1, 0, 3, 2, 4)

    def step(state, inp):
        qi, ki, vi = inp
        qf, kf, vf = qi.astype(jnp.float32), ki.astype(jnp.float32), vi.astype(jnp.float32)
        s = jnp.einsum('bhid,bhjd->bhij', qf, kf) * inner_decay
        o = jnp.einsum('bhij,bhjd->bhid', s, vf) + jnp.einsum('bhid,bhde->bhie', qf * q_decay[..., None], state)
        state = state * chunk_decay[:, None, None] + jnp.einsum('bhjd,bhje->bhde', kf * k_decay[..., None], vf)
        return state, o

    state0 = jnp.zeros((B, H, d, d), jnp.float32)
    _, o = lax.scan(step, state0, (to_chunks(q), to_chunks(k), to_chunks(v)))
    o = o.transpose(1, 0, 3, 2, 4).reshape(B, S, H, d)
    mu = jnp.mean(o, axis=-1, keepdims=True)
    var = jnp.mean(jnp.square(o - mu), axis=-1, keepdims=True)
    return ((o - mu) * lax.rsqrt(var + EPS)).astype(v.dtype)


def hybrid_ab(h, w_in, conv_w, conv_b, ga_w, ga_b, gx_w, gx_b, lam, w_out):
    B, S, _ = h.shape
    proj = h @ w_in
    L, R = LRU_WIDTH, RET_WIDTH
    y, xr, q, k, v, g = jnp.split(proj, [L, 2 * L, 2 * L + R, 2 * L + 2 * R, 2 * L + 3 * R], axis=-1)
    xr = causal_depthwise_conv(xr, conv_w, conv_b)
    lru_out = rg_lru(xr, ga_w, ga_b, gx_w, gx_b, lam) * jax.nn.gelu(y)
    shp = (B, S, RET_HEADS, RET_DIM)
    ret = retention(q.reshape(shp), k.reshape(shp), v.reshape(shp)).reshape(B, S, R)
    ret_out = ret * jax.nn.silu(g)
    return jnp.concatenate([lru_out, ret_out], axis=-1) @ w_out


def nsa(h, w_in, pe_k, k_w1, k_w2, pe_v, v_w1, v_w2, w_out):
    B, S, _ = h.shape
    H, G, Rr, dh, QB = NSA_HEADS, NSA_KV_GROUPS, NSA_REP, NSA_HEAD_DIM, NSA_QBLOCK
    HD = H * dh
    proj = h @ w_in
    q = proj[..., :HD].reshape(B, S, H, dh)
    kv = proj[..., HD:HD + 6 * KV_WIDTH].reshape(B, S, 6, G, dh)
    k_cmp, v_cmp, k_slc, v_slc, k_win, v_win = [kv[:, :, j] for j in range(6)]
    gates = jax.nn.sigmoid(proj[..., HD + 6 * KV_WIDTH:].reshape(B, S, H, 3))
    pos = jnp.arange(S)
    q = rotary(q, pos, ROT_DIM, ROPE_THETA) * (dh ** -0.5)
    k_slc = rotary(k_slc, pos, ROT_DIM, ROPE_THETA)
    k_win = rotary(k_win, pos, ROT_DIM, ROPE_THETA)

    n_cmp = (S - CMP_LEN) // CMP_STRIDE + 1
    blk_idx = np.arange(n_cmp)[:, None] * CMP_STRIDE + np.arange(CMP_LEN)[None, :]
    cmp_end_np = blk_idx[:, -1]
    cmp_end = jnp.asarray(cmp_end_np)

    def compress(t, pe, w1, w2):
        blocks = t[:, blk_idx] + pe[:, None, :]
        blocks = blocks.transpose(0, 1, 3, 2, 4).reshape(B, n_cmp, G, CMP_LEN * dh)
        return jax.nn.gelu(blocks @ w1) @ w2

    kc = rotary(compress(k_cmp, pe_k, k_w1, k_w2), cmp_end_np, ROT_DIM, ROPE_THETA)
    vc = compress(v_cmp, pe_v, v_w1, v_w2)

    n_slc = S // SLC_LEN
    c_start = blk_idx[:, 0][:, None]
    s_start = np.arange(n_slc)[None, :] * SLC_LEN
    overlap = np.clip(np.minimum(c_start + CMP_LEN, s_start + SLC_LEN) - np.maximum(c_start, s_start), 0, None) / CMP_LEN
    cmp_to_slc = jnp.asarray(overlap, jnp.float32)
    top_k = min(SLC_TOPK, n_slc)
    ks_blocks = k_slc.reshape(B, n_slc, SLC_LEN, G, dh).transpose(0, 3, 1, 2, 4)
    vs_blocks = v_slc.reshape(B, n_slc, SLC_LEN, G, dh).transpose(0, 3, 1, 2, 4)
    gather = jax.vmap(jax.vmap(lambda blocks, idx: blocks[idx]))
    kw_pad = jnp.pad(k_win, ((0, 0), (WINDOW, 0), (0, 0), (0, 0)))
    vw_pad = jnp.pad(v_win, ((0, 0), (WINDOW, 0), (0, 0), (0, 0)))
    blk = jnp.arange(n_slc)

    def attend_block(qi):
        t0 = qi * QB
        qpos = t0 + jnp.arange(QB)
        qb = lax.dynamic_slice_in_dim(q, t0, QB, axis=1).reshape(B, QB, G, Rr, dh)
        s = jnp.einsum('bqgrd,bngd->bgrqn', qb, kc).astype(jnp.float32)
        vis = cmp_end[None, :] <= qpos[:, None]
        p_cmp = jnp.where(vis, jax.nn.softmax(jnp.where(vis, s, NEG), axis=-1), 0.0)
        o_cmp = jnp.einsum('bgrqn,bngd->bqgrd', p_cmp.astype(vc.dtype), vc)
        imp = jnp.einsum('bgrqn,nj->bgqj', p_cmp, cmp_to_slc)
        cur = qpos[:, None] // SLC_LEN
        forced = (blk[None, :] == 0) | (blk[None, :] == cur) | (blk[None, :] == cur - 1)
        causal_blk = blk[None, :] * SLC_LEN <= qpos[:, None]
        imp = jnp.where(forced, BIG, jnp.where(causal_blk, imp, NEG))
        _, sel = lax.top_k(imp, top_k)
        kg = gather(ks_blocks, sel)
        vg = gather(vs_blocks, sel)
        s = jnp.einsum('bqgrd,bgqkld->bgrqkl', qb, kg).astype(jnp.float32)
        tok = sel[..., None] * SLC_LEN + jnp.arange(SLC_LEN)
        ok = tok <= qpos[:, None, None]
        s = jnp.where(ok[:, :, None], s, NEG).reshape(B, G, Rr, QB, top_k * SLC_LEN)
        p_slc = jax.nn.softmax(s, axis=-1).reshape(B, G, Rr, QB, top_k, SLC_LEN)
        o_slc = jnp.einsum('bgrqkl,bgqkld->bqgrd', p_slc.astype(vg.dtype), vg)
        kw = lax.dynamic_slice_in_dim(kw_pad, t0, WINDOW + QB, axis=1)
        vw = lax.dynamic_slice_in_dim(vw_pad, t0, WINDOW + QB, axis=1)
        kpos = t0 - WINDOW + jnp.arange(WINDOW + QB)
        dist = qpos[:, None] - kpos[None, :]
        band = (dist >= 0) & (dist < WINDOW) & (kpos[None, :] >= 0)
        s = jnp.einsum('bqgrd,bkgd->bgrqk', qb, kw).astype(jnp.float32)
        p_win = jax.nn.softmax(jnp.where(band, s, NEG), axis=-1)
        o_win = jnp.einsum('bgrqk,bkgd->bqgrd', p_win.astype(vw.dtype), vw)
        gb = lax.dynamic_slice_in_dim(gates, t0, QB, axis=1).reshape(B, QB, G, Rr, 3)
        o = gb[..., 0:1] * o_cmp + gb[..., 1:2] * o_slc + gb[..., 2:3] * o_win
        return o.reshape(B, QB, HD)

    out = lax.map(attend_block, jnp.arange(S // QB))
    out = out.transpose(1, 0, 2, 3).reshape(B, S, HD)
    return out @ w_out


def setup_inputs(seed: int = 0) -> dict:
    key = jax.random.key(seed)
    ks = jax.random.split(key, 24)
    f32 = jnp.float32

    def w(k, shape, fan_in):
        return jax.random.normal(k, shape, f32) * (fan_in ** -0.5)

    def gain(k, shape):
        return 1.0 + 0.01 * jax.random.normal(k, shape, f32)

    a_pow_c = jax.random.uniform(ks[9], (N_EVEN, LRU_WIDTH), f32, 0.9, 0.999)
    root = a_pow_c ** (1.0 / LRU_C)
    lru_lambda = jnp.log(root) - jnp.log1p(-root)
    return {
        'x': jax.random.normal(ks[0], (BATCH, SEQ, D_MODEL), f32),
        'attn_norm': gain(ks[1], (DEPTH, D_MODEL)),
        'ab_w_in': w(ks[2], (N_EVEN, D_MODEL, AB_IN), D_MODEL),
        'conv_w': w(ks[3], (N_EVEN, CONV_WIDTH, LRU_WIDTH), CONV_WIDTH),
        'conv_b': 0.01 * jax.random.normal(ks[4], (N_EVEN, LRU_WIDTH), f32),
        'gate_a_w': w(ks[5], (N_EVEN, LRU_BLOCKS, LRU_BLOCK, LRU_BLOCK), LRU_BLOCK),
        'gate_a_b': 0.01 * jax.random.normal(ks[6], (N_EVEN, LRU_WIDTH), f32),
        'gate_x_w': w(ks[7], (N_EVEN, LRU_BLOCKS, LRU_BLOCK, LRU_BLOCK), LRU_BLOCK),
        'gate_x_b': 0.01 * jax.random.normal(ks[8], (N_EVEN, LRU_WIDTH), f32),
        'lru_lambda': lru_lambda,
        'ab_w_out': w(ks[10], (N_EVEN, AB_OUT, D_MODEL), AB_OUT),
        'nsa_w_in': w(ks[11], (N_ODD, D_MODEL, NSA_IN), D_MODEL),
        'cmp_pe_k': 0.02 * jax.random.normal(ks[12], (N_ODD, CMP_LEN, NSA_HEAD_DIM), f32),
        'cmp_k_w1': w(ks[13], (N_ODD, CMP_LEN * NSA_HEAD_DIM, NSA_HEAD_DIM), CMP_LEN * NSA_HEAD_DIM),
        'cmp_k_w2': w(ks[14], (N_ODD, NSA_HEAD_DIM, NSA_HEAD_DIM), NSA_HEAD_DIM),
        'cmp_pe_v': 0.02 * jax.random.normal(ks[15], (N_ODD, CMP_LEN, NSA_HEAD_DIM), f32),
        'cmp_v_w1': w(ks[16], (N_ODD, CMP_LEN * NSA_HEAD_DIM, NSA_HEAD_DIM), CMP_LEN * NSA_HEAD_DIM),
        'cmp_v_w2': w(ks[17], (N_ODD, NSA_HEAD_DIM, NSA_HEAD_DIM), NSA_HEAD_DIM),
        'nsa_w_out': w(ks[18], (N_ODD, NSA_HEADS * NSA_HEAD_DIM, D_MODEL), NSA_HEADS * NSA_HEAD_DIM),
        'ffn_norm': gain(ks[19], (DEPTH, D_MODEL)),
        'ffn_w1': w(ks[20], (DEPTH, D_MODEL, FFN_HIDDEN), D_MODEL),
        'ffn_w3': w(ks[21], (DEPTH, D_MODEL, FFN_HIDDEN), D_MODEL),
        'ffn_w2': w(ks[22], (DEPTH, FFN_HIDDEN, D_MODEL), FFN_HIDDEN),
        'final_norm': gain(ks[23], (D_MODEL,)),
    }


def reference(x, attn_norm, ab_w_in, conv_w, conv_b, gate_a_w, gate_a_b, gate_x_w, gate_x_b,
              lru_lambda, ab_w_out, nsa_w_in, cmp_pe_k, cmp_k_w1, cmp_k_w2, cmp_pe_v, cmp_v_w1,
              cmp_v_w2, nsa_w_out, ffn_norm, ffn_w1, ffn_w3, ffn_w2, final_norm):
    h = x
    for layer in range(DEPTH):
        hn = rmsnorm(h, attn_norm[layer])
        i = layer // 2
        if layer % 2 == 0:
            h = h + hybrid_ab(hn, ab_w_in[i], conv_w[i], conv_b[i], gate_a_w[i], gate_a_b[i],
                              gate_x_w[i], gate_x_b[i], lru_lambda[i], ab_w_out[i])
        else:
            h = h + nsa(hn, nsa_w_in[i], cmp_pe_k[i], cmp_k_w1[i], cmp_k_w2[i],
                        cmp_pe_v[i], cmp_v_w1[i], cmp_v_w2[i], nsa_w_out[i])
        h = h + swiglu(rmsnorm(h, ffn_norm[layer]), ffn_w1[layer], ffn_w3[layer], ffn_w2[layer])
    return rmsnorm(h, final_norm)
```

```python
import numpy as np
from contextlib import ExitStack
import concourse.bass as bass
import concourse.mybir as mybir
from concourse.bass_utils import run_bass_kernel_spmd

F32 = mybir.dt.float32
BF16 = mybir.dt.bfloat16
F32R = mybir.dt.float32r
AF = mybir.ActivationFunctionType
ALU = mybir.AluOpType
AX = mybir.AxisListType


class Prog:
    CE = ('pe', 'act', 'dve', 'pool')
    ALL = ('pe', 'act', 'dve', 'pool', 'sp')

    def __init__(self):
        self.nc = bass.Bass("TRN2", target_bir_lowering=False)
        self.es = ExitStack()
        self.streams = {e: [] for e in self.ALL}
        self.cnt = {}
        self.semh = {}
        for e in self.CE:
            self.semh['tl_' + e] = self.nc.alloc_semaphore(name='tl_' + e)
            self.cnt['tl_' + e] = 0
        self.seen = {e: {} for e in self.ALL}
        self.track = {}
        self.n_ops = 0

    def dram(self, name, shape, dtype, kind):
        return self.nc.dram_tensor(name, list(shape), dtype, kind=kind).ap()

    def sb(self, name, shape, dtype):
        return self.es.enter_context(self.nc.sbuf_tensor(name, list(shape), dtype))

    def ps(self, name, shape, dtype=F32):
        return self.es.enter_context(self.nc.psum_tensor(name, list(shape), dtype))

    def _t(self, k):
        t = self.track.get(k)
        if t is None:
            t = self.track[k] = {'w': None, 'r': {}}
        return t

    def _waits(self, eng, reads, writes):
        waits = {}

        def need(s, v):
            if waits.get(s, 0) < v:
                waits[s] = v
        for k in reads:
            t = self._t(k)
            if t['w'] is not None:
                need(*t['w'])
        for k in writes:
            t = self._t(k)
            if t['w'] is not None:
                need(*t['w'])
            for s, v in t['r'].items():
                need(s, v)
        out = []
        seen = self.seen[eng]
        for s, v in waits.items():
            if eng == 'pe' and s == 'tl_pe':
                continue
            if seen.get(s, 0) < v:
                seen[s] = v
                out.append((s, v))
        return out

    def _mark(self, tok, reads, writes):
        s, v = tok
        for k in reads:
            t = self._t(k)
            if t['r'].get(s, 0) < v:
                t['r'][s] = v
        for k in writes:
            t = self._t(k)
            t['w'] = tok
            t['r'] = {}

    def op(self, eng, fn, r=(), w=()):
        wl = self._waits(eng, r, w)
        s = 'tl_' + eng
        self.cnt[s] += 1
        v = self.cnt[s]
        self.seen[eng][s] = max(self.seen[eng].get(s, 0), 0)
        semh = self.semh

        def emit(e):
            for ws, wv in wl:
                e.wait_ge(semh[ws], wv)
            fn(e).then_inc(semh[s], 1)
        self.streams[eng].append(emit)
        self._mark((s, v), r, w)
        self.n_ops += 1

    def dma(self, q, out, in_, r=(), w=(), sem=None):
        assert sem is not None
        s = 'd_' + str(sem)
        if s not in self.semh:
            self.semh[s] = self.nc.alloc_semaphore(name=s.replace(' ', '').replace(',', '_').replace('(', '').replace(')', '').replace("'", ''))
            self.cnt[s] = 0
        wl = self._waits(q, r, w)
        self.cnt[s] += 16
        v = self.cnt[s]
        semh = self.semh

        def emit(e):
            for ws, wv in wl:
                e.wait_ge(semh[ws], wv)
            e.dma_start(out=out, in_=in_).then_inc(semh[s], 16)
        self.streams[q].append(emit)
        self._mark((s, v), r, w)
        self.n_ops += 1

    def finish(self):
        nc = self.nc
        final = [(s, v) for s, v in self.cnt.items() if v > 0]
        semh = self.semh

        def fin(e):
            for s, v in final:
                e.wait_ge(semh[s], v)
        self.streams['sp'].append(fin)
        streams = self.streams
        with nc.Block() as block:
            @block.sync
            def _(e):
                for f in streams['sp']:
                    f(e)

            @block.scalar
            def _(e):
                for f in streams['act']:
                    f(e)

            @block.vector
            def _(e):
                for f in streams['dve']:
                    f(e)

            @block.gpsimd
            def _(e):
                for f in streams['pool']:
                    f(e)

            @block.tensor
            def _(e):
                for f in streams['pe']:
                    f(e)
        self.es.close()
        return nc
D = 1024; S = 4096; NCH = 8; TC = 512
FFH = 2816
LOG2 = np.log(2.0)


def lay_fm(W):
    K, M = W.shape
    return np.ascontiguousarray(W.reshape(K // 128, 128, M // 128, 128).transpose(1, 2, 0, 3))


def host_prep(inp):
    f32 = np.float32
    W = {}
    w_in = inp['ab_w_in'][0]
    y_w, xr_w = w_in[:, 0:512], w_in[:, 512:1024]
    q_w, k_w, v_w, g_w = (w_in[:, 1024 + i * 512:1024 + (i + 1) * 512] for i in range(4))
    perm = np.arange(512).reshape(8, 64)
    perm = np.concatenate([perm[:, 32:], perm[:, :32]], axis=1).reshape(-1)
    ext0 = np.concatenate([y_w, xr_w, q_w, q_w[:, perm], k_w, k_w[:, perm], g_w, v_w], axis=1)
    W['in0'] = lay_fm(ext0)
    ga = np.zeros((4, 128, 128), f32); gx = np.zeros((4, 128, 128), f32)
    for b in range(8):
        c, o = b // 2, (b % 2) * 64
        ga[c, o:o + 64, o:o + 64] = inp['gate_a_w'][0, b]
        gx[c, o:o + 64, o:o + 64] = inp['gate_x_w'][0, b]
    W['ga'] = np.ascontiguousarray(ga.transpose(1, 0, 2))
    W['gx'] = np.ascontiguousarray(gx.transpose(1, 0, 2))
    W['out0'] = lay_fm(inp['ab_w_out'][0])
    def ffw(l):
        W['w1_%d' % l] = lay_fm(inp['ffn_w1'][l])
        W['w3_%d' % l] = lay_fm(inp['ffn_w3'][l])
        W['w2_%d' % l] = lay_fm(inp['ffn_w2'][l])
    ffw(0)
    wn = inp['nsa_w_in'][0]
    qn = wn[:, 0:1024]
    kvw = wn[:, 1024:1024 + 1536].reshape(1024, 6, 4, 64)
    gw = wn[:, 2560:2608]
    p16 = np.arange(64); p16[0:8] = np.arange(8, 16); p16[8:16] = np.arange(0, 8)
    permq = (np.arange(16)[:, None] * 64 + p16[None, :]).reshape(-1)
    cols = [qn, qn[:, permq]]
    for g in range(4):
        for j in (2, 4):
            kk = kvw[:, j, g]; ks = kk[:, p16]
            cols += [np.concatenate([kk, kk], 1), np.concatenate([ks, ks], 1)]
        cols.append(np.concatenate([kvw[:, 3, g], kvw[:, 5, g]], 1))
        cols.append(np.concatenate([kvw[:, 0, g], kvw[:, 1, g]], 1))
    cols.append(np.concatenate([gw, np.zeros((1024, 80), f32)], 1))
    ext1 = np.concatenate(cols, axis=1)
    W['in1'] = lay_fm(ext1)
    W['out1'] = lay_fm(inp['nsa_w_out'][0])
    w1k = inp['cmp_k_w1'][0].reshape(32, 64, 64).transpose(1, 0, 2)
    w1v = inp['cmp_v_w1'][0].reshape(32, 64, 64).transpose(1, 0, 2)
    W['cw1'] = np.ascontiguousarray(np.concatenate([w1k, w1v], 0))
    w2k = inp['cmp_k_w2'][0]; w2v = inp['cmp_v_w2'][0]
    top = np.zeros((64, 4, 128), f32); bot = np.zeros((64, 4, 128), f32)
    top[:, 0] = np.concatenate([w2k, w2k], 1); top[:, 1] = np.concatenate([w2k[:, p16], w2k[:, p16]], 1)
    bot[:, 2, 0:64] = w2v
    W['cw2'] = np.ascontiguousarray(np.concatenate([top, bot], 0))
    ffw(1)
    offs = {}; parts = []; o = 0
    for n, a in W.items():
        a2 = a.reshape(128, -1).astype(f32)
        offs[n] = (o, a2.shape[1], a.shape[1:]); parts.append(a2); o += a2.shape[1]
    pad = (-o) % 2048
    if pad:
        parts.append(np.zeros((128, pad), f32)); o += pad
    wcat = np.ascontiguousarray(np.concatenate(parts, axis=1))

    def col4(v):
        return np.ascontiguousarray(v.reshape(4, 128).T)

    def col8(v):
        return np.ascontiguousarray(v.reshape(8, 128).T)
    sp = {}
    sp['g_attn0'] = col8(inp['attn_norm'][0]); sp['g_attn1'] = col8(inp['attn_norm'][1])
    sp['g_ffn0'] = col8(inp['ffn_norm'][0]); sp['g_ffn1'] = col8(inp['ffn_norm'][1])
    sp['g_fin'] = col8(inp['final_norm'])
    for i in range(4):
        sp['cw%d' % i] = col4(inp['conv_w'][0, i])
    sp['cb'] = col4(inp['conv_b'][0]); sp['gab'] = col4(inp['gate_a_b'][0]); sp['gxb'] = col4(inp['gate_x_b'][0])
    sp['lam'] = col4(inp['lru_lambda'][0])
    pek = inp['cmp_pe_k'][0].T; pev = inp['cmp_pe_v'][0].T
    sp['pe'] = np.ascontiguousarray(np.concatenate([pek, pev], 0))
    soffs = {}; sparts = []; o = 0
    for n, a in sp.items():
        soffs[n] = (o, a.shape[1]); sparts.append(a.astype(f32)); o += a.shape[1]
    spcat = np.ascontiguousarray(np.concatenate(sparts, 1))
    return wcat, offs, spcat, soffs


def const_tables():
    f32 = np.float32
    C = {}
    pos = np.arange(S, dtype=np.float64)
    inv = 10000.0 ** (-np.arange(0, 64, 2, dtype=np.float64) / 64)
    d = np.arange(128) % 64
    ang = pos[None, :] * inv[d % 32][:, None]
    sgn = np.where(d < 32, -1.0, 1.0)[:, None]
    C['rcos'] = np.cos(ang).astype(f32); C['rsin'] = (np.sin(ang) * sgn).astype(f32)
    C['rcosk'] = (np.cos(ang) * 0.125).astype(f32); C['rsink'] = (np.sin(ang) * sgn * 0.125).astype(f32)
    hh = np.arange(8, dtype=np.float64)
    log_g = np.log1p(-(2.0 ** (-5.0 - hh)))
    ci = np.arange(128, dtype=np.float64)
    qdec = np.zeros((4, 128, 512)); kdec = np.zeros((128, 8)); dm = np.zeros((4, 128, 2, 128)); cdec = np.zeros((128, 4))
    for h in range(8):
        j, o = h // 2, (h % 2) * 64
        qdec[j, o:o + 64, :] = np.tile(np.exp((ci + 1.0) * log_g[h]), 4)[None, :]
        kdec[:, h] = np.exp((127.0 - ci) * log_g[h])
        diff = ci[None, :] - ci[:, None]
        dm[j, :, h % 2, :] = np.where(diff >= 0, np.exp(np.maximum(diff, 0) * log_g[h]), 0.0)
        cdec[o:o + 64, j] = np.exp(128.0 * log_g[h])
    C['qdec'] = qdec.astype(f32); C['kdec'] = kdec.astype(f32); C['dm'] = dm.astype(f32); C['cdec'] = cdec.astype(f32)
    avg = np.zeros((128, 128)); avg[0:64, 0:64] = 1 / 64; avg[64:, 64:] = 1 / 64
    C['avg'] = avg.astype(f32)
    C['ones'] = np.ones((128, 128), f32)
    C['ident'] = np.eye(128, dtype=f32)
    inv2 = 500000.0 ** (-np.arange(0, 16, 2, dtype=np.float64) / 16)
    fi = np.where(d < 16, d % 8, 0)
    ang2 = pos[None, :] * inv2[fi][:, None]
    rot = (d < 16)[:, None]
    sg2 = np.where(d < 8, -1.0, 1.0)[:, None]
    ncos = np.where(rot, np.cos(ang2), 1.0); nsin = np.where(rot, np.sin(ang2) * sg2, 0.0)
    C['ncosq'] = (ncos * 0.125).astype(f32); C['nsinq'] = (nsin * 0.125).astype(f32)
    C['ncos'] = ncos.astype(f32); C['nsin'] = nsin.astype(f32)
    cend = (np.arange(255) * 16 + 31).astype(np.float64)
    angc = cend[None, :] * inv2[fi][:, None]
    cc = np.where(rot, np.cos(angc), 1.0); cs = np.where(rot, np.sin(angc) * sg2, 0.0)
    ccp = np.zeros((128, 256)); csp = np.zeros((128, 256)); ccp[:, :255] = cc; csp[:, :255] = cs
    C['ccos'] = ccp.astype(f32); C['csin'] = csp.astype(f32)
    n = np.arange(256)
    vis = ((n[:, None] * 16 + 31) <= pos[None, :]) & (n[:, None] < 255)
    C['vis'] = np.ascontiguousarray(vis.reshape(2, 128, S).transpose(1, 0, 2)).astype(f32)
    cst = np.arange(255)[:, None] * 16; sst = np.arange(64)[None, :] * 64
    ov = np.clip(np.minimum(cst + 32, sst + 64) - np.maximum(cst, sst), 0, None) / 32.0
    ovp = np.zeros((256, 64)); ovp[:255] = ov
    C['c2s'] = np.ascontiguousarray(ovp.reshape(2, 128, 64).transpose(1, 0, 2)).astype(f32)
    q = np.arange(S)[:, None]; blk = np.arange(64)[None, :]
    cur = q // 64
    causal = (blk * 64 <= q)
    Am = np.where(causal, 0.0, -1e30)
    Am = np.where(blk == 0, 1e30, Am)
    Am = np.where((blk == cur - 1) & (blk != 0), 2e30, Am)
    Am = np.where((blk == cur) & (blk != 0), 3e30, Am)
    Cm = (causal & ~((blk == 0) | (blk == cur) | (blk == cur - 1))).astype(np.float64)
    C['tkC'] = np.ascontiguousarray(Cm.reshape(32, 128, 64).transpose(1, 0, 2)).astype(f32)
    C['tkA'] = np.ascontiguousarray(Am.reshape(32, 128, 64).transpose(1, 0, 2)).astype(f32)
    kk = np.arange(128)[:, None]; qq = np.arange(128)[None, :]
    C['caus'] = (kk <= qq).astype(f32)
    C['wlow'] = (kk > qq).astype(f32)
    sel = np.zeros((128, 16, 3, 64))
    for h in range(16):
        for b in range(3):
            sel[h * 3 + b, h, b, :] = 1.0
    C['gsel'] = sel.astype(f32)
    return C


def const_tables2(C):
    f32 = np.float32
    kk = np.arange(128)[:, None, None]; o = np.arange(8)[None, :, None]; qc = np.arange(512)[None, None, :]
    dist = qc - ((o - 4) * 128 + kk)
    o2 = np.arange(2)[None, :, None]; q2 = np.arange(256)[None, None, :]
    C['cm2'] = ((o2 * 128 + kk) <= q2).astype(f32)
    o6 = np.arange(6)[None, :, None]
    d6 = q2 - ((o6 - 4) * 128 + kk)
    C['wm2'] = ((d6 >= 0) & (d6 < 512)).astype(f32)
    return C
EPS = 1e-6


class Ctx:
    pass


def mm(k, out, lhsT, rhs, start, stop, r, w):
    k.op('pe', lambda e: e.matmul(out, lhsT=lhsT, rhs=rhs, start=start, stop=stop), r=r, w=w)


def act(k, out, in_, func, r, w, bias=None, scale=None):
    kw = {}
    if bias is not None:
        kw['bias'] = bias
    if scale is not None:
        kw['scale'] = scale
    k.op('act', lambda e: e.activation(out=out, in_=in_, func=func, **kw), r=r, w=w)


def tt(k, eng, out, in0, in1, op, r, w):
    k.op(eng, lambda e: e.tensor_tensor(out=out, in0=in0, in1=in1, op=op), r=r, w=w)


def ts(k, eng, out, in0, s1, s2, op0, op1, r, w):
    if op1 is None:
        k.op(eng, lambda e: e.tensor_scalar(out=out, in0=in0, scalar1=s1, scalar2=None, op0=op0), r=r, w=w)
    else:
        k.op(eng, lambda e: e.tensor_scalar(out=out, in0=in0, scalar1=s1, scalar2=s2, op0=op0, op1=op1), r=r, w=w)


def stt(k, out, in0, sc, in1, op0, op1, r, w):
    k.op('dve', lambda e: e.scalar_tensor_tensor(out=out, in0=in0, scalar=sc, in1=in1, op0=op0, op1=op1), r=r, w=w)


def cp(k, eng, out, in_, r, w):
    if eng == 'act':
        k.op('act', lambda e: e.activation(out=out, in_=in_, func=AF.Copy), r=r, w=w)
    else:
        k.op(eng, lambda e: e.tensor_copy(out=out, in_=in_), r=r, w=w)


def barrier(k):
    final = [(s, v) for s, v in k.cnt.items() if v > 0 and not s.startswith("d_('wbs'")]
    semh = k.semh
    for eng in k.ALL:
        wl = [(s, v) for s, v in final if k.seen[eng].get(s, 0) < v and not (s == 'tl_' + eng)]
        for s, v in wl:
            k.seen[eng][s] = v

        def emit(e, wl=wl):
            for s, v in wl:
                e.wait_ge(semh[s], v)
        k.streams[eng].append(emit)
    k.track = {kk: vv for kk, vv in k.track.items() if isinstance(kk, tuple) and kk[0] == 'wb'}


def load_w(k, c, dst, name, lo, n, key, q='sp'):
    o = c.offs[name][0]
    k.dma(q, dst, c.WB[:, o + lo:o + lo + n], r=[('wb', name)], w=[key], sem=key)


def norm_chunk(k, c, src, srckey, gcol, dst_fn, dstkeys, ps, pskey):
    sq = c.sq
    act(k, sq[:].bitcast(F32R), src, AF.Square, r=[srckey], w=['sq'])
    for kc in range(8):
        mm(k, ps[:, :], c.ones32[:].bitcast(F32R), sq[:, kc, :].bitcast(F32R), kc == 0, kc == 7, r=['sq', 'ones32'], w=[pskey])
    act(k, c.rs[:], ps[:, :], AF.Ln, r=[pskey], w=['rs'], scale=1.0 / 1024, bias=c.epsc[:, 0:1])
    act(k, c.rs[:], c.rs[:], AF.Exp, r=['rs'], w=['rs'], scale=-0.5)
    for kc in range(8):
        stt(k, dst_fn(kc), src[:, kc, :], gcol[:, kc:kc + 1], c.rs[:], ALU.mult, ALU.mult,
            r=[srckey, 'rs', 'sp'], w=[dstkeys[kc] if isinstance(dstkeys, list) else dstkeys])


def phase_weights(k, c, names):
    CH = 4096
    for name in names:
        o, n, _ = c.offs[name]
        i = 0
        for lo in range(0, n, CH):
            hi = min(n, lo + CH)
            k.dma('pool', c.WB[:, o + lo:o + hi], c.WC[:, o + lo:o + hi], w=[('wbc', name, i)], sem=('wbs', name))
            i += 1
        k.track[('wb', name)] = {'w': ("d_" + str(('wbs', name)), k.cnt["d_" + str(('wbs', name))]), 'r': {}}


def phase_consts(k, c):
    k.dma('sp', c.spt[:], c.SPD[:, :], w=['sp'], sem='sp')
    k.dma('sp', c.onesf[:], c.CD['ones'][:, :], w=['onesf'], sem='onesf')
    act(k, c.ones32[:].bitcast(F32R), c.onesf[:], AF.Copy, r=['onesf'], w=['ones32'])
    k.op('dve', lambda e: e.memset(c.epsc[:], EPS), w=['epsc'])


def phase_norm0(k, c, SRC, gname):
    src_v = SRC.rearrange("(kc p) s -> p kc s", p=128)
    g = c.sp(gname)
    for ch in range(NCH):
        sl = ch % 2
        k.dma('sp', c.xc[sl][:], src_v[:, :, ch * TC:(ch + 1) * TC], w=[('xc', sl)], sem=('xc', sl))
        norm_chunk(k, c, c.xc[sl][:], ('xc', sl), g, lambda kc: c.hn[:, kc, ch * TC:(ch + 1) * TC],
                   ('hn', ch), c.pst[6], 'ps6')


def phase_outproj(k, c, wname, RES, HOUT, gname):
    res_v = RES.rearrange("(kc p) s -> p kc s", p=128)
    out_v = HOUT.rearrange("(kc p) s -> p kc s", p=128)
    ot_v = c.OT.rearrange("(kc p) s -> p kc s", p=128)
    load_w(k, c, c.wout[:].rearrange("p a b m -> p (a b m)"), wname, 0, 8 * 8 * 128, 'wout')
    g = c.sp(gname)
    for ch in range(NCH):
        sl = ch % 2
        cs = slice(ch * TC, (ch + 1) * TC)
        k.dma('sp', c.oc[sl][:], ot_v[:, :, cs], w=[('oc', sl)], sem=('oc', sl))
        k.dma('sp', c.xc[sl][:], res_v[:, :, cs], w=[('xc', sl)], sem=('xc', sl))
        for m in range(8):
            ps = c.pst[m % 4]; pk = 'ps%d' % (m % 4)
            for kc in range(8):
                mm(k, ps[:, :], c.wout[:, m, kc, :], c.oc[sl][:, kc, :], kc == 0, kc == 7, r=['wout', ('oc', sl)], w=[pk])
            tt(k, 'dve', c.xc[sl][:, m, :], ps[:, :], c.xc[sl][:, m, :], ALU.add, r=[pk, ('xc', sl)], w=[('xc', sl)])
        k.dma('sp', out_v[:, :, cs], c.xc[sl][:], r=[('xc', sl)], w=[], sem=('xcs', sl))
        norm_chunk(k, c, c.xc[sl][:], ('xc', sl), g, lambda kc: c.hn[:, kc, cs], ('hn', ch), c.pst[6], 'ps6')


def phase_ffn(k, c, l, RES, HOUT, gname, final):
    res_v = RES.rearrange("(kc p) s -> p kc s", p=128)
    out_v = HOUT.rearrange("(kc p) s -> p kc s", p=128)
    g = c.sp(gname)
    TS = 1024
    for sc in range(S // TS):
        for j in range(22):
            sl = j % 2
            load_w(k, c, c.w13[sl][:, 0].rearrange("p kc m -> p (kc m)"), 'w1_%d' % l, j * 1024, 1024, ('w1', sl))
            load_w(k, c, c.w13[sl][:, 1].rearrange("p kc m -> p (kc m)"), 'w3_%d' % l, j * 1024, 1024, ('w3', sl))
            for hf in range(2):
                cs = slice(sc * TS + hf * 512, sc * TS + (hf + 1) * 512)
                ch = (sc * TS + hf * 512) // TC
                pa = c.pst[2 * hf]; pb = c.pst[2 * hf + 1]; ka = 'ps%d' % (2 * hf); kb = 'ps%d' % (2 * hf + 1)
                for kc in range(8):
                    mm(k, pa[:, :], c.w13[sl][:, 0, kc, :], c.hn[:, kc, cs], kc == 0, kc == 7, r=[('w1', sl), ('hn', ch)], w=[ka])
                for kc in range(8):
                    mm(k, pb[:, :], c.w13[sl][:, 1, kc, :], c.hn[:, kc, cs], kc == 0, kc == 7, r=[('w3', sl), ('hn', ch)], w=[kb])
                act(k, c.sil[hf][:], pa[:, :], AF.Silu, r=[ka], w=[('sil', hf)])
                tt(k, 'dve', c.hid[:, j, hf * 512:(hf + 1) * 512], c.sil[hf][:], pb[:, :], ALU.mult, r=[('sil', hf), kb], w=[('hid', j, hf)])
        for hf in range(2):
            sl = hf
            cs = slice(sc * TS + hf * 512, sc * TS + (hf + 1) * 512)
            k.dma('sp', c.xc[sl][:], res_v[:, :, cs], w=[('xc', sl)], sem=('xc', sl))
        for m in range(8):
            sl = m % 2
            load_w(k, c, c.w2t[sl][:].rearrange("p kc m -> p (kc m)"), 'w2_%d' % l, m * 22 * 128, 22 * 128, ('w2', sl))
            for hf in range(2):
                ps = c.pst[4 + hf]; pk = 'ps%d' % (4 + hf)
                for j in range(22):
                    mm(k, ps[:, :], c.w2t[sl][:, j, :], c.hid[:, j, hf * 512:(hf + 1) * 512], j == 0, j == 21,
                       r=[('w2', sl), ('hid', j, hf)], w=[pk])
                tt(k, 'dve', c.xc[hf][:, m, :], ps[:, :], c.xc[hf][:, m, :], ALU.add, r=[pk, ('xc', hf)], w=[('xc', hf)])
        for hf in range(2):
            cs = slice(sc * TS + hf * 512, sc * TS + (hf + 1) * 512)
            ch = (sc * TS + hf * 512) // TC
            if not final:
                k.dma('sp', out_v[:, :, cs], c.xc[hf][:], r=[('xc', hf)], w=[], sem=('xcs', hf))
                norm_chunk(k, c, c.xc[hf][:], ('xc', hf), g, lambda kc: c.hn[:, kc, cs], ('hn', ch), c.pst[6], 'ps6')
            else:
                norm_chunk(k, c, c.xc[hf][:], ('xc', hf), g, lambda kc: c.xc[hf][:, kc, :], ('xc', hf), c.pst[6], 'ps6')
                k.dma('sp', out_v[:, :, cs], c.xc[hf][:], r=[('xc', hf)], w=[], sem=('xcs', hf))


def phase_mix0(k, c):
    hn = c.hn
    T = 1024
    lam = c.sp('lam')
    act(k, c.cv[:, 0:4], lam, AF.Exp, r=['sp'], w=['cv'], scale=-1.0)
    act(k, c.cv[:, 0:4], c.cv[:, 0:4], AF.Ln, r=['cv'], w=['cv'], bias=c.onec[:, 0:1], scale=1.0)
    ts(k, 'dve', c.cv[:, 4:8], c.cv[:, 0:4], -16.0, None, ALU.mult, None, r=['cv'], w=['cv2'])
    ts(k, 'dve', c.cv[:, 0:4], c.cv[:, 0:4], -8.0, None, ALU.mult, None, r=['cv', 'cv2'], w=['cv'])
    k.dma('sp', c.avg[:], c.CD['avg'][:, :], w=['avg'], sem='avg')
    k.dma('sp', c.kdec[:], c.CD['kdec'][:, :], w=['kdec'], sem='kdec')
    k.dma('sp', c.cdec[:], c.CD['cdec'][:, :], w=['cdec'], sem='cdec')
    k.dma('sp', c.identf[:], c.CD['ident'][:, :], w=['identf'], sem='identf')
    cp(k, 'dve', c.identb[:], c.identf[:], r=['identf'], w=['identb'])
    load_w(k, c, c.gaw[:].rearrange("p a m -> p (a m)"), 'ga', 0, 512, 'gaw')
    load_w(k, c, c.gxw[:].rearrange("p a m -> p (a m)"), 'gx', 0, 512, 'gxw')
    k.op('pool', lambda e: e.memset(c.KRA[:], 0.0), w=['KRA'])
    k.op('pool', lambda e: e.memset(c.KRB[:], 0.0), w=['KRB'])
    WY, WX, WQ, WQS, WK, WKS, WG, WV = range(8)
    for j in range(4):
        for typ in range(8):
            load_w(k, c, c.gw[:, typ].rearrange("p kc m -> p (kc m)"), 'in0', (typ * 4 + j) * 1024, 1024, ('gw', typ))
        k.dma('sp', c.qdec[:], c.CD['qdec'][j], w=['qdec'], sem='qdec')
        k.dma('sp', c.dm[:], c.CD['dm'][j], w=['dm'], sem='dm')

        def proj(typ, ch, ps, pk):
            for kc in range(8):
                mm(k, ps[:, :], c.gw[:, typ, kc, :], hn[:, kc, ch * TC:(ch + 1) * TC], kc == 0, kc == 7,
                   r=[('gw', typ), ('hn', ch)], w=[pk])
        def lru_stages(tb):
            def s0():
                if tb == 0:
                    k.op('dve', lambda e: e.memset(c.XR[:, 0:3], 0.0), w=['XR'])
                else:
                    cp(k, 'dve', c.XR[:, 0:3], c.XR[:, T:T + 3], r=['XR'], w=['XR'])
                for cc in range(2):
                    ch = tb * 2 + cc
                    proj(WY, ch, c.pst[0], 'ps0')
                    cp(k, 'act', c.Y[:, cc * 512:(cc + 1) * 512], c.pst[0][:, :], r=['ps0'], w=['Y'])
                    proj(WX, ch, c.pst[1], 'ps1')
                    cp(k, 'act', c.XR[:, 3 + cc * 512:3 + (cc + 1) * 512], c.pst[1][:, :], r=['ps1'], w=['XR'])

            def s1():
                ts(k, 'dve', c.XC[:], c.XR[:, 3:3 + T], c.sp('cw3')[:, j:j + 1], c.sp('cb')[:, j:j + 1], ALU.mult, ALU.add,
                   r=['XR', 'sp'], w=['XC'])
                for i in range(3):
                    stt(k, c.XC[:], c.XR[:, i:i + T], c.sp('cw%d' % i)[:, j:j + 1], c.XC[:], ALU.mult, ALU.add,
                        r=['XR', 'XC', 'sp'], w=['XC'])
                cp(k, 'act', c.XCb[:], c.XC[:], r=['XC'], w=['XCb'])

            def s2():
                for cc in range(2):
                    cs = slice(cc * 512, (cc + 1) * 512)
                    mm(k, c.pst[2][:, :], c.gaw[:, j, :], c.XCb[:, cs], True, True, r=['gaw', 'XCb'], w=['ps2'])
                    act(k, c.R[:, cs], c.pst[2][:, :], AF.Sigmoid, r=['ps2', 'sp'], w=['R'], bias=c.sp('gab')[:, j:j + 1], scale=1.0)
                    mm(k, c.pst[3][:, :], c.gxw[:, j, :], c.XCb[:, cs], True, True, r=['gxw', 'XCb'], w=['ps3'])
                    act(k, c.I[:, cs], c.pst[3][:, :], AF.Sigmoid, r=['ps3', 'sp'], w=['I'], bias=c.sp('gxb')[:, j:j + 1], scale=1.0)

            def s3():
                act(k, c.A[:], c.R[:], AF.Exp, r=['R', 'cv'], w=['A'], scale=c.cv[:, j:j + 1])
                act(k, c.R[:], c.R[:], AF.Exp, r=['R', 'cv2'], w=['R'], scale=c.cv[:, 4 + j:5 + j])
                act(k, c.R[:], c.R[:], AF.Sqrt, r=['R'], w=['R'], scale=-1.0, bias=c.onec[:, 0:1])

            def s4():
                tt(k, 'dve', c.I[:], c.I[:], c.R[:], ALU.mult, r=['I', 'R'], w=['I'])
                tt(k, 'dve', c.I[:], c.I[:], c.XC[:], ALU.mult, r=['I', 'XC'], w=['I'])
                if tb == 0:
                    k.op('dve', lambda e: e.tensor_tensor_scan(out=c.XC[:], data0=c.A[:], data1=c.I[:], initial=0.0,
                                                               op0=ALU.mult, op1=ALU.add), r=['A', 'I', 'XC'], w=['XC'])
                else:
                    k.op('dve', lambda e: e.tensor_tensor_scan(out=c.XC[:], data0=c.A[:], data1=c.I[:], initial=c.hl[:, 0:1],
                                                               op0=ALU.mult, op1=ALU.add), r=['A', 'I', 'XC', 'hl'], w=['XC'])
                cp(k, 'dve', c.hl[:, 0:1], c.XC[:, T - 1:T], r=['XC'], w=['hl'])

            def s5():
                act(k, c.Y[:], c.Y[:], AF.Gelu_apprx_tanh, r=['Y'], w=['Y'])
                tt(k, 'dve', c.ob[:], c.XC[:], c.Y[:], ALU.mult, r=['XC', 'Y'], w=['ob'])
                k.dma('sp', c.OT[j * 128:(j + 1) * 128, tb * T:(tb + 1) * T], c.ob[:], r=['ob'], w=[], sem='ob')
            return [s0, s1, s2, s3, s4, s5]
        k.op('dve', lambda e: e.memset(c.ST32[:], 0.0), w=['ST32'])
        k.op('dve', lambda e: e.memset(c.STp[:], 0.0), w=['STp'])

        def ret_stages(ch):
            cs = slice(ch * TC, (ch + 1) * TC)

            def r0():
                def ld_tabs(chx):
                    rsl = (chx % 2) * 4
                    csx = slice(chx * TC, (chx + 1) * TC)
                    for ti, nm in enumerate(('rcos', 'rsin', 'rcosk', 'rsink')):
                        k.dma('act', c.rt[rsl + ti][:], c.CD[nm][:, csx], w=[('rt', rsl + ti)], sem=('rt', rsl + ti))
                if ch == 0:
                    ld_tabs(0)
                if ch + 1 < NCH:
                    ld_tabs(ch + 1)
                proj(WQ, ch, c.pst[0], 'ps0'); proj(WQS, ch, c.pst[1], 'ps1')
                tt(k, 'dve', c.t1[:], c.pst[0][:, :], c.rt[(ch % 2) * 4 + 0][:], ALU.mult, r=['ps0', ('rt', (ch % 2) * 4 + 0)], w=['t1'])
                tt(k, 'dve', c.t2[:], c.pst[1][:, :], c.rt[(ch % 2) * 4 + 1][:], ALU.mult, r=['ps1', ('rt', (ch % 2) * 4 + 1)], w=['t2'])
                tt(k, 'pool', c.QR[:], c.t1[:], c.t2[:], ALU.add, r=['t1', 't2'], w=['QR'])
                tt(k, 'pool', c.QD[:], c.QR[:], c.qdec[:], ALU.mult, r=['QR', 'qdec'], w=['QD'])

            def r1():
                proj(WK, ch, c.pst[0], 'ps0'); proj(WKS, ch, c.pst[1], 'ps1')
                tt(k, 'dve', c.t1[:], c.pst[0][:, :], c.rt[(ch % 2) * 4 + 2][:], ALU.mult, r=['ps0', ('rt', (ch % 2) * 4 + 2)], w=['t1'])
                tt(k, 'dve', c.t2[:], c.pst[1][:, :], c.rt[(ch % 2) * 4 + 3][:], ALU.mult, r=['ps1', ('rt', (ch % 2) * 4 + 3)], w=['t2'])
                tt(k, 'pool', c.KR[:], c.t1[:], c.t2[:], ALU.add, r=['t1', 't2'], w=['KR'])
                cp(k, 'act', c.KRA[0:64, :], c.KR[0:64, :], r=['KR'], w=['KRA'])
                cp(k, 'act', c.KRB[64:128, :], c.KR[64:128, :], r=['KR'], w=['KRB'])

            def r2():
                proj(WG, ch, c.pst[0], 'ps0')
                act(k, c.SG[:], c.pst[0][:, :], AF.Silu, r=['ps0'], w=['SG'])
                for s in range(4):
                    tk = slice(ch * TC + s * 128, ch * TC + (s + 1) * 128)
                    for kc in range(8):
                        mm(k, c.pst[2][:, s * 128:(s + 1) * 128], hn[:, kc, tk], c.gw[:, WV, kc, :], kc == 0, kc == 7,
                           r=[('gw', WV), ('hn', ch)], w=['ps2'])
                cp(k, 'act', c.VT[:].rearrange("p s m -> p (s m)"), c.pst[2][:, :], r=['ps2'], w=['VT'])
                for s in range(4):
                    k.op('pe', lambda e, s=s: e.transpose(c.psb[:, s * 128:(s + 1) * 128], c.KR[:, s * 128:(s + 1) * 128], c.identb[:]),
                         r=['KR', 'identb'], w=['psb'])
                for hh in range(2):
                    ts(k, 'dve', c.KD[:, :, hh * 64:(hh + 1) * 64],
                       c.psb[:, 0:512].rearrange("p (s m) -> p s m", s=4)[:, :, hh * 64:(hh + 1) * 64],
                       c.kdec[:, 2 * j + hh:2 * j + hh + 1], None, ALU.mult, None, r=['psb', 'kdec'], w=['KD'])

            def sub(s):
                def f():
                    sc_ = slice(s * 128, (s + 1) * 128)
                    for hh in range(2):
                        KRp = c.KRA if hh == 0 else c.KRB
                        ps_s = c.pst[3 + hh]; pks = 'ps%d' % (3 + hh)
                        mm(k, ps_s[:, 0:128], KRp[:, sc_], c.QR[:, sc_], True, True, r=['KRA', 'KRB', 'QR'], w=[pks])
                    for hh in range(2):
                        ps_s = c.pst[3 + hh]; pks = 'ps%d' % (3 + hh)
                        tt(k, 'dve', c.SD[:, hh, :], ps_s[:, 0:128], c.dm[:, hh, :], ALU.mult, r=[pks, 'dm'], w=[('SD', hh)])
                    for hh in range(2):
                        hs = slice(hh * 64, (hh + 1) * 64)
                        mm(k, c.pst[5][hs, sc_], c.VT[:, s, hs], c.SD[:, hh, :], True, False, r=['VT', ('SD', hh)], w=['ps5'])
                        mm(k, c.pst[5][hs, sc_], c.STp[:, hh, :], c.QD[:, sc_], False, True, r=['STp', 'QD'], w=['ps5'])
                    for hh in range(2):
                        hs = slice(hh * 64, (hh + 1) * 64)
                        mm(k, c.pst[2][hs, 0:64], c.KD[:, s, hs], c.VT[:, s, hs], True, True, r=['KD', 'VT'], w=['ps2'])
                    stt(k, c.ST32[:], c.ST32[:], c.cdec[:, j:j + 1], c.pst[2][:, 0:64], ALU.mult, ALU.add,
                        r=['ST32', 'ps2', 'cdec'], w=['ST32'])
                    cp(k, 'dve', c.STp[0:64, 0, :], c.ST32[0:64, :], r=['ST32'], w=['STp'])
                    cp(k, 'dve', c.STp[64:128, 1, :], c.ST32[64:128, :], r=['ST32'], w=['STp'])
                return f

            def r7():
                cp(k, 'act', c.O32[:], c.pst[5][:, :], r=['ps5'], w=['O32'])
                mm(k, c.pst[6][:, :], c.avg[:], c.O32[:], True, True, r=['avg', 'O32'], w=['ps6'])
                tt(k, 'dve', c.O32[:], c.O32[:], c.pst[6][:, :], ALU.subtract, r=['O32', 'ps6'], w=['O32'])
                act(k, c.SQr[:], c.O32[:], AF.Square, r=['O32'], w=['SQr'])
                mm(k, c.pst[6][:, :], c.avg[:], c.SQr[:], True, True, r=['avg', 'SQr'], w=['ps6'])
                act(k, c.SQr[:], c.pst[6][:, :], AF.Ln, r=['ps6'], w=['SQr'], scale=1.0, bias=c.epsc[:, 0:1])
                act(k, c.SQr[:], c.SQr[:], AF.Exp, r=['SQr'], w=['SQr'], scale=-0.5)
                tt(k, 'dve', c.O32[:], c.O32[:], c.SQr[:], ALU.mult, r=['O32', 'SQr'], w=['O32'])
                tt(k, 'dve', c.ob2[:], c.O32[:], c.SG[:], ALU.mult, r=['O32', 'SG'], w=['ob2'])
                k.dma('sp', c.OT[512 + j * 128:512 + (j + 1) * 128, cs], c.ob2[:], r=['ob2'], w=[], sem='ob2')
            return [r0, r1, r2, sub(0), sub(1), sub(2), sub(3), r7]

        Lq = [st for tb in range(S // T) for st in lru_stages(tb)]
        Rq = [st for ch in range(NCH) for st in ret_stages(ch)]
        li = ri = 0
        while li < len(Lq) or ri < len(Rq):
            if ri < len(Rq) and (li >= len(Lq) or ri * len(Lq) <= li * len(Rq)):
                Rq[ri](); ri += 1
            else:
                Lq[li](); li += 1


def phase_nsa_a(k, c):
    hn = c.hn
    cnt = [0]

    def ldw(ci):
        sl = cnt[0] % 4; cnt[0] += 1
        load_w(k, c, c.gw4[sl][:].rearrange("p kc m -> p (kc m)"), 'in1', ci * 1024, 1024, ('gw4', sl), q='act')
        return sl

    def proj(sl, ch, ps, pk):
        for kc in range(8):
            mm(k, ps[:, :], c.gw4[sl][:, kc, :], hn[:, kc, ch * TC:(ch + 1) * TC], kc == 0, kc == 7,
               r=[('gw4', sl), ('hn', ch)], w=[pk])
    ob_i = [0]

    def store(dst, src_fn, view=None):
        sl = ob_i[0] % 2; ob_i[0] += 1
        src_fn(c.obn[sl], ('obn', sl))
        src = c.obn[sl][:] if view is None else view(c.obn[sl])
        k.dma('sp', dst, src, r=[('obn', sl)], w=[], sem=('obn', sl))

    def rot_job(ci, cis, cosn, sinn, dst_rows):
        sa = ldw(ci); sb_ = ldw(cis)
        for ch in range(NCH):
            cs = slice(ch * TC, (ch + 1) * TC)
            tsl = ch % 2
            k.dma('act', c.rt[tsl][:], c.CD[cosn][:, cs], w=[('rt', tsl)], sem=('rt', tsl))
            k.dma('act', c.rt[2 + tsl][:], c.CD[sinn][:, cs], w=[('rt', 2 + tsl)], sem=('rt', 2 + tsl))
            pa, ka = c.pst[2 * tsl], 'ps%d' % (2 * tsl)
            pb, kb = c.pst[2 * tsl + 1], 'ps%d' % (2 * tsl + 1)
            proj(sa, ch, pa, ka); proj(sb_, ch, pb, kb)
            tt(k, 'dve', c.t1[:], pa[:, :], c.rt[tsl][:], ALU.mult, r=[ka, ('rt', tsl)], w=['t1'])
            tt(k, 'dve', c.t2[:], pb[:, :], c.rt[2 + tsl][:], ALU.mult, r=[kb, ('rt', 2 + tsl)], w=['t2'])
            store(dst_rows[:, cs], lambda o, ok: tt(k, 'pool', o[:], c.t1[:], c.t2[:], ALU.add, r=['t1', 't2'], w=[ok]))
    for c2 in range(8):
        rot_job(c2, 8 + c2, 'ncosq', 'nsinq', c.QS[c2 * 128:(c2 + 1) * 128, :])
    for g in range(4):
        base = 16 + 6 * g
        rot_job(base + 0, base + 1, 'ncos', 'nsin', c.KSD[g * 128:(g + 1) * 128, :])
        rot_job(base + 2, base + 3, 'ncos', 'nsin', c.KWD[g * 128:(g + 1) * 128, :])
        sa = ldw(base + 5)
        for ch in range(NCH):
            cs = slice(ch * TC, (ch + 1) * TC)
            pa, ka = c.pst[ch % 4], 'ps%d' % (ch % 4)
            proj(sa, ch, pa, ka)
            store(c.KVCD[g * 128:(g + 1) * 128, cs], lambda o, ok: cp(k, 'act', o[:], pa[:, :], r=[ka], w=[ok]))
        sa = ldw(base + 4)
        for ch in range(NCH):
            pa, ka = c.pst[4 + ch % 2], 'ps%d' % (4 + ch % 2)
            for s in range(4):
                tk = slice(ch * TC + s * 128, ch * TC + (s + 1) * 128)
                for kc in range(8):
                    mm(k, pa[:, s * 128:(s + 1) * 128], hn[:, kc, tk], c.gw4[sa][:, kc, :], kc == 0, kc == 7,
                       r=[('gw4', sa), ('hn', ch)], w=[ka])
            dst = c.VD[g, ch * TC:(ch + 1) * TC, :].rearrange("(s p) d -> p s d", p=128)
            store(dst, lambda o, ok: cp(k, 'act', o[:], pa[:, :], r=[ka], w=[ok]),
                  view=lambda o: o[:].rearrange("p (s d) -> p s d", s=4))
    sa = ldw(40)
    for ch in range(NCH):
        cs = slice(ch * TC, (ch + 1) * TC)
        pa, ka = c.pst[ch % 4], 'ps%d' % (ch % 4)
        proj(sa, ch, pa, ka)
        store(c.GSD[:, cs], lambda o, ok: act(k, o[:], pa[:, :], AF.Sigmoid, r=[ka], w=[ok]))


def phase_nsa_b(k, c):
    QB = 256
    NQB = S // QB
    def ldc(dst, src, key, tmp=None):
        k.dma('sp', dst, src, w=[key], sem=key)
    k.dma('sp', c.identfN[:], c.CD['ident'][:, :], w=['identf'], sem='identf')
    cp(k, 'dve', c.identbN[:], c.identfN[:], r=['identf'], w=['identb'])
    def ldcb(dst2d, src2d, n, key):
        k.dma('sp', c.cst[:, 0:n], src2d, w=['cst'], sem='cst')
        cp(k, 'dve', dst2d, c.cst[:, 0:n], r=['cst'], w=[key])
    ldcb(c.c2s[:].rearrange("p a b -> p (a b)"), c.CD['c2s'].rearrange("p a b -> p (a b)"), 128, 'c2s')
    ldcb(c.cm2[:].rearrange("p a b -> p (a b)"), c.CD['cm2'].rearrange("p a b -> p (a b)"), 512, 'cm2')
    ldcb(c.wm2[:].rearrange("p a b -> p (a b)"), c.CD['wm2'].rearrange("p a b -> p (a b)"), 1536, 'wm2')
    cp(k, 'dve', c.peb[:], c.sp('pe'), r=['sp'], w=['peb'])
    k.dma('sp', c.ccos[:], c.CD['ccos'][:, :], w=['ccos'], sem='ccos')
    k.dma('sp', c.csin[:], c.CD['csin'][:, :], w=['csin'], sem='csin')
    load_w(k, c, c.cw1[:].rearrange("p l o -> p (l o)"), 'cw1', 0, 2048, 'cw1')
    load_w(k, c, c.cw2[:].rearrange("p a m -> p (a m)"), 'cw2', 0, 512, 'cw2')
    k.op('pool', lambda e: e.memset(c.VS[:, :, 64:128], 1.0), w=['VSo'])
    k.op('pool', lambda e: e.memset(c.VW[:, :, 64:128], 1.0), w=['VWo'])
    k.op('pool', lambda e: e.memset(c.VC[:, :, 64:128], 1.0), w=['VCo'])
    k.op('pool', lambda e: e.memset(c.KC2[:], 0.0), w=['KC2'])
    k.op('pool', lambda e: e.memset(c.GH[:], 0.0), w=['GH'])
    k.op('pool', lambda e: e.memset(c.tiny[:], 1e-30), w=['tiny'])
    for g in range(4):
        for qc in range(2):
            k.dma('sp', c.Q[:, qc, :], c.QS[(2 * g + qc) * 128:(2 * g + qc + 1) * 128, :], w=['Q'], sem='Q')
        k.dma('sp', c.KS2[:], c.KSD[g * 128:(g + 1) * 128, :], w=['KS2'], sem='KS2')
        k.dma('sp', c.KW2[:], c.KWD[g * 128:(g + 1) * 128, :], w=['KW2'], sem='KW2')
        k.dma('sp', c.KVC[:], c.KVCD[g * 128:(g + 1) * 128, :], w=['KVC'], sem='KVC')
        vd = c.VD[g].rearrange("(kt p) d -> p kt d", p=128)
        k.dma('sp', c.VS[:, :, 0:64], vd[:, :, 0:64], w=['VS'], sem='VS')
        k.dma('sp', c.VW[:, :, 0:64], vd[:, :, 64:128], w=['VW'], sem='VW')
        for l in range(32):
            mm(k, c.pst[0][0:64, 0:1], c.cw1[0:64, l, :], c.peb[0:64, l:l + 1], l == 0, l == 31, r=['cw1', 'peb'], w=['pss0'])
        for l in range(32):
            mm(k, c.pst[1][64:128, 0:1], c.cw1[64:128, l, :], c.peb[64:128, l:l + 1], l == 0, l == 31, r=['cw1', 'peb'], w=['pss0'])
        cp(k, 'dve', c.cbias[0:64, :], c.pst[0][0:64, 0:1], r=['pss0'], w=['cbias'])
        cp(k, 'dve', c.cbias[64:128, :], c.pst[1][64:128, 0:1], r=['pss0'], w=['cbias'])
        for l in range(32):
            mm(k, c.pst[2][0:64, 0:255], c.cw1[0:64, l, :], c.KVC[0:64, l:l + 4065:16], l == 0, l == 31, r=['cw1', 'KVC'], w=['pss1'])
        for l in range(32):
            mm(k, c.pst[3][64:128, 0:255], c.cw1[64:128, l, :], c.KVC[64:128, l:l + 4065:16], l == 0, l == 31, r=['cw1', 'KVC'], w=['pss1'])
        act(k, c.GH[0:64, 0:255], c.pst[2][0:64, 0:255], AF.Gelu_apprx_tanh, r=['pss1', 'cbias'], w=['GH'], bias=c.cbias[0:64, 0:1], scale=1.0)
        act(k, c.GH[64:128, 0:255], c.pst[3][64:128, 0:255], AF.Gelu_apprx_tanh, r=['pss1', 'cbias'], w=['GH'], bias=c.cbias[64:128, 0:1], scale=1.0)
        mm(k, c.pst[0][:, 0:256], c.cw2[0:64, 0, :], c.GH[0:64, :], True, True, r=['cw2', 'GH'], w=['pss0'])
        mm(k, c.pst[1][:, 0:256], c.cw2[0:64, 1, :], c.GH[0:64, :], True, True, r=['cw2', 'GH'], w=['pss0'])
        tt(k, 'dve', c.t1[:, 0:256], c.pst[0][:, 0:256], c.ccos[:], ALU.mult, r=['pss0', 'ccos'], w=['t1'])
        tt(k, 'dve', c.t2[:, 0:256], c.pst[1][:, 0:256], c.csin[:], ALU.mult, r=['pss0', 'csin'], w=['t2'])
        tt(k, 'pool', c.KC2[:], c.t1[:, 0:256], c.t2[:, 0:256], ALU.add, r=['t1', 't2'], w=['KC2'])
        for nt in range(2):
            mm(k, c.pst[2 + nt][:, 0:64], c.GH[64:128, nt * 128:(nt + 1) * 128], c.cw2[64:128, 2, 0:64], True, True,
               r=['GH', 'cw2'], w=['pss1'])
            cp(k, 'dve', c.VC[:, nt, 0:64], c.pst[2 + nt][:, 0:64], r=['pss1'], w=['VC'])
        LA = 4
        scnt = [0]
        ecnt = [0]

        def prefetch_tabs(qb):
            sl = qb % 2
            qs_ = slice(qb * QB, (qb + 1) * QB)
            for nt in range(2):
                k.dma('sp', c.visf[sl][:, nt, :], c.CD['vis'][:, nt, qs_], w=[('visf', sl, nt)], sem=('visf', sl, nt))
            k.dma('sp', c.tkc[sl][:], c.CD['tkC'][:, 2 * qb:2 * qb + 2, :], w=[('tkc', sl)], sem=('tkc', sl))
            k.dma('sp', c.tka[sl][:], c.CD['tkA'][:, 2 * qb:2 * qb + 2, :], w=[('tka', sl)], sem=('tka', sl))

        def prefetch_gbt(qb):
            sl = qb % 2
            qs_ = slice(qb * QB, (qb + 1) * QB)
            gv = c.GSD[12 * g:12 * g + 12, qs_].rearrange("(hp b br) q -> br b hp q", hp=2, b=2, br=3)
            for br in range(3):
                for b in range(2):
                    k.dma('sp', c.GBT[sl][:, br, b], gv[br:br + 1, b].broadcast_to([64, 2, QB]), w=[('GBT', sl, br, b)], sem=('GBT', sl, br))

        deferred = []

        def epilogue_a():
            act(k, c.rec[64:128, :], c.osb[64:128, :], AF.Ln, r=['osb', 'tiny'], w=['rec'], bias=c.tiny[64:128, 0:1], scale=1.0)
            act(k, c.rec[0:64, :], c.rec[64:128, :], AF.Exp, r=['rec'], w=['rec'], scale=-1.0)

        def epilogue_b(br, qbx):
            osl = qbx % 2
            first = (br == 0)
            gt = c.GBT[osl][:, br].rearrange("p b h q -> p (b h q)")
            tt(k, 'dve', c.fg[0:64, :], c.rec[0:64, :], gt, ALU.mult, r=['rec', ('GBT', osl, br, 0), ('GBT', osl, br, 1)], w=['fg'])
            og = c.OG[osl][0:64, :]
            if first:
                tt(k, 'dve', og, c.osb[0:64, :], c.fg[0:64, :], ALU.mult, r=['osb', 'fg'], w=[('OG', osl)])
            else:
                tt(k, 'dve', c.tmp[0:64, :], c.osb[0:64, :], c.fg[0:64, :], ALU.mult, r=['osb', 'fg'], w=['tmp'])
                tt(k, 'pool', og, og, c.tmp[0:64, :], ALU.add, r=[('OG', osl), 'tmp'], w=[('OG', osl)])
            if br == 0:
                cp(k, 'pool', c.recD[64:128, :], c.rec[0:64, :], r=['rec'], w=['recD'])
                cp(k, 'pool', c.recD[0:64, :], c.rec[0:64, :], r=['rec'], w=['recD'])
                for nt2 in range(2):
                    tt(k, 'dve' if nt2 == 0 else 'pool', c.PN[nt2][:], c.EC[nt2][:], c.recD[:], ALU.mult,
                       r=[('EC', nt2), 'recD'], w=[('PN', nt2)])

        def mk_job(qbx, br, K2, kcols, Vt, mask, first, last, nt=None):
            st = {}
            qs = slice(qbx * QB, (qbx + 1) * QB)
            osl = qbx % 2

            def front():
                slot = scnt[0] % 2; scnt[0] += 1
                ps, pk = c.pss[slot], 'pss%d' % slot
                for b in range(2):
                    for hp in range(2):
                        mm(k, ps[:, b * 512 + hp * QB:b * 512 + (hp + 1) * QB], K2[b * 64:(b + 1) * 64, kcols],
                           c.Q[b * 64:(b + 1) * 64, hp, qs], True, True, r=['KS2', 'KW2', 'KC2', 'Q'], w=[pk])
                if br == 0:
                    E, ek = c.EC[nt], ('EC', nt)
                else:
                    es = ecnt[0] % 5; ecnt[0] += 1
                    E, ek = c.EB[es], ('EB', es)
                act(k, E[:], ps[:, :], AF.Exp, r=[pk], w=[ek])
                if mask is not None:
                    mt, mkey = mask
                    mb = mt.unsqueeze(1).broadcast_to([128, 4, QB])
                    ev = E[:].rearrange("p (h q) -> p h q", h=4)
                    tt(k, 'dve', ev, ev, mb, ALU.mult, r=[ek] + (list(mkey) if isinstance(mkey, list) else [mkey]), w=[ek])
                st['E'] = (E, ek)

            def back():
                E, ek = st['E']
                for b in range(2):
                    mm(k, c.pso[:, b * 512:(b + 1) * 512], Vt, E[:, b * 512:(b + 1) * 512], first, last,
                       r=['VS', 'VSo', 'VW', 'VWo', 'VC', 'VCo', ek], w=['pso%d' % b])
                if last:
                    cp(k, 'act', c.osb[:], c.pso[:, :], r=['pso0', 'pso1'], w=['osb'])
                    epilogue_a()
                    deferred.append([2, (lambda br=br, qbx=qbx: epilogue_b(br, qbx))])
            return front, back

        def topk_q(qbx, qt):
            tsl = qbx % 2
            pi = c.pst[6][:, 0:64]; pik = 'ps6'
            i = 0
            for b in range(2):
                for hp in range(2):
                    for nt in range(2):
                        o_ = b * 512 + hp * QB + qt * 128
                        mm(k, pi, c.PN[nt][:, o_:o_ + 128], c.c2s[:, nt, :], i == 0, i == 7, r=[('PN', nt), 'c2s'], w=[pik])
                        i += 1
            IM, IM2, m8 = c.IMs[qt], c.IM2s[qt], c.m8s[qt]
            tt(k, 'dve', IM[:], pi, c.tkc[tsl][:, qt, :], ALU.mult, r=[pik, ('tkc', tsl)], w=[('IM', qt)])
            tt(k, 'dve', IM[:], IM[:], c.tka[tsl][:, qt, :], ALU.add, r=[('IM', qt), ('tka', tsl)], w=[('IM', qt)])
            k.op('dve', lambda e, IM=IM, m8=m8: e.max(out=m8[:, 0:8], in_=IM[:]), r=[('IM', qt)], w=[('m8', qt)])
            k.op('dve', lambda e, IM=IM, IM2=IM2, m8=m8: e.match_replace(out=IM2[:], in_to_replace=m8[:, 0:8], in_values=IM[:], imm_value=-3e38),
                 r=[('IM', qt), ('m8', qt)], w=[('IM2', qt)])
            k.op('dve', lambda e, IM2=IM2, m8=m8: e.max(out=m8[:, 8:16], in_=IM2[:]), r=[('IM2', qt)], w=[('m8', qt)])
            ts(k, 'dve', c.SELMs[qt][:], IM[:], m8[:, 15:16], None, ALU.is_ge, None, r=[('IM', qt), ('m8', qt)], w=[('SELM', qt)])

        def topk_b(qbx):
            msl = qbx % 2
            qs = slice(qbx * QB, (qbx + 1) * QB)
            for qt in range(2):
                k.op('pe', lambda e, qt=qt: e.transpose(c.psb[0:64, qt * 128:(qt + 1) * 128], c.SELMs[qt][:], c.identbN[:]),
                     r=[('SELM', qt), 'identb'], w=['psb'])
            cp(k, 'dve', c.SELT[0:64, :], c.psb[0:64, 0:256], r=['psb'], w=['SELT'])
            k.dma('sp', c.SELD[g, :, qs], c.SELT[0:64, :], r=['SELT'], w=[('seld', g, qbx)], sem='SELT')
            nkt_ = 2 * qbx + 2
            seld_v = c.SELD[g].rearrange("(kt two) q -> two kt q", two=2)
            for two in range(2):
                k.dma('sp', c.MB[msl][two * 64:(two + 1) * 64, 0:nkt_, :],
                      seld_v[two:two + 1, 0:nkt_, qs].broadcast_to([64, nkt_, QB]),
                      r=[('seld', g, qbx)], w=[('MB', msl, two)], sem=('MB', msl, two))

        def diag_mask(qbx):
            msl = qbx % 2
            for o in range(2):
                tt(k, 'pool', c.MB[msl][:, 2 * qbx + o, :], c.MB[msl][:, 2 * qbx + o, :], c.cm2[:, o, :], ALU.mult,
                   r=[('MB', msl, 0), ('MB', msl, 1), 'cm2'], w=[('MB', msl, 0), ('MB', msl, 1)])

        def cmp_jobs(qbx):
            tsl = qbx % 2
            cp(k, 'pool', c.visb[:].rearrange("p a b -> p (a b)"), c.visf[tsl][:].rearrange("p a b -> p (a b)"),
               r=[('visf', tsl, 0), ('visf', tsl, 1)], w=['visb'])
            return [mk_job(qbx, 0, c.KC2, slice(nt * 128, (nt + 1) * 128), c.VC[:, nt, :], (c.visb[:, nt, :], 'visb'),
                           nt == 0, nt == 1, nt=nt) for nt in range(2)]

        def run_deferred(flush=False):
            keep = []
            for d in deferred:
                d[0] -= 1
                if flush or d[0] <= 0:
                    d[1]()
                else:
                    keep.append(d)
            deferred[:] = keep

        def run_jobs(jobs, hooks):
            nj = len(jobs)
            for i in range(nj + LA):
                run_deferred()
                for h in hooks.pop(i, []):
                    h()
                if i < nj:
                    jobs[i][0]()
                if i >= LA:
                    jobs[i - LA][1]()
            run_deferred(flush=True)
            for i in sorted(hooks):
                for h in hooks[i]:
                    h()

        prefetch_tabs(0)
        prefetch_tabs(1)
        prefetch_gbt(0)
        run_jobs(cmp_jobs(0), {})
        topk_q(0, 0)
        topk_q(0, 1)
        topk_b(0)
        for qb in range(NQB):
            qs = slice(qb * QB, (qb + 1) * QB)
            if qb + 2 < NQB:
                prefetch_tabs(qb + 2)
            if qb + 1 < NQB:
                prefetch_gbt(qb + 1)
            jobs = []
            kts = [(o, 2 * qb - 4 + o) for o in range(6) if 2 * qb - 4 + o >= 0]
            for idx, (o, kt) in enumerate(kts):
                jobs.append(mk_job(qb, 2, c.KW2, slice(kt * 128, (kt + 1) * 128), c.VW[:, kt, :],
                                   None if o in (2, 3) else (c.wm2[:, o, :], 'wm2'), idx == 0, idx == len(kts) - 1))
            hooks = {}
            if qb + 1 < NQB:
                jobs += cmp_jobs(qb + 1)
                hooks.setdefault(len(jobs) + LA + 5, []).append(lambda q1=qb + 1: topk_q(q1, 0))
                hooks.setdefault(len(jobs) + LA + 8, []).append(lambda q1=qb + 1: topk_q(q1, 1))
                hooks.setdefault(len(jobs) + LA + 11, []).append(lambda q1=qb + 1: topk_b(q1))
            msl = qb % 2
            nkt = 2 * qb + 2
            hooks.setdefault(len(jobs), []).append(lambda q0=qb: diag_mask(q0))
            for kt in range(nkt):
                jobs.append(mk_job(qb, 1, c.KS2, slice(kt * 128, (kt + 1) * 128), c.VS[:, kt, :],
                                   (c.MB[msl][:, kt, :], [('MB', msl, 0), ('MB', msl, 1)]), kt == 0, kt == nkt - 1))
            run_jobs(jobs, hooks)
            osl = qb % 2
            for b in range(2):
                cp(k, 'pool', c.OGb[b * 64:(b + 1) * 64, :], c.OG[osl][0:64, b * 512:(b + 1) * 512], r=[('OG', osl)], w=['OGb'])
            for hp in range(2):
                k.dma('sp', c.OT[(2 * g + hp) * 128:(2 * g + hp + 1) * 128, qs], c.OGb[:, hp * QB:(hp + 1) * QB], r=['OGb'], w=[], sem='OGb')


def alloc_global(k, c):
    c.sq = k.sb('sq', [128, 8, TC], F32)
    c.rs = k.sb('rs', [128, TC], F32)
    c.spt = k.sb('spt', [128, c.NSP], F32)
    c.ones32 = k.sb('ones32', [128, 128], F32)
    c.onesf = k.sb('onesf', [128, 128], F32)
    c.epsc = k.sb('epsc', [128, 1], F32)
    c.onec = k.sb('onec', [128, 1], F32)
    c.psbig = k.ps('psbig', [128, 3072], F32)
    c.pst = [c.psbig[:, i * 512:(i + 1) * 512] for i in range(6)] + [k.ps('ps6', [128, 512], F32)]
    c.pss = [c.psbig[:, 0:1024], c.psbig[:, 1024:2048]]
    c.pso = c.psbig[:, 2048:3072]
    c.psb = k.ps('psb', [128, 1024], BF16)
    c.sp = lambda n: c.spt[:, c.soffs[n][0]:c.soffs[n][0] + c.soffs[n][1]]


class Scope:
    CNT = [0]

    def __init__(self, k):
        self.k = k
        self.es = ExitStack()
        Scope.CNT[0] += 1
        self.id = Scope.CNT[0]

    def sb(self, name, shape, dtype):
        return self.es.enter_context(self.k.nc.sbuf_tensor('%s_s%d' % (name, self.id), list(shape), dtype))

    def close(self):
        self.es.close()


def build(offs, XTOT, soffs, NSP, cshapes, upto=99, dbg=False):
    k = Prog(); c = Ctx()
    c.offs, c.XTOT, c.soffs, c.NSP = offs, XTOT, soffs, NSP
    kind_s = 'ExternalOutput' if dbg else 'Internal'
    c.XT = k.dram('xT', [D, S], F32, 'ExternalInput')
    c.WC = k.dram('wcat', [128, XTOT], F32, 'ExternalInput')
    c.SPD = k.dram('spcat', [128, NSP], F32, 'ExternalInput')
    c.CD = {n: k.dram('c_' + n, list(shp), F32, 'ExternalInput') for n, shp in cshapes.items()}
    c.OUT = k.dram('outT', [D, S], F32, 'ExternalOutput')
    c.WB = k.dram('wb', [128, XTOT], BF16, 'Internal')
    c.H1 = k.dram('h1', [D, S], F32, kind_s)
    c.H2 = k.dram('h2', [D, S], F32, kind_s)
    c.H3 = k.dram('h3', [D, S], F32, kind_s)
    c.OT = k.dram('oT', [D, S], BF16, 'Internal')
    alloc_global(k, c)
    k.op('dve', lambda e: e.memset(c.onec[:], 1.0), w=['onec'])
    phase_consts(k, c)
    phase_weights(k, c, ['in0', 'ga', 'gx', 'out0', 'w1_0', 'w3_0', 'w2_0'])
    hsc = Scope(k)
    c.hn = hsc.sb('hn', [128, 8, S], BF16)
    sc = Scope(k)
    c.xc = [sc.sb('xc%d' % i, [128, 8, TC], F32) for i in range(2)]
    phase_norm0(k, c, c.XT, 'g_attn0')
    barrier(k); sc.close()
    if upto >= 1:
        sc = Scope(k)
        T = 1024
        c.gw = sc.sb('gw', [128, 8, 8, 128], BF16)
        c.gaw = sc.sb('gaw', [128, 4, 128], BF16); c.gxw = sc.sb('gxw', [128, 4, 128], BF16)
        c.cv = sc.sb('cv', [128, 8], F32)
        c.avg = sc.sb('avg', [128, 128], F32); c.kdec = sc.sb('kdec', [128, 8], F32); c.cdec = sc.sb('cdec', [128, 4], F32)
        c.identf = sc.sb('identf', [128, 128], F32); c.identb = sc.sb('identb', [128, 128], BF16)
        c.qdec = sc.sb('qdec', [128, 512], F32); c.dm = sc.sb('dm', [128, 2, 128], F32)
        c.XR = sc.sb('XR', [128, T + 4], F32); c.Y = sc.sb('Y', [128, T], F32); c.XC = sc.sb('XC', [128, T], F32)
        c.XCb = sc.sb('XCb', [128, T], BF16); c.R = sc.sb('R', [128, T], F32); c.I = sc.sb('I', [128, T], F32)
        c.A = sc.sb('A', [128, T], F32); c.hl = sc.sb('hl', [128, 1], F32); c.ob = sc.sb('ob', [128, T], BF16)
        c.rt = [sc.sb('rt%d' % i, [128, TC], F32) for i in range(8)]
        c.t1 = sc.sb('t1', [128, TC], F32); c.t2 = sc.sb('t2', [128, TC], F32)
        c.QR = sc.sb('QR', [128, TC], BF16); c.QD = sc.sb('QD', [128, TC], BF16); c.KR = sc.sb('KR', [128, TC], BF16)
        c.KRA = sc.sb('KRA', [128, TC], BF16); c.KRB = sc.sb('KRB', [128, TC], BF16)
        c.SG = sc.sb('SG', [128, TC], F32); c.VT = sc.sb('VT', [128, 4, 128], BF16); c.KD = sc.sb('KD', [128, 4, 128], BF16)
        c.SD = sc.sb('SD', [128, 2, 128], BF16); c.ST32 = sc.sb('ST32', [128, 64], F32); c.STp = sc.sb('STp', [128, 2, 64], BF16)
        c.O32 = sc.sb('O32', [128, TC], F32); c.SQr = sc.sb('SQr', [128, TC], F32); c.ob2 = sc.sb('ob2', [128, TC], BF16)
        phase_mix0(k, c)
        barrier(k); sc.close()
    if upto >= 2:
        sc = Scope(k)
        c.xc = [sc.sb('xc%d' % i, [128, 8, TC], F32) for i in range(2)]
        c.oc = [sc.sb('oc%d' % i, [128, 8, TC], BF16) for i in range(2)]
        c.wout = sc.sb('wout', [128, 8, 8, 128], BF16)
        phase_weights(k, c, ['in1', 'out1', 'cw1', 'cw2', 'w1_1', 'w3_1', 'w2_1'])
        phase_outproj(k, c, 'out0', c.XT, c.H1, 'g_ffn0')
        barrier(k); sc.close()
    if upto >= 3:
        sc = Scope(k)
        c.xc = [sc.sb('xc%d' % i, [128, 8, TC], F32) for i in range(2)]
        c.w13 = [sc.sb('w13_%d' % i, [128, 2, 8, 128], BF16) for i in range(2)]
        c.w2t = [sc.sb('w2t%d' % i, [128, 22, 128], BF16) for i in range(2)]
        c.sil = [sc.sb('sil%d' % i, [128, TC], F32) for i in range(2)]
        c.hid = sc.sb('hid', [128, 22, 1024], BF16)
        phase_ffn(k, c, 0, c.H1, c.H2, 'g_attn1', final=False)
        barrier(k); sc.close()
    if upto >= 4:
        c.QS = k.dram('qs', [D, S], BF16, 'Internal')
        c.KSD = k.dram('ksd', [512, S], BF16, 'Internal')
        c.KWD = k.dram('kwd', [512, S], BF16, 'Internal')
        c.KVCD = k.dram('kvcd', [512, S], BF16, 'Internal')
        c.VD = k.dram('vd', [4, S, 128], BF16, 'Internal')
        c.GSD = k.dram('gsd', [128, S], BF16, 'Internal')
        c.SELD = k.dram('seld', [4, 64, S], BF16, 'Internal')
        sc = Scope(k)
        c.gw4 = [sc.sb('gw4_%d' % i, [128, 8, 128], BF16) for i in range(4)]
        c.rt = [sc.sb('rt%d' % i, [128, TC], F32) for i in range(4)]
        c.t1 = sc.sb('t1', [128, TC], F32); c.t2 = sc.sb('t2', [128, TC], F32)
        c.obn = [sc.sb('obn%d' % i, [128, TC], BF16) for i in range(2)]
        phase_nsa_a(k, c)
        barrier(k); sc.close(); hsc.close()
        sc = Scope(k)
        QB = 256
        c.Q = sc.sb('Q', [128, 2, S], BF16); c.KS2 = sc.sb('KS2', [128, S], BF16); c.KW2 = sc.sb('KW2', [128, S], BF16)
        c.KVC = sc.sb('KVC', [128, S], BF16); c.VS = sc.sb('VS', [128, 32, 128], BF16); c.VW = sc.sb('VW', [128, 32, 128], BF16)
        c.MB = [sc.sb('MB%d' % i, [128, 32, QB], BF16) for i in range(2)]
        c.EB = [sc.sb('EB%d' % i, [128, 1024], BF16) for i in range(5)]
        c.EC = [sc.sb('EC%d' % i, [128, 1024], BF16) for i in range(2)]
        c.PN = [sc.sb('PN%d' % i, [128, 1024], BF16) for i in range(2)]
        c.rec = sc.sb('rec', [128, 1024], F32); c.recD = sc.sb('recD', [128, 1024], F32); c.fg = sc.sb('fg', [128, 1024], F32)
        c.tmp = sc.sb('tmp', [128, 1024], F32); c.OG = [sc.sb('OG%d' % i, [128, 1024], F32) for i in range(2)]; c.GBT = [sc.sb('GBT%d' % i, [64, 3, 2, 2, QB], BF16) for i in range(2)]; c.tiny = sc.sb('tiny', [128, 1], F32); c.osb = sc.sb('osb', [128, 1024], F32); c.OGb = sc.sb('OGb', [128, 512], BF16)
        c.visf = [sc.sb('visf%d' % i, [128, 2, QB], F32) for i in range(2)]; c.visb = sc.sb('visb', [128, 2, QB], BF16)
        c.tkc = [sc.sb('tkc%d' % i, [128, 2, 64], F32) for i in range(2)]; c.tka = [sc.sb('tka%d' % i, [128, 2, 64], F32) for i in range(2)]
        c.IMs = [sc.sb('IM%d' % i, [128, 64], F32) for i in range(2)]; c.IM2s = [sc.sb('IM2%d' % i, [128, 64], F32) for i in range(2)]; c.m8s = [sc.sb('m8%d' % i, [128, 16], F32) for i in range(2)]
        c.SELMs = [sc.sb('SELM%d' % i, [128, 64], BF16) for i in range(2)]; c.SELT = sc.sb('SELT', [128, QB], BF16)
        c.identfN = sc.sb('identf', [128, 128], F32); c.identbN = sc.sb('identb', [128, 128], BF16)
        c.cst = sc.sb('cst', [128, 1536], F32)
        c.c2s = sc.sb('c2s', [128, 2, 64], BF16); c.cm2 = sc.sb('cm2', [128, 2, QB], BF16); c.wm2 = sc.sb('wm2', [128, 6, QB], BF16)
        c.peb = sc.sb('peb', [128, 32], BF16); c.ccos = sc.sb('ccos', [128, 256], F32); c.csin = sc.sb('csin', [128, 256], F32)
        c.cw1 = sc.sb('cw1', [128, 32, 64], BF16); c.cw2 = sc.sb('cw2', [128, 4, 128], BF16)
        c.cbias = sc.sb('cbias', [128, 1], F32); c.GH = sc.sb('GH', [128, 256], BF16); c.KC2 = sc.sb('KC2', [128, 256], BF16)
        c.VC = sc.sb('VC', [128, 2, 128], BF16)
        c.t1 = sc.sb('t1', [128, TC], F32); c.t2 = sc.sb('t2', [128, TC], F32)
        phase_nsa_b(k, c)
        barrier(k); sc.close()
    if upto >= 5:
        hsc = Scope(k)
        c.hn = hsc.sb('hn', [128, 8, S], BF16)
        sc = Scope(k)
        c.xc = [sc.sb('xc%d' % i, [128, 8, TC], F32) for i in range(2)]
        c.oc = [sc.sb('oc%d' % i, [128, 8, TC], BF16) for i in range(2)]
        c.wout = sc.sb('wout', [128, 8, 8, 128], BF16)
        phase_outproj(k, c, 'out1', c.H2, c.H3, 'g_ffn1')
        barrier(k); sc.close()
    if upto >= 6:
        sc = Scope(k)
        c.xc = [sc.sb('xc%d' % i, [128, 8, TC], F32) for i in range(2)]
        c.w13 = [sc.sb('w13_%d' % i, [128, 2, 8, 128], BF16) for i in range(2)]
        c.w2t = [sc.sb('w2t%d' % i, [128, 22, 128], BF16) for i in range(2)]
        c.sil = [sc.sb('sil%d' % i, [128, TC], F32) for i in range(2)]
        c.hid = sc.sb('hid', [128, 22, 1024], BF16)
        phase_ffn(k, c, 1, c.H3, c.OUT, 'g_fin', final=True)
        barrier(k); sc.close()
    c.upto = upto
    try:
        hsc.close()
    except Exception:
        pass
    return k, c


_CACHE = {}


def kernel(**inputs):
    inp = {n: np.asarray(v) for n, v in inputs.items()}
    wcat, offs, spcat, soffs = host_prep(inp)
    C = const_tables2(const_tables())
    k, c = build(offs, wcat.shape[1], soffs, spcat.shape[1], {n: a.shape for n, a in C.items()}, upto=6, dbg=False)
    nc = k.finish()
    x = inp['x']
    B = x.shape[0]
    shared = {"wcat": wcat, "spcat": spcat}
    for n, a in C.items():
        shared['c_' + n] = np.ascontiguousarray(a)
    in_maps = []
    for b in range(B):
        m = dict(shared)
        m["xT"] = np.ascontiguousarray(x[b].T)
        in_maps.append(m)
    res = run_bass_kernel_spmd(nc, in_maps, core_ids=list(range(B)))
    out = np.stack([np.ascontiguousarray(r["outT"].T) for r in res.results], axis=0)
    return out.astype(np.float32)
```

```python
import numpy as np
from contextlib import ExitStack
import concourse.bass as bass
import concourse.mybir as mybir
from concourse.bass_utils import run_bass_kernel_spmd

F32 = mybir.dt.float32
BF16 = mybir.dt.bfloat16
F32R = mybir.dt.float32r
AF = mybir.ActivationFunctionType
ALU = mybir.AluOpType
AX = mybir.AxisListType


class Prog:
    CE = ('pe', 'act', 'dve', 'pool')
    ALL = ('pe', 'act', 'dve', 'pool', 'sp')

    def __init__(self):
        self.nc = bass.Bass("TRN2", target_bir_lowering=False)
        self.es = ExitStack()
        self.streams = {e: [] for e in self.ALL}
        self.cnt = {}
        self.semh = {}
        for e in self.CE:
            self.semh['tl_' + e] = self.nc.alloc_semaphore(name='tl_' + e)
            self.cnt['tl_' + e] = 0
        self.seen = {e: {} for e in self.ALL}
        self.track = {}
        self.n_ops = 0

    def dram(self, name, shape, dtype, kind):
        return self.nc.dram_tensor(name, list(shape), dtype, kind=kind).ap()

    def sb(self, name, shape, dtype):
        return self.es.enter_context(self.nc.sbuf_tensor(name, list(shape), dtype))

    def ps(self, name, shape, dtype=F32):
        return self.es.enter_context(self.nc.psum_tensor(name, list(shape), dtype))

    def _t(self, k):
        t = self.track.get(k)
        if t is None:
            t = self.track[k] = {'w': None, 'r': {}}
        return t

    def _waits(self, eng, reads, writes):
        waits = {}

        def need(s, v):
            if waits.get(s, 0) < v:
                waits[s] = v
        for k in reads:
            t = self._t(k)
            if t['w'] is not None:
                need(*t['w'])
        for k in writes:
            t = self._t(k)
            if t['w'] is not None:
                need(*t['w'])
            for s, v in t['r'].items():
                need(s, v)
        out = []
        seen = self.seen[eng]
        for s, v in waits.items():
            if eng == 'pe' and s == 'tl_pe':
                continue
            if seen.get(s, 0) < v:
                seen[s] = v
                out.append((s, v))
        return out

    def _mark(self, tok, reads, writes):
        s, v = tok
        for k in reads:
            t = self._t(k)
            if t['r'].get(s, 0) < v:
                t['r'][s] = v
        for k in writes:
            t = self._t(k)
            t['w'] = tok
            t['r'] = {}

    def op(self, eng, fn, r=(), w=()):
        wl = self._waits(eng, r, w)
        s = 'tl_' + eng
        self.cnt[s] += 1
        v = self.cnt[s]
        self.seen[eng][s] = max(self.seen[eng].get(s, 0), 0)
        semh = self.semh

        def emit(e):
            for ws, wv in wl:
                e.wait_ge(semh[ws], wv)
            fn(e).then_inc(semh[s], 1)
        self.streams[eng].append(emit)
        self._mark((s, v), r, w)
        self.n_ops += 1

    def dma(self, q, out, in_, r=(), w=(), sem=None):
        assert sem is not None
        s = 'd_' + str(sem)
        if s not in self.semh:
            self.semh[s] = self.nc.alloc_semaphore(name=s.replace(' ', '').replace(',', '_').replace('(', '').replace(')', '').replace("'", ''))
            self.cnt[s] = 0
        wl = self._waits(q, r, w)
        self.cnt[s] += 16
        v = self.cnt[s]
        semh = self.semh

        def emit(e):
            for ws, wv in wl:
                e.wait_ge(semh[ws], wv)
            e.dma_start(out=out, in_=in_).then_inc(semh[s], 16)
        self.streams[q].append(emit)
        self._mark((s, v), r, w)
        self.n_ops += 1

    def finish(self):
        nc = self.nc
        final = [(s, v) for s, v in self.cnt.items() if v > 0]
        semh = self.semh

        def fin(e):
            for s, v in final:
                e.wait_ge(semh[s], v)
        self.streams['sp'].append(fin)
        streams = self.streams
        with nc.Block() as block:
            @block.sync
            def _(e):
                for f in streams['sp']:
                    f(e)

            @block.scalar
            def _(e):
                for f in streams['act']:
                    f(e)

            @block.vector
            def _(e):
                for f in streams['dve']:
                    f(e)

            @block.gpsimd
            def _(e):
                for f in streams['pool']:
                    f(e)

            @block.tensor
            def _(e):
                for f in streams['pe']:
                    f(e)
        self.es.close()
        return nc
D = 1024; S = 4096; NCH = 8; TC = 512
FFH = 2816
LOG2 = np.log(2.0)


def lay_fm(W):
    K, M = W.shape
    return np.ascontiguousarray(W.reshape(K // 128, 128, M // 128, 128).transpose(1, 2, 0, 3))


def host_prep(inp):
    f32 = np.float32
    W = {}
    w_in = inp['ab_w_in'][0]
    y_w, xr_w = w_in[:, 0:512], w_in[:, 512:1024]
    q_w, k_w, v_w, g_w = (w_in[:, 1024 + i * 512:1024 + (i + 1) * 512] for i in range(4))
    perm = np.arange(512).reshape(8, 64)
    perm = np.concatenate([perm[:, 32:], perm[:, :32]], axis=1).reshape(-1)
    ext0 = np.concatenate([y_w, xr_w, q_w, q_w[:, perm], k_w, k_w[:, perm], g_w, v_w], axis=1)
    W['in0'] = lay_fm(ext0)
    ga = np.zeros((4, 128, 128), f32); gx = np.zeros((4, 128, 128), f32)
    for b in range(8):
        c, o = b // 2, (b % 2) * 64
        ga[c, o:o + 64, o:o + 64] = inp['gate_a_w'][0, b]
        gx[c, o:o + 64, o:o + 64] = inp['gate_x_w'][0, b]
    W['ga'] = np.ascontiguousarray(ga.transpose(1, 0, 2))
    W['gx'] = np.ascontiguousarray(gx.transpose(1, 0, 2))
    W['out0'] = lay_fm(inp['ab_w_out'][0])
    def ffw(l):
        W['w1_%d' % l] = lay_fm(inp['ffn_w1'][l])
        W['w3_%d' % l] = lay_fm(inp['ffn_w3'][l])
        W['w2_%d' % l] = lay_fm(inp['ffn_w2'][l])
    ffw(0)
    wn = inp['nsa_w_in'][0]
    qn = wn[:, 0:1024]
    kvw = wn[:, 1024:1024 + 1536].reshape(1024, 6, 4, 64)
    gw = wn[:, 2560:2608]
    p16 = np.arange(64); p16[0:8] = np.arange(8, 16); p16[8:16] = np.arange(0, 8)
    permq = (np.arange(16)[:, None] * 64 + p16[None, :]).reshape(-1)
    cols = [qn, qn[:, permq]]
    for g in range(4):
        for j in (2, 4):
            kk = kvw[:, j, g]; ks = kk[:, p16]
            cols += [np.concatenate([kk, kk], 1), np.concatenate([ks, ks], 1)]
        cols.append(np.concatenate([kvw[:, 3, g], kvw[:, 5, g]], 1))
        cols.append(np.concatenate([kvw[:, 0, g], kvw[:, 1, g]], 1))
    cols.append(np.concatenate([gw, np.zeros((1024, 80), f32)], 1))
    ext1 = np.concatenate(cols, axis=1)
    W['in1'] = lay_fm(ext1)
    W['out1'] = lay_fm(inp['nsa_w_out'][0])
    w1k = inp['cmp_k_w1'][0].reshape(32, 64, 64).transpose(1, 0, 2)
    w1v = inp['cmp_v_w1'][0].reshape(32, 64, 64).transpose(1, 0, 2)
    W['cw1'] = np.ascontiguousarray(np.concatenate([w1k, w1v], 0))
    w2k = inp['cmp_k_w2'][0]; w2v = inp['cmp_v_w2'][0]
    top = np.zeros((64, 4, 128), f32); bot = np.zeros((64, 4, 128), f32)
    top[:, 0] = np.concatenate([w2k, w2k], 1); top[:, 1] = np.concatenate([w2k[:, p16], w2k[:, p16]], 1)
    bot[:, 2, 0:64] = w2v
    W['cw2'] = np.ascontiguousarray(np.concatenate([top, bot], 0))
    ffw(1)
    offs = {}; parts = []; o = 0
    for n, a in W.items():
        a2 = a.reshape(128, -1).astype(f32)
        offs[n] = (o, a2.shape[1], a.shape[1:]); parts.append(a2); o += a2.shape[1]
    pad = (-o) % 2048
    if pad:
        parts.append(np.zeros((128, pad), f32)); o += pad
    wcat = np.ascontiguousarray(np.concatenate(parts, axis=1))

    def col4(v):
        return np.ascontiguousarray(v.reshape(4, 128).T)

    def col8(v):
        return np.ascontiguousarray(v.reshape(8, 128).T)
    sp = {}
    sp['g_attn0'] = col8(inp['attn_norm'][0]); sp['g_attn1'] = col8(inp['attn_norm'][1])
    sp['g_ffn0'] = col8(inp['ffn_norm'][0]); sp['g_ffn1'] = col8(inp['ffn_norm'][1])
    sp['g_fin'] = col8(inp['final_norm'])
    for i in range(4):
        sp['cw%d' % i] = col4(inp['conv_w'][0, i])
    sp['cb'] = col4(inp['conv_b'][0]); sp['gab'] = col4(inp['gate_a_b'][0]); sp['gxb'] = col4(inp['gate_x_b'][0])
    sp['lam'] = col4(inp['lru_lambda'][0])
    pek = inp['cmp_pe_k'][0].T; pev = inp['cmp_pe_v'][0].T
    sp['pe'] = np.ascontiguousarray(np.concatenate([pek, pev], 0))
    soffs = {}; sparts = []; o = 0
    for n, a in sp.items():
        soffs[n] = (o, a.shape[1]); sparts.append(a.astype(f32)); o += a.shape[1]
    spcat = np.ascontiguousarray(np.concatenate(sparts, 1))
    return wcat, offs, spcat, soffs


def const_tables():
    f32 = np.float32
    C = {}
    pos = np.arange(S, dtype=np.float64)
    inv = 10000.0 ** (-np.arange(0, 64, 2, dtype=np.float64) / 64)
    d = np.arange(128) % 64
    ang = pos[None, :] * inv[d % 32][:, None]
    sgn = np.where(d < 32, -1.0, 1.0)[:, None]
    C['rcos'] = np.cos(ang).astype(f32); C['rsin'] = (np.sin(ang) * sgn).astype(f32)
    C['rcosk'] = (np.cos(ang) * 0.125).astype(f32); C['rsink'] = (np.sin(ang) * sgn * 0.125).astype(f32)
    hh = np.arange(8, dtype=np.float64)
    log_g = np.log1p(-(2.0 ** (-5.0 - hh)))
    ci = np.arange(128, dtype=np.float64)
    qdec = np.zeros((4, 128, 512)); kdec = np.zeros((128, 8)); dm = np.zeros((4, 128, 2, 128)); cdec = np.zeros((128, 4))
    for h in range(8):
        j, o = h // 2, (h % 2) * 64
        qdec[j, o:o + 64, :] = np.tile(np.exp((ci + 1.0) * log_g[h]), 4)[None, :]
        kdec[:, h] = np.exp((127.0 - ci) * log_g[h])
        diff = ci[None, :] - ci[:, None]
        dm[j, :, h % 2, :] = np.where(diff >= 0, np.exp(np.maximum(diff, 0) * log_g[h]), 0.0)
        cdec[o:o + 64, j] = np.exp(128.0 * log_g[h])
    C['qdec'] = qdec.astype(f32); C['kdec'] = kdec.astype(f32); C['dm'] = dm.astype(f32); C['cdec'] = cdec.astype(f32)
    avg = np.zeros((128, 128)); avg[0:64, 0:64] = 1 / 64; avg[64:, 64:] = 1 / 64
    C['avg'] = avg.astype(f32)
    C['ones'] = np.ones((128, 128), f32)
    C['ident'] = np.eye(128, dtype=f32)
    inv2 = 500000.0 ** (-np.arange(0, 16, 2, dtype=np.float64) / 16)
    fi = np.where(d < 16, d % 8, 0)
    ang2 = pos[None, :] * inv2[fi][:, None]
    rot = (d < 16)[:, None]
    sg2 = np.where(d < 8, -1.0, 1.0)[:, None]
    ncos = np.where(rot, np.cos(ang2), 1.0); nsin = np.where(rot, np.sin(ang2) * sg2, 0.0)
    C['ncosq'] = (ncos * 0.125).astype(f32); C['nsinq'] = (nsin * 0.125).astype(f32)
    C['ncos'] = ncos.astype(f32); C['nsin'] = nsin.astype(f32)
    cend = (np.arange(255) * 16 + 31).astype(np.float64)
    angc = cend[None, :] * inv2[fi][:, None]
    cc = np.where(rot, np.cos(angc), 1.0); cs = np.where(rot, np.sin(angc) * sg2, 0.0)
    ccp = np.zeros((128, 256)); csp = np.zeros((128, 256)); ccp[:, :255] = cc; csp[:, :255] = cs
    C['ccos'] = ccp.astype(f32); C['csin'] = csp.astype(f32)
    n = np.arange(256)
    vis = ((n[:, None] * 16 + 31) <= pos[None, :]) & (n[:, None] < 255)
    C['vis'] = np.ascontiguousarray(vis.reshape(2, 128, S).transpose(1, 0, 2)).astype(f32)
    cst = np.arange(255)[:, None] * 16; sst = np.arange(64)[None, :] * 64
    ov = np.clip(np.minimum(cst + 32, sst + 64) - np.maximum(cst, sst), 0, None) / 32.0
    ovp = np.zeros((256, 64)); ovp[:255] = ov
    C['c2s'] = np.ascontiguousarray(ovp.reshape(2, 128, 64).transpose(1, 0, 2)).astype(f32)
    q = np.arange(S)[:, None]; blk = np.arange(64)[None, :]
    cur = q // 64
    causal = (blk * 64 <= q)
    Am = np.where(causal, 0.0, -1e30)
    Am = np.where(blk == 0, 1e30, Am)
    Am = np.where((blk == cur - 1) & (blk != 0), 2e30, Am)
    Am = np.where((blk == cur) & (blk != 0), 3e30, Am)
    Cm = (causal & ~((blk == 0) | (blk == cur) | (blk == cur - 1))).astype(np.float64)
    C['tkC'] = np.ascontiguousarray(Cm.reshape(32, 128, 64).transpose(1, 0, 2)).astype(f32)
    C['tkA'] = np.ascontiguousarray(Am.reshape(32, 128, 64).transpose(1, 0, 2)).astype(f32)
    kk = np.arange(128)[:, None]; qq = np.arange(128)[None, :]
    C['caus'] = (kk <= qq).astype(f32)
    C['wlow'] = (kk > qq).astype(f32)
    sel = np.zeros((128, 16, 3, 64))
    for h in range(16):
        for b in range(3):
            sel[h * 3 + b, h, b, :] = 1.0
    C['gsel'] = sel.astype(f32)
    return C


def const_tables2(C):
    f32 = np.float32
    kk = np.arange(128)[:, None, None]; o = np.arange(8)[None, :, None]; qc = np.arange(512)[None, None, :]
    dist = qc - ((o - 4) * 128 + kk)
    o2 = np.arange(2)[None, :, None]; q2 = np.arange(256)[None, None, :]
    C['cm2'] = ((o2 * 128 + kk) <= q2).astype(f32)
    o6 = np.arange(6)[None, :, None]
    d6 = q2 - ((o6 - 4) * 128 + kk)
    C['wm2'] = ((d6 >= 0) & (d6 < 512)).astype(f32)
    return C
EPS = 1e-6


class Ctx:
    pass


def mm(k, out, lhsT, rhs, start, stop, r, w):
    k.op('pe', lambda e: e.matmul(out, lhsT=lhsT, rhs=rhs, start=start, stop=stop), r=r, w=w)


def act(k, out, in_, func, r, w, bias=None, scale=None):
    kw = {}
    if bias is not None:
        kw['bias'] = bias
    if scale is not None:
        kw['scale'] = scale
    k.op('act', lambda e: e.activation(out=out, in_=in_, func=func, **kw), r=r, w=w)


def tt(k, eng, out, in0, in1, op, r, w):
    k.op(eng, lambda e: e.tensor_tensor(out=out, in0=in0, in1=in1, op=op), r=r, w=w)


def ts(k, eng, out, in0, s1, s2, op0, op1, r, w):
    if op1 is None:
        k.op(eng, lambda e: e.tensor_scalar(out=out, in0=in0, scalar1=s1, scalar2=None, op0=op0), r=r, w=w)
    else:
        k.op(eng, lambda e: e.tensor_scalar(out=out, in0=in0, scalar1=s1, scalar2=s2, op0=op0, op1=op1), r=r, w=w)


def stt(k, out, in0, sc, in1, op0, op1, r, w):
    k.op('dve', lambda e: e.scalar_tensor_tensor(out=out, in0=in0, scalar=sc, in1=in1, op0=op0, op1=op1), r=r, w=w)


def cp(k, eng, out, in_, r, w):
    if eng == 'act':
        k.op('act', lambda e: e.activation(out=out, in_=in_, func=AF.Copy), r=r, w=w)
    else:
        k.op(eng, lambda e: e.tensor_copy(out=out, in_=in_), r=r, w=w)


def barrier(k):
    final = [(s, v) for s, v in k.cnt.items() if v > 0 and not s.startswith("d_('wbs'")]
    semh = k.semh
    for eng in k.ALL:
        wl = [(s, v) for s, v in final if k.seen[eng].get(s, 0) < v and not (s == 'tl_' + eng)]
        for s, v in wl:
            k.seen[eng][s] = v

        def emit(e, wl=wl):
            for s, v in wl:
                e.wait_ge(semh[s], v)
        k.streams[eng].append(emit)
    k.track = {kk: vv for kk, vv in k.track.items() if isinstance(kk, tuple) and kk[0] == 'wb'}


def load_w(k, c, dst, name, lo, n, key, q='sp'):
    o = c.offs[name][0]
    k.dma(q, dst, c.WB[:, o + lo:o + lo + n], r=[('wb', name)], w=[key], sem=key)


def norm_chunk(k, c, src, srckey, gcol, dst_fn, dstkeys, ps, pskey):
    sq = c.sq
    act(k, sq[:].bitcast(F32R), src, AF.Square, r=[srckey], w=['sq'])
    for kc in range(8):
        mm(k, ps[:, :], c.ones32[:].bitcast(F32R), sq[:, kc, :].bitcast(F32R), kc == 0, kc == 7, r=['sq', 'ones32'], w=[pskey])
    act(k, c.rs[:], ps[:, :], AF.Ln, r=[pskey], w=['rs'], scale=1.0 / 1024, bias=c.epsc[:, 0:1])
    act(k, c.rs[:], c.rs[:], AF.Exp, r=['rs'], w=['rs'], scale=-0.5)
    for kc in range(8):
        stt(k, dst_fn(kc), src[:, kc, :], gcol[:, kc:kc + 1], c.rs[:], ALU.mult, ALU.mult,
            r=[srckey, 'rs', 'sp'], w=[dstkeys[kc] if isinstance(dstkeys, list) else dstkeys])


def phase_weights(k, c, names):
    CH = 4096
    for name in names:
        o, n, _ = c.offs[name]
        i = 0
        for lo in range(0, n, CH):
            hi = min(n, lo + CH)
            k.dma('pool', c.WB[:, o + lo:o + hi], c.WC[:, o + lo:o + hi], w=[('wbc', name, i)], sem=('wbs', name))
            i += 1
        k.track[('wb', name)] = {'w': ("d_" + str(('wbs', name)), k.cnt["d_" + str(('wbs', name))]), 'r': {}}


def phase_consts(k, c):
    k.dma('sp', c.spt[:], c.SPD[:, :], w=['sp'], sem='sp')
    k.dma('sp', c.onesf[:], c.CD['ones'][:, :], w=['onesf'], sem='onesf')
    act(k, c.ones32[:].bitcast(F32R), c.onesf[:], AF.Copy, r=['onesf'], w=['ones32'])
    k.op('dve', lambda e: e.memset(c.epsc[:], EPS), w=['epsc'])


def phase_norm0(k, c, SRC, gname):
    src_v = SRC.rearrange("(kc p) s -> p kc s", p=128)
    g = c.sp(gname)
    for ch in range(NCH):
        sl = ch % 2
        k.dma('sp', c.xc[sl][:], src_v[:, :, ch * TC:(ch + 1) * TC], w=[('xc', sl)], sem=('xc', sl))
        norm_chunk(k, c, c.xc[sl][:], ('xc', sl), g, lambda kc: c.hn[:, kc, ch * TC:(ch + 1) * TC],
                   ('hn', ch), c.pst[6], 'ps6')


def phase_outproj(k, c, wname, RES, HOUT, gname):
    res_v = RES.rearrange("(kc p) s -> p kc s", p=128)
    out_v = HOUT.rearrange("(kc p) s -> p kc s", p=128)
    ot_v = c.OT.rearrange("(kc p) s -> p kc s", p=128)
    load_w(k, c, c.wout[:].rearrange("p a b m -> p (a b m)"), wname, 0, 8 * 8 * 128, 'wout')
    g = c.sp(gname)
    for ch in range(NCH):
        sl = ch % 2
        cs = slice(ch * TC, (ch + 1) * TC)
        k.dma('sp', c.oc[sl][:], ot_v[:, :, cs], w=[('oc', sl)], sem=('oc', sl))
        k.dma('sp', c.xc[sl][:], res_v[:, :, cs], w=[('xc', sl)], sem=('xc', sl))
        for m in range(8):
            ps = c.pst[m % 4]; pk = 'ps%d' % (m % 4)
            for kc in range(8):
                mm(k, ps[:, :], c.wout[:, m, kc, :], c.oc[sl][:, kc, :], kc == 0, kc == 7, r=['wout', ('oc', sl)], w=[pk])
            tt(k, 'dve', c.xc[sl][:, m, :], ps[:, :], c.xc[sl][:, m, :], ALU.add, r=[pk, ('xc', sl)], w=[('xc', sl)])
        k.dma('sp', out_v[:, :, cs], c.xc[sl][:], r=[('xc', sl)], w=[], sem=('xcs', sl))
        norm_chunk(k, c, c.xc[sl][:], ('xc', sl), g, lambda kc: c.hn[:, kc, cs], ('hn', ch), c.pst[6], 'ps6')


def phase_ffn(k, c, l, RES, HOUT, gname, final):
    res_v = RES.rearrange("(kc p) s -> p kc s", p=128)
    out_v = HOUT.rearrange("(kc p) s -> p kc s", p=128)
    g = c.sp(gname)
    TS = 1024
    for sc in range(S // TS):
        for j in range(22):
            sl = j % 2
            load_w(k, c, c.w13[sl][:, 0].rearrange("p kc m -> p (kc m)"), 'w1_%d' % l, j * 1024, 1024, ('w1', sl))
            load_w(k, c, c.w13[sl][:, 1].rearrange("p kc m -> p (kc m)"), 'w3_%d' % l, j * 1024, 1024, ('w3', sl))
            for hf in range(2):
                cs = slice(sc * TS + hf * 512, sc * TS + (hf + 1) * 512)
                ch = (sc * TS + hf * 512) // TC
                pa = c.pst[2 * hf]; pb = c.pst[2 * hf + 1]; ka = 'ps%d' % (2 * hf); kb = 'ps%d' % (2 * hf + 1)
                for kc in range(8):
                    mm(k, pa[:, :], c.w13[sl][:, 0, kc, :], c.hn[:, kc, cs], kc == 0, kc == 7, r=[('w1', sl), ('hn', ch)], w=[ka])
                for kc in range(8):
                    mm(k, pb[:, :], c.w13[sl][:, 1, kc, :], c.hn[:, kc, cs], kc == 0, kc == 7, r=[('w3', sl), ('hn', ch)], w=[kb])
                act(k, c.sil[hf][:], pa[:, :], AF.Silu, r=[ka], w=[('sil', hf)])
                tt(k, 'dve', c.hid[:, j, hf * 512:(hf + 1) * 512], c.sil[hf][:], pb[:, :], ALU.mult, r=[('sil', hf), kb], w=[('hid', j, hf)])
        for hf in range(2):
            sl = hf
            cs = slice(sc * TS + hf * 512, sc * TS + (hf + 1) * 512)
            k.dma('sp', c.xc[sl][:], res_v[:, :, cs], w=[('xc', sl)], sem=('xc', sl))
        for m in range(8):
            sl = m % 2
            load_w(k, c, c.w2t[sl][:].rearrange("p kc m -> p (kc m)"), 'w2_%d' % l, m * 22 * 128, 22 * 128, ('w2', sl))
            for hf in range(2):
                ps = c.pst[4 + hf]; pk = 'ps%d' % (4 + hf)
                for j in range(22):
                    mm(k, ps[:, :], c.w2t[sl][:, j, :], c.hid[:, j, hf * 512:(hf + 1) * 512], j == 0, j == 21,
                       r=[('w2', sl), ('hid', j, hf)], w=[pk])
                tt(k, 'dve', c.xc[hf][:, m, :], ps[:, :], c.xc[hf][:, m, :], ALU.add, r=[pk, ('xc', hf)], w=[('xc', hf)])
        for hf in range(2):
            cs = slice(sc * TS + hf * 512, sc * TS + (hf + 1) * 512)
            ch = (sc * TS + hf * 512) // TC
            if not final:
                k.dma('sp', out_v[:, :, cs], c.xc[hf][:], r=[('xc', hf)], w=[], sem=('xcs', hf))
                norm_chunk(k, c, c.xc[hf][:], ('xc', hf), g, lambda kc: c.hn[:, kc, cs], ('hn', ch), c.pst[6], 'ps6')
            else:
                norm_chunk(k, c, c.xc[hf][:], ('xc', hf), g, lambda kc: c.xc[hf][:, kc, :], ('xc', hf), c.pst[6], 'ps6')
                k.dma('sp', out_v[:, :, cs], c.xc[hf][:], r=[('xc', hf)], w=[], sem=('xcs', hf))


def phase_mix0(k, c):
    hn = c.hn
    T = 1024
    lam = c.sp('lam')
    act(k, c.cv[:, 0:4], lam, AF.Exp, r=['sp'], w=['cv'], scale=-1.0)
    act(k, c.cv[:, 0:4], c.cv[:, 0:4], AF.Ln, r=['cv'], w=['cv'], bias=c.onec[:, 0:1], scale=1.0)
    ts(k, 'dve', c.cv[:, 4:8], c.cv[:, 0:4], -16.0, None, ALU.mult, None, r=['cv'], w=['cv2'])
    ts(k, 'dve', c.cv[:, 0:4], c.cv[:, 0:4], -8.0, None, ALU.mult, None, r=['cv', 'cv2'], w=['cv'])
    k.dma('sp', c.avg[:], c.CD['avg'][:, :], w=['avg'], sem='avg')
    k.dma('sp', c.kdec[:], c.CD['kdec'][:, :], w=['kdec'], sem='kdec')
    k.dma('sp', c.cdec[:], c.CD['cdec'][:, :], w=['cdec'], sem='cdec')
    k.dma('sp', c.identf[:], c.CD['ident'][:, :], w=['identf'], sem='identf')
    cp(k, 'dve', c.identb[:], c.identf[:], r=['identf'], w=['identb'])
    load_w(k, c, c.gaw[:].rearrange("p a m -> p (a m)"), 'ga', 0, 512, 'gaw')
    load_w(k, c, c.gxw[:].rearrange("p a m -> p (a m)"), 'gx', 0, 512, 'gxw')
    k.op('pool', lambda e: e.memset(c.KRA[:], 0.0), w=['KRA'])
    k.op('pool', lambda e: e.memset(c.KRB[:], 0.0), w=['KRB'])
    WY, WX, WQ, WQS, WK, WKS, WG, WV = range(8)
    for j in range(4):
        for typ in range(8):
            load_w(k, c, c.gw[:, typ].rearrange("p kc m -> p (kc m)"), 'in0', (typ * 4 + j) * 1024, 1024, ('gw', typ))
        k.dma('sp', c.qdec[:], c.CD['qdec'][j], w=['qdec'], sem='qdec')
        k.dma('sp', c.dm[:], c.CD['dm'][j], w=['dm'], sem='dm')

        def proj(typ, ch, ps, pk):
            for kc in range(8):
                mm(k, ps[:, :], c.gw[:, typ, kc, :], hn[:, kc, ch * TC:(ch + 1) * TC], kc == 0, kc == 7,
                   r=[('gw', typ), ('hn', ch)], w=[pk])
        def lru_stages(tb):
            def s0():
                if tb == 0:
                    k.op('dve', lambda e: e.memset(c.XR[:, 0:3], 0.0), w=['XR'])
                else:
                    cp(k, 'dve', c.XR[:, 0:3], c.XR[:, T:T + 3], r=['XR'], w=['XR'])
                for cc in range(2):
                    ch = tb * 2 + cc
                    proj(WY, ch, c.pst[0], 'ps0')
                    cp(k, 'act', c.Y[:, cc * 512:(cc + 1) * 512], c.pst[0][:, :], r=['ps0'], w=['Y'])
                    proj(WX, ch, c.pst[1], 'ps1')
                    cp(k, 'act', c.XR[:, 3 + cc * 512:3 + (cc + 1) * 512], c.pst[1][:, :], r=['ps1'], w=['XR'])

            def s1():
                ts(k, 'dve', c.XC[:], c.XR[:, 3:3 + T], c.sp('cw3')[:, j:j + 1], c.sp('cb')[:, j:j + 1], ALU.mult, ALU.add,
                   r=['XR', 'sp'], w=['XC'])
                for i in range(3):
                    stt(k, c.XC[:], c.XR[:, i:i + T], c.sp('cw%d' % i)[:, j:j + 1], c.XC[:], ALU.mult, ALU.add,
                        r=['XR', 'XC', 'sp'], w=['XC'])
                cp(k, 'act', c.XCb[:], c.XC[:], r=['XC'], w=['XCb'])

            def s2():
                for cc in range(2):
                    cs = slice(cc * 512, (cc + 1) * 512)
                    mm(k, c.pst[2][:, :], c.gaw[:, j, :], c.XCb[:, cs], True, True, r=['gaw', 'XCb'], w=['ps2'])
                    act(k, c.R[:, cs], c.pst[2][:, :], AF.Sigmoid, r=['ps2', 'sp'], w=['R'], bias=c.sp('gab')[:, j:j + 1], scale=1.0)
                    mm(k, c.pst[3][:, :], c.gxw[:, j, :], c.XCb[:, cs], True, True, r=['gxw', 'XCb'], w=['ps3'])
                    act(k, c.I[:, cs], c.pst[3][:, :], AF.Sigmoid, r=['ps3', 'sp'], w=['I'], bias=c.sp('gxb')[:, j:j + 1], scale=1.0)

            def s3():
                act(k, c.A[:], c.R[:], AF.Exp, r=['R', 'cv'], w=['A'], scale=c.cv[:, j:j + 1])
                act(k, c.R[:], c.R[:], AF.Exp, r=['R', 'cv2'], w=['R'], scale=c.cv[:, 4 + j:5 + j])
                act(k, c.R[:], c.R[:], AF.Sqrt, r=['R'], w=['R'], scale=-1.0, bias=c.onec[:, 0:1])

            def s4():
                tt(k, 'dve', c.I[:], c.I[:], c.R[:], ALU.mult, r=['I', 'R'], w=['I'])
                tt(k, 'dve', c.I[:], c.I[:], c.XC[:], ALU.mult, r=['I', 'XC'], w=['I'])
                if tb == 0:
                    k.op('dve', lambda e: e.tensor_tensor_scan(out=c.XC[:], data0=c.A[:], data1=c.I[:], initial=0.0,
                                                               op0=ALU.mult, op1=ALU.add), r=['A', 'I', 'XC'], w=['XC'])
                else:
                    k.op('dve', lambda e: e.tensor_tensor_scan(out=c.XC[:], data0=c.A[:], data1=c.I[:], initial=c.hl[:, 0:1],
                                                               op0=ALU.mult, op1=ALU.add), r=['A', 'I', 'XC', 'hl'], w=['XC'])
                cp(k, 'dve', c.hl[:, 0:1], c.XC[:, T - 1:T], r=['XC'], w=['hl'])

            def s5():
                act(k, c.Y[:], c.Y[:], AF.Gelu_apprx_tanh, r=['Y'], w=['Y'])
                tt(k, 'dve', c.ob[:], c.XC[:], c.Y[:], ALU.mult, r=['XC', 'Y'], w=['ob'])
                k.dma('sp', c.OT[j * 128:(j + 1) * 128, tb * T:(tb + 1) * T], c.ob[:], r=['ob'], w=[], sem='ob')
            return [s0, s1, s2, s3, s4, s5]
        k.op('dve', lambda e: e.memset(c.ST32[:], 0.0), w=['ST32'])
        k.op('dve', lambda e: e.memset(c.STp[:], 0.0), w=['STp'])

        def ret_stages(ch):
            cs = slice(ch * TC, (ch + 1) * TC)

            def r0():
                def ld_tabs(chx):
                    rsl = (chx % 2) * 4
                    csx = slice(chx * TC, (chx + 1) * TC)
                    for ti, nm in enumerate(('rcos', 'rsin', 'rcosk', 'rsink')):
                        k.dma('act', c.rt[rsl + ti][:], c.CD[nm][:, csx], w=[('rt', rsl + ti)], sem=('rt', rsl + ti))
                if ch == 0:
                    ld_tabs(0)
                if ch + 1 < NCH:
                    ld_tabs(ch + 1)
                proj(WQ, ch, c.pst[0], 'ps0'); proj(WQS, ch, c.pst[1], 'ps1')
                tt(k, 'dve', c.t1[:], c.pst[0][:, :], c.rt[(ch % 2) * 4 + 0][:], ALU.mult, r=['ps0', ('rt', (ch % 2) * 4 + 0)], w=['t1'])
                tt(k, 'dve', c.t2[:], c.pst[1][:, :], c.rt[(ch % 2) * 4 + 1][:], ALU.mult, r=['ps1', ('rt', (ch % 2) * 4 + 1)], w=['t2'])
                tt(k, 'pool', c.QR[:], c.t1[:], c.t2[:], ALU.add, r=['t1', 't2'], w=['QR'])
                tt(k, 'pool', c.QD[:], c.QR[:], c.qdec[:], ALU.mult, r=['QR', 'qdec'], w=['QD'])

            def r1():
                proj(WK, ch, c.pst[0], 'ps0'); proj(WKS, ch, c.pst[1], 'ps1')
                tt(k, 'dve', c.t1[:], c.pst[0][:, :], c.rt[(ch % 2) * 4 + 2][:], ALU.mult, r=['ps0', ('rt', (ch % 2) * 4 + 2)], w=['t1'])
                tt(k, 'dve', c.t2[:], c.pst[1][:, :], c.rt[(ch % 2) * 4 + 3][:], ALU.mult, r=['ps1', ('rt', (ch % 2) * 4 + 3)], w=['t2'])
                tt(k, 'pool', c.KR[:], c.t1[:], c.t2[:], ALU.add, r=['t1', 't2'], w=['KR'])
                cp(k, 'act', c.KRA[0:64, :], c.KR[0:64, :], r=['KR'], w=['KRA'])
                cp(k, 'act', c.KRB[64:128, :], c.KR[64:128, :], r=['KR'], w=['KRB'])

            def r2():
                proj(WG, ch, c.pst[0], 'ps0')
                act(k, c.SG[:], c.pst[0][:, :], AF.Silu, r=['ps0'], w=['SG'])
                for s in range(4):
                    tk = slice(ch * TC + s * 128, ch * TC + (s + 1) * 128)
                    for kc in range(8):
                        mm(k, c.pst[2][:, s * 128:(s + 1) * 128], hn[:, kc, tk], c.gw[:, WV, kc, :], kc == 0, kc == 7,
                           r=[('gw', WV), ('hn', ch)], w=['ps2'])
                cp(k, 'act', c.VT[:].rearrange("p s m -> p (s m)"), c.pst[2][:, :], r=['ps2'], w=['VT'])
                for s in range(4):
                    k.op('pe', lambda e, s=s: e.transpose(c.psb[:, s * 128:(s + 1) * 128], c.KR[:, s * 128:(s + 1) * 128], c.identb[:]),
                         r=['KR', 'identb'], w=['psb'])
                for hh in range(2):
                    ts(k, 'dve', c.KD[:, :, hh * 64:(hh + 1) * 64],
                       c.psb[:, 0:512].rearrange("p (s m) -> p s m", s=4)[:, :, hh * 64:(hh + 1) * 64],
                       c.kdec[:, 2 * j + hh:2 * j + hh + 1], None, ALU.mult, None, r=['psb', 'kdec'], w=['KD'])

            def sub(s):
                def f():
                    sc_ = slice(s * 128, (s + 1) * 128)
                    for hh in range(2):
                        KRp = c.KRA if hh == 0 else c.KRB
                        ps_s = c.pst[3 + hh]; pks = 'ps%d' % (3 + hh)
                        mm(k, ps_s[:, 0:128], KRp[:, sc_], c.QR[:, sc_], True, True, r=['KRA', 'KRB', 'QR'], w=[pks])
                    for hh in range(2):
                        ps_s = c.pst[3 + hh]; pks = 'ps%d' % (3 + hh)
                        tt(k, 'dve', c.SD[:, hh, :], ps_s[:, 0:128], c.dm[:, hh, :], ALU.mult, r=[pks, 'dm'], w=[('SD', hh)])
                    for hh in range(2):
                        hs = slice(hh * 64, (hh + 1) * 64)
                        mm(k, c.pst[5][hs, sc_], c.VT[:, s, hs], c.SD[:, hh, :], True, False, r=['VT', ('SD', hh)], w=['ps5'])
                        mm(k, c.pst[5][hs, sc_], c.STp[:, hh, :], c.QD[:, sc_], False, True, r=['STp', 'QD'], w=['ps5'])
                    for hh in range(2):
                        hs = slice(hh * 64, (hh + 1) * 64)
                        mm(k, c.pst[2][hs, 0:64], c.KD[:, s, hs], c.VT[:, s, hs], True, True, r=['KD', 'VT'], w=['ps2'])
                    stt(k, c.ST32[:], c.ST32[:], c.cdec[:, j:j + 1], c.pst[2][:, 0:64], ALU.mult, ALU.add,
                        r=['ST32', 'ps2', 'cdec'], w=['ST32'])
                    cp(k, 'dve', c.STp[0:64, 0, :], c.ST32[0:64, :], r=['ST32'], w=['STp'])
                    cp(k, 'dve', c.STp[64:128, 1, :], c.ST32[64:128, :], r=['ST32'], w=['STp'])
                return f

            def r7():
                cp(k, 'act', c.O32[:], c.pst[5][:, :], r=['ps5'], w=['O32'])
                mm(k, c.pst[6][:, :], c.avg[:], c.O32[:], True, True, r=['avg', 'O32'], w=['ps6'])
                tt(k, 'dve', c.O32[:], c.O32[:], c.pst[6][:, :], ALU.subtract, r=['O32', 'ps6'], w=['O32'])
                act(k, c.SQr[:], c.O32[:], AF.Square, r=['O32'], w=['SQr'])
                mm(k, c.pst[6][:, :], c.avg[:], c.SQr[:], True, True, r=['avg', 'SQr'], w=['ps6'])
                act(k, c.SQr[:], c.pst[6][:, :], AF.Ln, r=['ps6'], w=['SQr'], scale=1.0, bias=c.epsc[:, 0:1])
                act(k, c.SQr[:], c.SQr[:], AF.Exp, r=['SQr'], w=['SQr'], scale=-0.5)
                tt(k, 'dve', c.O32[:], c.O32[:], c.SQr[:], ALU.mult, r=['O32', 'SQr'], w=['O32'])
                tt(k, 'dve', c.ob2[:], c.O32[:], c.SG[:], ALU.mult, r=['O32', 'SG'], w=['ob2'])
                k.dma('sp', c.OT[512 + j * 128:512 + (j + 1) * 128, cs], c.ob2[:], r=['ob2'], w=[], sem='ob2')
            return [r0, r1, r2, sub(0), sub(1), sub(2), sub(3), r7]

        Lq = [st for tb in range(S // T) for st in lru_stages(tb)]
        Rq = [st for ch in range(NCH) for st in ret_stages(ch)]
        li = ri = 0
        while li < len(Lq) or ri < len(Rq):
            if ri < len(Rq) and (li >= len(Lq) or ri * len(Lq) <= li * len(Rq)):
                Rq[ri](); ri += 1
            else:
                Lq[li](); li += 1


def phase_nsa_a(k, c):
    hn = c.hn
    cnt = [0]

    def ldw(ci):
        sl = cnt[0] % 4; cnt[0] += 1
        load_w(k, c, c.gw4[sl][:].rearrange("p kc m -> p (kc m)"), 'in1', ci * 1024, 1024, ('gw4', sl), q='act')
        return sl

    def proj(sl, ch, ps, pk):
        for kc in range(8):
            mm(k, ps[:, :], c.gw4[sl][:, kc, :], hn[:, kc, ch * TC:(ch + 1) * TC], kc == 0, kc == 7,
               r=[('gw4', sl), ('hn', ch)], w=[pk])
    ob_i = [0]

    def store(dst, src_fn, view=None):
        sl = ob_i[0] % 2; ob_i[0] += 1
        src_fn(c.obn[sl], ('obn', sl))
        src = c.obn[sl][:] if view is None else view(c.obn[sl])
        k.dma('sp', dst, src, r=[('obn', sl)], w=[], sem=('obn', sl))

    def rot_job(ci, cis, cosn, sinn, dst_rows):
        sa = ldw(ci); sb_ = ldw(cis)
        for ch in range(NCH):
            cs = slice(ch * TC, (ch + 1) * TC)
            tsl = ch % 2
            k.dma('act', c.rt[tsl][:], c.CD[cosn][:, cs], w=[('rt', tsl)], sem=('rt', tsl))
            k.dma('act', c.rt[2 + tsl][:], c.CD[sinn][:, cs], w=[('rt', 2 + tsl)], sem=('rt', 2 + tsl))
            pa, ka = c.pst[2 * tsl], 'ps%d' % (2 * tsl)
            pb, kb = c.pst[2 * tsl + 1], 'ps%d' % (2 * tsl + 1)
            proj(sa, ch, pa, ka); proj(sb_, ch, pb, kb)
            tt(k, 'dve', c.t1[:], pa[:, :], c.rt[tsl][:], ALU.mult, r=[ka, ('rt', tsl)], w=['t1'])
            tt(k, 'dve', c.t2[:], pb[:, :], c.rt[2 + tsl][:], ALU.mult, r=[kb, ('rt', 2 + tsl)], w=['t2'])
            store(dst_rows[:, cs], lambda o, ok: tt(k, 'pool', o[:], c.t1[:], c.t2[:], ALU.add, r=['t1', 't2'], w=[ok]))
    for c2 in range(8):
        rot_job(c2, 8 + c2, 'ncosq', 'nsinq', c.QS[c2 * 128:(c2 + 1) * 128, :])
    for g in range(4):
        base = 16 + 6 * g
        rot_job(base + 0, base + 1, 'ncos', 'nsin', c.KSD[g * 128:(g + 1) * 128, :])
        rot_job(base + 2, base + 3, 'ncos', 'nsin', c.KWD[g * 128:(g + 1) * 128, :])
        sa = ldw(base + 5)
        for ch in range(NCH):
            cs = slice(ch * TC, (ch + 1) * TC)
            pa, ka = c.pst[ch % 4], 'ps%d' % (ch % 4)
            proj(sa, ch, pa, ka)
            store(c.KVCD[g * 128:(g + 1) * 128, cs], lambda o, ok: cp(k, 'act', o[:], pa[:, :], r=[ka], w=[ok]))
        sa = ldw(base + 4)
        for ch in range(NCH):
            pa, ka = c.pst[4 + ch % 2], 'ps%d' % (4 + ch % 2)
            for s in range(4):
                tk = slice(ch * TC + s * 128, ch * TC + (s + 1) * 128)
                for kc in range(8):
                    mm(k, pa[:, s * 128:(s + 1) * 128], hn[:, kc, tk], c.gw4[sa][:, kc, :], kc == 0, kc == 7,
                       r=[('gw4', sa), ('hn', ch)], w=[ka])
            dst = c.VD[g, ch * TC:(ch + 1) * TC, :].rearrange("(s p) d -> p s d", p=128)
            store(dst, lambda o, ok: cp(k, 'act', o[:], pa[:, :], r=[ka], w=[ok]),
                  view=lambda o: o[:].rearrange("p (s d) -> p s d", s=4))
    sa = ldw(40)
    for ch in range(NCH):
        cs = slice(ch * TC, (ch + 1) * TC)
        pa, ka = c.pst[ch % 4], 'ps%d' % (ch % 4)
        proj(sa, ch, pa, ka)
        store(c.GSD[:, cs], lambda o, ok: act(k, o[:], pa[:, :], AF.Sigmoid, r=[ka], w=[ok]))


def phase_nsa_b(k, c):
    QB = 256
    NQB = S // QB
    def ldc(dst, src, key, tmp=None):
        k.dma('sp', dst, src, w=[key], sem=key)
    k.dma('sp', c.identfN[:], c.CD['ident'][:, :], w=['identf'], sem='identf')
    cp(k, 'dve', c.identbN[:], c.identfN[:], r=['identf'], w=['identb'])
    def ldcb(dst2d, src2d, n, key):
        k.dma('sp', c.cst[:, 0:n], src2d, w=['cst'], sem='cst')
        cp(k, 'dve', dst2d, c.cst[:, 0:n], r=['cst'], w=[key])
    ldcb(c.c2s[:].rearrange("p a b -> p (a b)"), c.CD['c2s'].rearrange("p a b -> p (a b)"), 128, 'c2s')
    ldcb(c.cm2[:].rearrange("p a b -> p (a b)"), c.CD['cm2'].rearrange("p a b -> p (a b)"), 512, 'cm2')
    ldcb(c.wm2[:].rearrange("p a b -> p (a b)"), c.CD['wm2'].rearrange("p a b -> p (a b)"), 1536, 'wm2')
    cp(k, 'dve', c.peb[:], c.sp('pe'), r=['sp'], w=['peb'])
    k.dma('sp', c.ccos[:], c.CD['ccos'][:, :], w=['ccos'], sem='ccos')
    k.dma('sp', c.csin[:], c.CD['csin'][:, :], w=['csin'], sem='csin')
    load_w(k, c, c.cw1[:].rearrange("p l o -> p (l o)"), 'cw1', 0, 2048, 'cw1')
    load_w(k, c, c.cw2[:].rearrange("p a m -> p (a m)"), 'cw2', 0, 512, 'cw2')
    k.op('pool', lambda e: e.memset(c.VS[:, :, 64:128], 1.0), w=['VSo'])
    k.op('pool', lambda e: e.memset(c.VW[:, :, 64:128], 1.0), w=['VWo'])
    k.op('pool', lambda e: e.memset(c.VC[:, :, 64:128], 1.0), w=['VCo'])
    k.op('pool', lambda e: e.memset(c.KC2[:], 0.0), w=['KC2'])
    k.op('pool', lambda e: e.memset(c.GH[:], 0.0), w=['GH'])
    k.op('pool', lambda e: e.memset(c.tiny[:], 1e-30), w=['tiny'])
    for g in range(4):
        for qc in range(2):
            k.dma('sp', c.Q[:, qc, :], c.QS[(2 * g + qc) * 128:(2 * g + qc + 1) * 128, :], w=['Q'], sem='Q')
        k.dma('sp', c.KS2[:], c.KSD[g * 128:(g + 1) * 128, :], w=['KS2'], sem='KS2')
        k.dma('sp', c.KW2[:], c.KWD[g * 128:(g + 1) * 128, :], w=['KW2'], sem='KW2')
        k.dma('sp', c.KVC[:], c.KVCD[g * 128:(g + 1) * 128, :], w=['KVC'], sem='KVC')
        vd = c.VD[g].rearrange("(kt p) d -> p kt d", p=128)
        k.dma('sp', c.VS[:, :, 0:64], vd[:, :, 0:64], w=['VS'], sem='VS')
        k.dma('sp', c.VW[:, :, 0:64], vd[:, :, 64:128], w=['VW'], sem='VW')
        for l in range(32):
            mm(k, c.pst[0][0:64, 0:1], c.cw1[0:64, l, :], c.peb[0:64, l:l + 1], l == 0, l == 31, r=['cw1', 'peb'], w=['pss0'])
        for l in range(32):
            mm(k, c.pst[1][64:128, 0:1], c.cw1[64:128, l, :], c.peb[64:128, l:l + 1], l == 0, l == 31, r=['cw1', 'peb'], w=['pss0'])
        cp(k, 'dve', c.cbias[0:64, :], c.pst[0][0:64, 0:1], r=['pss0'], w=['cbias'])
        cp(k, 'dve', c.cbias[64:128, :], c.pst[1][64:128, 0:1], r=['pss0'], w=['cbias'])
        for l in range(32):
            mm(k, c.pst[2][0:64, 0:255], c.cw1[0:64, l, :], c.KVC[0:64, l:l + 4065:16], l == 0, l == 31, r=['cw1', 'KVC'], w=['pss1'])
        for l in range(32):
            mm(k, c.pst[3][64:128, 0:255], c.cw1[64:128, l, :], c.KVC[64:128, l:l + 4065:16], l == 0, l == 31, r=['cw1', 'KVC'], w=['pss1'])
        act(k, c.GH[0:64, 0:255], c.pst[2][0:64, 0:255], AF.Gelu_apprx_tanh, r=['pss1', 'cbias'], w=['GH'], bias=c.cbias[0:64, 0:1], scale=1.0)
        act(k, c.GH[64:128, 0:255], c.pst[3][64:128, 0:255], AF.Gelu_apprx_tanh, r=['pss1', 'cbias'], w=['GH'], bias=c.cbias[64:128, 0:1], scale=1.0)
        mm(k, c.pst[0][:, 0:256], c.cw2[0:64, 0, :], c.GH[0:64, :], True, True, r=['cw2', 'GH'], w=['pss0'])
        mm(k, c.pst[1][:, 0:256], c.cw2[0:64, 1, :], c.GH[0:64, :], True, True, r=['cw2', 'GH'], w=['pss0'])
        tt(k, 'dve', c.t1[:, 0:256], c.pst[0][:, 0:256], c.ccos[:], ALU.mult, r=['pss0', 'ccos'], w=['t1'])
        tt(k, 'dve', c.t2[:, 0:256], c.pst[1][:, 0:256], c.csin[:], ALU.mult, r=['pss0', 'csin'], w=['t2'])
        tt(k, 'pool', c.KC2[:], c.t1[:, 0:256], c.t2[:, 0:256], ALU.add, r=['t1', 't2'], w=['KC2'])
        for nt in range(2):
            mm(k, c.pst[2 + nt][:, 0:64], c.GH[64:128, nt * 128:(nt + 1) * 128], c.cw2[64:128, 2, 0:64], True, True,
               r=['GH', 'cw2'], w=['pss1'])
            cp(k, 'dve', c.VC[:, nt, 0:64], c.pst[2 + nt][:, 0:64], r=['pss1'], w=['VC'])
        LA = 3
        scnt = [0]
        gcnt = [0]
        ecnt = [0]

        def prefetch_tabs(qb):
            sl = qb % 2
            qs_ = slice(qb * QB, (qb + 1) * QB)
            for nt in range(2):
                k.dma('sp', c.visf[sl][:, nt, :], c.CD['vis'][:, nt, qs_], w=[('visf', sl, nt)], sem=('visf', sl, nt))
            k.dma('sp', c.tkc[sl][:], c.CD['tkC'][:, 2 * qb:2 * qb + 2, :], w=[('tkc', sl)], sem=('tkc', sl))
            k.dma('sp', c.tka[sl][:], c.CD['tkA'][:, 2 * qb:2 * qb + 2, :], w=[('tka', sl)], sem=('tka', sl))

        def prefetch_gbt(qb):
            sl = qb % 2
            qs_ = slice(qb * QB, (qb + 1) * QB)
            gv = c.GSD[12 * g:12 * g + 12, qs_].rearrange("(hp b br) q -> br b hp q", hp=2, b=2, br=3)
            for br in range(3):
                for b in range(2):
                    k.dma('sp', c.GBT[sl][:, br, b], gv[br:br + 1, b].broadcast_to([64, 2, QB]), w=[('GBT', sl, br, b)], sem=('GBT', sl, br))

        deferred = []

        def epilogue_a(gs):
            rec, osb = c.recs[gs], c.osbs[gs]
            act(k, rec[64:128, :], osb[64:128, :], AF.Ln, r=[('osb', gs), 'tiny'], w=[('rec', gs)], bias=c.tiny[64:128, 0:1], scale=1.0)
            act(k, rec[0:64, :], rec[64:128, :], AF.Exp, r=[('rec', gs)], w=[('rec', gs)], scale=-1.0)

        def epilogue_b(br, qbx, gs):
            rec, osb = c.recs[gs], c.osbs[gs]
            osl = qbx % 2
            first = (br == 0)
            gt = c.GBT[osl][:, br].rearrange("p b h q -> p (b h q)")
            tt(k, 'pool', c.fg[0:64, :], rec[0:64, :], gt, ALU.mult, r=[('rec', gs), ('GBT', osl, br, 0), ('GBT', osl, br, 1)], w=['fg'])
            og = c.OG[osl][0:64, :]
            if first:
                tt(k, 'pool', og, osb[0:64, :], c.fg[0:64, :], ALU.mult, r=[('osb', gs), 'fg'], w=[('OG', osl)])
            else:
                tt(k, 'pool', c.tmp[0:64, :], osb[0:64, :], c.fg[0:64, :], ALU.mult, r=[('osb', gs), 'fg'], w=['tmp'])
                tt(k, 'pool', og, og, c.tmp[0:64, :], ALU.add, r=[('OG', osl), 'tmp'], w=[('OG', osl)])
            if br == 0:
                cp(k, 'pool', c.recD[64:128, :], rec[0:64, :], r=[('rec', gs)], w=['recD'])
                cp(k, 'pool', c.recD[0:64, :], rec[0:64, :], r=[('rec', gs)], w=['recD'])
                for nt2 in range(2):
                    tt(k, 'dve' if nt2 == 0 else 'pool', c.PN[nt2][:], c.EC[nt2][:], c.recD[:], ALU.mult,
                       r=[('EC', nt2), 'recD'], w=[('PN', nt2)])

        def mk_job(qbx, br, K2, kcols, Vt, mask, first, last, nt=None):
            st = {}
            qs = slice(qbx * QB, (qbx + 1) * QB)
            osl = qbx % 2

            def front():
                slot = scnt[0] % 2; scnt[0] += 1
                ps, pk = c.pss[slot], 'pss%d' % slot
                for b in range(2):
                    for hp in range(2):
                        mm(k, ps[:, b * 512 + hp * QB:b * 512 + (hp + 1) * QB], K2[b * 64:(b + 1) * 64, kcols],
                           c.Q[b * 64:(b + 1) * 64, hp, qs], True, True, r=['KS2', 'KW2', 'KC2', 'Q'], w=[pk])
                if br == 0:
                    E, ek = c.EC[nt], ('EC', nt)
                else:
                    es = ecnt[0] % 4; ecnt[0] += 1
                    E, ek = c.EB[es], ('EB', es)
                act(k, E[:], ps[:, :], AF.Exp, r=[pk], w=[ek])
                if mask is not None:
                    mt, mkey = mask
                    mb = mt.unsqueeze(1).broadcast_to([128, 4, QB])
                    ev = E[:].rearrange("p (h q) -> p h q", h=4)
                    tt(k, 'dve', ev, ev, mb, ALU.mult, r=[ek] + (list(mkey) if isinstance(mkey, list) else [mkey]), w=[ek])
                st['E'] = (E, ek)

            def back():
                E, ek = st['E']
                for b in range(2):
                    mm(k, c.pso[:, b * 512:(b + 1) * 512], Vt, E[:, b * 512:(b + 1) * 512], first, last,
                       r=['VS', 'VSo', 'VW', 'VWo', 'VC', 'VCo', ek], w=['pso%d' % b])
                if last:
                    gs = gcnt[0] % 2; gcnt[0] += 1
                    cp(k, 'act', c.osbs[gs][:], c.pso[:, :], r=['pso0', 'pso1'], w=[('osb', gs)])
                    epilogue_a(gs)
                    deferred.append([2, (lambda br=br, qbx=qbx, gs=gs: epilogue_b(br, qbx, gs))])
            return front, back

        def topk_q(qbx, qt):
            tsl = qbx % 2
            pi = c.pst[6][:, 0:64]; pik = 'ps6'
            i = 0
            for b in range(2):
                for hp in range(2):
                    for nt in range(2):
                        o_ = b * 512 + hp * QB + qt * 128
                        mm(k, pi, c.PN[nt][:, o_:o_ + 128], c.c2s[:, nt, :], i == 0, i == 7, r=[('PN', nt), 'c2s'], w=[pik])
                        i += 1
            IM, IM2, m8 = c.IMs[qt], c.IM2s[qt], c.m8s[qt]
            tt(k, 'dve', IM[:], pi, c.tkc[tsl][:, qt, :], ALU.mult, r=[pik, ('tkc', tsl)], w=[('IM', qt)])
            tt(k, 'dve', IM[:], IM[:], c.tka[tsl][:, qt, :], ALU.add, r=[('IM', qt), ('tka', tsl)], w=[('IM', qt)])
            k.op('dve', lambda e, IM=IM, m8=m8: e.max(out=m8[:, 0:8], in_=IM[:]), r=[('IM', qt)], w=[('m8', qt)])
            k.op('dve', lambda e, IM=IM, IM2=IM2, m8=m8: e.match_replace(out=IM2[:], in_to_replace=m8[:, 0:8], in_values=IM[:], imm_value=-3e38),
                 r=[('IM', qt), ('m8', qt)], w=[('IM2', qt)])
            k.op('dve', lambda e, IM2=IM2, m8=m8: e.max(out=m8[:, 8:16], in_=IM2[:]), r=[('IM2', qt)], w=[('m8', qt)])
            ts(k, 'dve', c.SELMs[qt][:], IM[:], m8[:, 15:16], None, ALU.is_ge, None, r=[('IM', qt), ('m8', qt)], w=[('SELM', qt)])

        def topk_b(qbx):
            msl = qbx % 2
            qs = slice(qbx * QB, (qbx + 1) * QB)
            for qt in range(2):
                k.op('pe', lambda e, qt=qt: e.transpose(c.psb[0:64, qt * 128:(qt + 1) * 128], c.SELMs[qt][:], c.identbN[:]),
                     r=[('SELM', qt), 'identb'], w=['psb'])
            cp(k, 'dve', c.SELT[0:64, :], c.psb[0:64, 0:256], r=['psb'], w=['SELT'])
            k.dma('sp', c.SELD[g, :, qs], c.SELT[0:64, :], r=['SELT'], w=[('seld', g, qbx)], sem='SELT')
            nkt_ = 2 * qbx + 2
            seld_v = c.SELD[g].rearrange("(kt two) q -> two kt q", two=2)
            for two in range(2):
                k.dma('sp', c.MB[msl][two * 64:(two + 1) * 64, 0:nkt_, :],
                      seld_v[two:two + 1, 0:nkt_, qs].broadcast_to([64, nkt_, QB]),
                      r=[('seld', g, qbx)], w=[('MB', msl, two)], sem=('MB', msl, two))

        def diag_mask(qbx):
            msl = qbx % 2
            for o in range(2):
                tt(k, 'pool', c.MB[msl][:, 2 * qbx + o, :], c.MB[msl][:, 2 * qbx + o, :], c.cm2[:, o, :], ALU.mult,
                   r=[('MB', msl, 0), ('MB', msl, 1), 'cm2'], w=[('MB', msl, 0), ('MB', msl, 1)])

        def cmp_jobs(qbx):
            tsl = qbx % 2
            cp(k, 'pool', c.visb[:].rearrange("p a b -> p (a b)"), c.visf[tsl][:].rearrange("p a b -> p (a b)"),
               r=[('visf', tsl, 0), ('visf', tsl, 1)], w=['visb'])
            return [mk_job(qbx, 0, c.KC2, slice(nt * 128, (nt + 1) * 128), c.VC[:, nt, :], (c.visb[:, nt, :], 'visb'),
                           nt == 0, nt == 1, nt=nt) for nt in range(2)]

        def run_deferred(flush=False):
            keep = []
            for d in deferred:
                d[0] -= 1
                if flush or d[0] <= 0:
                    d[1]()
                else:
                    keep.append(d)
            deferred[:] = keep

        def run_jobs(jobs, hooks):
            nj = len(jobs)
            for i in range(nj + LA):
                run_deferred()
                for h in hooks.pop(i, []):
                    h()
                if i < nj:
                    jobs[i][0]()
                if i >= LA:
                    jobs[i - LA][1]()
            run_deferred(flush=True)
            for i in sorted(hooks):
                for h in hooks[i]:
                    h()

        prefetch_tabs(0)
        prefetch_tabs(1)
        prefetch_gbt(0)
        run_jobs(cmp_jobs(0), {})
        topk_q(0, 0)
        topk_q(0, 1)
        topk_b(0)
        for qb in range(NQB):
            qs = slice(qb * QB, (qb + 1) * QB)
            if qb + 2 < NQB:
                prefetch_tabs(qb + 2)
            if qb + 1 < NQB:
                prefetch_gbt(qb + 1)
            jobs = []
            kts = [(o, 2 * qb - 4 + o) for o in range(6) if 2 * qb - 4 + o >= 0]
            for idx, (o, kt) in enumerate(kts):
                jobs.append(mk_job(qb, 2, c.KW2, slice(kt * 128, (kt + 1) * 128), c.VW[:, kt, :],
                                   None if o in (2, 3) else (c.wm2[:, o, :], 'wm2'), idx == 0, idx == len(kts) - 1))
            hooks = {}
            if qb + 1 < NQB:
                jobs += cmp_jobs(qb + 1)
                hooks.setdefault(len(jobs) + LA + 5, []).append(lambda q1=qb + 1: topk_q(q1, 0))
                hooks.setdefault(len(jobs) + LA + 8, []).append(lambda q1=qb + 1: topk_q(q1, 1))
                hooks.setdefault(len(jobs) + LA + 11, []).append(lambda q1=qb + 1: topk_b(q1))
            msl = qb % 2
            nkt = 2 * qb + 2
            hooks.setdefault(len(jobs), []).append(lambda q0=qb: diag_mask(q0))
            for kt in range(nkt):
                jobs.append(mk_job(qb, 1, c.KS2, slice(kt * 128, (kt + 1) * 128), c.VS[:, kt, :],
                                   (c.MB[msl][:, kt, :], [('MB', msl, 0), ('MB', msl, 1)]), kt == 0, kt == nkt - 1))
            run_jobs(jobs, hooks)
            osl = qb % 2
            for b in range(2):
                cp(k, 'pool', c.OGb[b * 64:(b + 1) * 64, :], c.OG[osl][0:64, b * 512:(b + 1) * 512], r=[('OG', osl)], w=['OGb'])
            for hp in range(2):
                k.dma('sp', c.OT[(2 * g + hp) * 128:(2 * g + hp + 1) * 128, qs], c.OGb[:, hp * QB:(hp + 1) * QB], r=['OGb'], w=[], sem='OGb')


def alloc_global(k, c):
    c.sq = k.sb('sq', [128, 8, TC], F32)
    c.rs = k.sb('rs', [128, TC], F32)
    c.spt = k.sb('spt', [128, c.NSP], F32)
    c.ones32 = k.sb('ones32', [128, 128], F32)
    c.onesf = k.sb('onesf', [128, 128], F32)
    c.epsc = k.sb('epsc', [128, 1], F32)
    c.onec = k.sb('onec', [128, 1], F32)
    c.psbig = k.ps('psbig', [128, 3072], F32)
    c.pst = [c.psbig[:, i * 512:(i + 1) * 512] for i in range(6)] + [k.ps('ps6', [128, 512], F32)]
    c.pss = [c.psbig[:, 0:1024], c.psbig[:, 1024:2048]]
    c.pso = c.psbig[:, 2048:3072]
    c.psb = k.ps('psb', [128, 1024], BF16)
    c.sp = lambda n: c.spt[:, c.soffs[n][0]:c.soffs[n][0] + c.soffs[n][1]]


class Scope:
    CNT = [0]

    def __init__(self, k):
        self.k = k
        self.es = ExitStack()
        Scope.CNT[0] += 1
        self.id = Scope.CNT[0]

    def sb(self, name, shape, dtype):
        return self.es.enter_context(self.k.nc.sbuf_tensor('%s_s%d' % (name, self.id), list(shape), dtype))

    def close(self):
        self.es.close()


def build(offs, XTOT, soffs, NSP, cshapes, upto=99, dbg=False):
    k = Prog(); c = Ctx()
    c.offs, c.XTOT, c.soffs, c.NSP = offs, XTOT, soffs, NSP
    kind_s = 'ExternalOutput' if dbg else 'Internal'
    c.XT = k.dram('xT', [D, S], F32, 'ExternalInput')
    c.WC = k.dram('wcat', [128, XTOT], F32, 'ExternalInput')
    c.SPD = k.dram('spcat', [128, NSP], F32, 'ExternalInput')
    c.CD = {n: k.dram('c_' + n, list(shp), F32, 'ExternalInput') for n, shp in cshapes.items()}
    c.OUT = k.dram('outT', [D, S], F32, 'ExternalOutput')
    c.WB = k.dram('wb', [128, XTOT], BF16, 'Internal')
    c.H1 = k.dram('h1', [D, S], F32, kind_s)
    c.H2 = k.dram('h2', [D, S], F32, kind_s)
    c.H3 = k.dram('h3', [D, S], F32, kind_s)
    c.OT = k.dram('oT', [D, S], BF16, 'Internal')
    alloc_global(k, c)
    k.op('dve', lambda e: e.memset(c.onec[:], 1.0), w=['onec'])
    phase_consts(k, c)
    phase_weights(k, c, ['in0', 'ga', 'gx', 'out0', 'w1_0', 'w3_0', 'w2_0'])
    hsc = Scope(k)
    c.hn = hsc.sb('hn', [128, 8, S], BF16)
    sc = Scope(k)
    c.xc = [sc.sb('xc%d' % i, [128, 8, TC], F32) for i in range(2)]
    phase_norm0(k, c, c.XT, 'g_attn0')
    barrier(k); sc.close()
    if upto >= 1:
        sc = Scope(k)
        T = 1024
        c.gw = sc.sb('gw', [128, 8, 8, 128], BF16)
        c.gaw = sc.sb('gaw', [128, 4, 128], BF16); c.gxw = sc.sb('gxw', [128, 4, 128], BF16)
        c.cv = sc.sb('cv', [128, 8], F32)
        c.avg = sc.sb('avg', [128, 128], F32); c.kdec = sc.sb('kdec', [128, 8], F32); c.cdec = sc.sb('cdec', [128, 4], F32)
        c.identf = sc.sb('identf', [128, 128], F32); c.identb = sc.sb('identb', [128, 128], BF16)
        c.qdec = sc.sb('qdec', [128, 512], F32); c.dm = sc.sb('dm', [128, 2, 128], F32)
        c.XR = sc.sb('XR', [128, T + 4], F32); c.Y = sc.sb('Y', [128, T], F32); c.XC = sc.sb('XC', [128, T], F32)
        c.XCb = sc.sb('XCb', [128, T], BF16); c.R = sc.sb('R', [128, T], F32); c.I = sc.sb('I', [128, T], F32)
        c.A = sc.sb('A', [128, T], F32); c.hl = sc.sb('hl', [128, 1], F32); c.ob = sc.sb('ob', [128, T], BF16)
        c.rt = [sc.sb('rt%d' % i, [128, TC], F32) for i in range(8)]
        c.t1 = sc.sb('t1', [128, TC], F32); c.t2 = sc.sb('t2', [128, TC], F32)
        c.QR = sc.sb('QR', [128, TC], BF16); c.QD = sc.sb('QD', [128, TC], BF16); c.KR = sc.sb('KR', [128, TC], BF16)
        c.KRA = sc.sb('KRA', [128, TC], BF16); c.KRB = sc.sb('KRB', [128, TC], BF16)
        c.SG = sc.sb('SG', [128, TC], F32); c.VT = sc.sb('VT', [128, 4, 128], BF16); c.KD = sc.sb('KD', [128, 4, 128], BF16)
        c.SD = sc.sb('SD', [128, 2, 128], BF16); c.ST32 = sc.sb('ST32', [128, 64], F32); c.STp = sc.sb('STp', [128, 2, 64], BF16)
        c.O32 = sc.sb('O32', [128, TC], F32); c.SQr = sc.sb('SQr', [128, TC], F32); c.ob2 = sc.sb('ob2', [128, TC], BF16)
        phase_mix0(k, c)
        barrier(k); sc.close()
    if upto >= 2:
        sc = Scope(k)
        c.xc = [sc.sb('xc%d' % i, [128, 8, TC], F32) for i in range(2)]
        c.oc = [sc.sb('oc%d' % i, [128, 8, TC], BF16) for i in range(2)]
        c.wout = sc.sb('wout', [128, 8, 8, 128], BF16)
        phase_weights(k, c, ['in1', 'out1', 'cw1', 'cw2', 'w1_1', 'w3_1', 'w2_1'])
        phase_outproj(k, c, 'out0', c.XT, c.H1, 'g_ffn0')
        barrier(k); sc.close()
    if upto >= 3:
        sc = Scope(k)
        c.xc = [sc.sb('xc%d' % i, [128, 8, TC], F32) for i in range(2)]
        c.w13 = [sc.sb('w13_%d' % i, [128, 2, 8, 128], BF16) for i in range(2)]
        c.w2t = [sc.sb('w2t%d' % i, [128, 22, 128], BF16) for i in range(2)]
        c.sil = [sc.sb('sil%d' % i, [128, TC], F32) for i in range(2)]
        c.hid = sc.sb('hid', [128, 22, 1024], BF16)
        phase_ffn(k, c, 0, c.H1, c.H2, 'g_attn1', final=False)
        barrier(k); sc.close()
    if upto >= 4:
        c.QS = k.dram('qs', [D, S], BF16, 'Internal')
        c.KSD = k.dram('ksd', [512, S], BF16, 'Internal')
        c.KWD = k.dram('kwd', [512, S], BF16, 'Internal')
        c.KVCD = k.dram('kvcd', [512, S], BF16, 'Internal')
        c.VD = k.dram('vd', [4, S, 128], BF16, 'Internal')
        c.GSD = k.dram('gsd', [128, S], BF16, 'Internal')
        c.SELD = k.dram('seld', [4, 64, S], BF16, 'Internal')
        sc = Scope(k)
        c.gw4 = [sc.sb('gw4_%d' % i, [128, 8, 128], BF16) for i in range(4)]
        c.rt = [sc.sb('rt%d' % i, [128, TC], F32) for i in range(4)]
        c.t1 = sc.sb('t1', [128, TC], F32); c.t2 = sc.sb('t2', [128, TC], F32)
        c.obn = [sc.sb('obn%d' % i, [128, TC], BF16) for i in range(2)]
        phase_nsa_a(k, c)
        barrier(k); sc.close(); hsc.close()
        sc = Scope(k)
        QB = 256
        c.Q = sc.sb('Q', [128, 2, S], BF16); c.KS2 = sc.sb('KS2', [128, S], BF16); c.KW2 = sc.sb('KW2', [128, S], BF16)
        c.KVC = sc.sb('KVC', [128, S], BF16); c.VS = sc.sb('VS', [128, 32, 128], BF16); c.VW = sc.sb('VW', [128, 32, 128], BF16)
        c.MB = [sc.sb('MB%d' % i, [128, 32, QB], BF16) for i in range(2)]
        c.EB = [sc.sb('EB%d' % i, [128, 1024], BF16) for i in range(4)]
        c.EC = [sc.sb('EC%d' % i, [128, 1024], BF16) for i in range(2)]
        c.PN = [sc.sb('PN%d' % i, [128, 1024], BF16) for i in range(2)]
        c.recs = [sc.sb('rec%d' % i, [128, 1024], F32) for i in range(2)]; c.recD = sc.sb('recD', [128, 1024], F32); c.fg = sc.sb('fg', [128, 1024], F32)
        c.tmp = sc.sb('tmp', [128, 1024], F32); c.OG = [sc.sb('OG%d' % i, [128, 1024], F32) for i in range(2)]; c.GBT = [sc.sb('GBT%d' % i, [64, 3, 2, 2, QB], BF16) for i in range(2)]; c.tiny = sc.sb('tiny', [128, 1], F32); c.osbs = [sc.sb('osb%d' % i, [128, 1024], F32) for i in range(2)]; c.OGb = sc.sb('OGb', [128, 512], BF16)
        c.visf = [sc.sb('visf%d' % i, [128, 2, QB], F32) for i in range(2)]; c.visb = sc.sb('visb', [128, 2, QB], BF16)
        c.tkc = [sc.sb('tkc%d' % i, [128, 2, 64], F32) for i in range(2)]; c.tka = [sc.sb('tka%d' % i, [128, 2, 64], F32) for i in range(2)]
        c.IMs = [sc.sb('IM%d' % i, [128, 64], F32) for i in range(2)]; c.IM2s = [sc.sb('IM2%d' % i, [128, 64], F32) for i in range(2)]; c.m8s = [sc.sb('m8%d' % i, [128, 16], F32) for i in range(2)]
        c.SELMs = [sc.sb('SELM%d' % i, [128, 64], BF16) for i in range(2)]; c.SELT = sc.sb('SELT', [128, QB], BF16)
        c.identfN = sc.sb('identf', [128, 128], F32); c.identbN = sc.sb('identb', [128, 128], BF16)
        c.cst = sc.sb('cst', [128, 1536], F32)
        c.c2s = sc.sb('c2s', [128, 2, 64], BF16); c.cm2 = sc.sb('cm2', [128, 2, QB], BF16); c.wm2 = sc.sb('wm2', [128, 6, QB], BF16)
        c.peb = sc.sb('peb', [128, 32], BF16); c.ccos = sc.sb('ccos', [128, 256], F32); c.csin = sc.sb('csin', [128, 256], F32)
        c.cw1 = sc.sb('cw1', [128, 32, 64], BF16); c.cw2 = sc.sb('cw2', [128, 4, 128], BF16)
        c.cbias = sc.sb('cbias', [128, 1], F32); c.GH = sc.sb('GH', [128, 256], BF16); c.KC2 = sc.sb('KC2', [128, 256], BF16)
        c.VC = sc.sb('VC', [128, 2, 128], BF16)
        c.t1 = sc.sb('t1', [128, TC], F32); c.t2 = sc.sb('t2', [128, TC], F32)
        phase_nsa_b(k, c)
        barrier(k); sc.close()
    if upto >= 5:
        hsc = Scope(k)
        c.hn = hsc.sb('hn', [128, 8, S], BF16)
        sc = Scope(k)
        c.xc = [sc.sb('xc%d' % i, [128, 8, TC], F32) for i in range(2)]
        c.oc = [sc.sb('oc%d' % i, [128, 8, TC], BF16) for i in range(2)]
        c.wout = sc.sb('wout', [128, 8, 8, 128], BF16)
        phase_outproj(k, c, 'out1', c.H2, c.H3, 'g_ffn1')
        barrier(k); sc.close()
    if upto >= 6:
        sc = Scope(k)
        c.xc = [sc.sb('xc%d' % i, [128, 8, TC], F32) for i in range(2)]
        c.w13 = [sc.sb('w13_%d' % i, [128, 2, 8, 128], BF16) for i in range(2)]
        c.w2t = [sc.sb('w2t%d' % i, [128, 22, 128], BF16) for i in range(2)]
        c.sil = [sc.sb('sil%d' % i, [128, TC], F32) for i in range(2)]
        c.hid = sc.sb('hid', [128, 22, 1024], BF16)
        phase_ffn(k, c, 1, c.H3, c.OUT, 'g_fin', final=True)
        barrier(k); sc.close()
    c.upto = upto
    try:
        hsc.close()
    except Exception:
        pass
    return k, c


_CACHE = {}


def kernel(**inputs):
    inp = {n: np.asarray(v) for n, v in inputs.items()}
    wcat, offs, spcat, soffs = host_prep(inp)
    C = const_tables2(const_tables())
    k, c = build(offs, wcat.shape[1], soffs, spcat.shape[1], {n: a.shape for n, a in C.items()}, upto=6, dbg=False)
    nc = k.finish()
    x = inp['x']
    B = x.shape[0]
    shared = {"wcat": wcat, "spcat": spcat}
    for n, a in C.items():
        shared['c_' + n] = np.ascontiguousarray(a)
    in_maps = []
    for b in range(B):
        m = dict(shared)
        m["xT"] = np.ascontiguousarray(x[b].T)
        in_maps.append(m)
    res = run_bass_kernel_spmd(nc, in_maps, core_ids=list(range(B)))
    out = np.stack([np.ascontiguousarray(r["outT"].T) for r in res.results], axis=0)
    return out.astype(np.float32)
```

```python
import numpy as np
from contextlib import ExitStack
import concourse.bass as bass
import concourse.mybir as mybir
from concourse.bass_utils import run_bass_kernel_spmd

F32 = mybir.dt.float32
BF16 = mybir.dt.bfloat16
F32R = mybir.dt.float32r
AF = mybir.ActivationFunctionType
ALU = mybir.AluOpType
AX = mybir.AxisListType


class Prog:
    CE = ('pe', 'act', 'dve', 'pool')
    ALL = ('pe', 'act', 'dve', 'pool', 'sp')

    def __init__(self):
        self.nc = bass.Bass("TRN2", target_bir_lowering=False)
        self.es = ExitStack()
        self.streams = {e: [] for e in self.ALL}
        self.cnt = {}
        self.semh = {}
        for e in self.CE:
            self.semh['tl_' + e] = self.nc.alloc_semaphore(name='tl_' + e)
            self.cnt['tl_' + e] = 0
        self.seen = {e: {} for e in self.ALL}
        self.track = {}
        self.n_ops = 0

    def dram(self, name, shape, dtype, kind):
        return self.nc.dram_tensor(name, list(shape), dtype, kind=kind).ap()

    def sb(self, name, shape, dtype):
        return self.es.enter_context(self.nc.sbuf_tensor(name, list(shape), dtype))

    def ps(self, name, shape, dtype=F32):
        return self.es.enter_context(self.nc.psum_tensor(name, list(shape), dtype))

    def _t(self, k):
        t = self.track.get(k)
        if t is None:
            t = self.track[k] = {'w': None, 'r': {}}
        return t

    def _waits(self, eng, reads, writes):
        waits = {}

        def need(s, v):
            if waits.get(s, 0) < v:
                waits[s] = v
        for k in reads:
            t = self._t(k)
            if t['w'] is not None:
                need(*t['w'])
        for k in writes:
            t = self._t(k)
            if t['w'] is not None:
                need(*t['w'])
            for s, v in t['r'].items():
                need(s, v)
        out = []
        seen = self.seen[eng]
        for s, v in waits.items():
            if eng == 'pe' and s == 'tl_pe':
                continue
            if seen.get(s, 0) < v:
                seen[s] = v
                out.append((s, v))
        return out

    def _mark(self, tok, reads, writes):
        s, v = tok
        for k in reads:
            t = self._t(k)
            if t['r'].get(s, 0) < v:
                t['r'][s] = v
        for k in writes:
            t = self._t(k)
            t['w'] = tok
            t['r'] = {}

    def op(self, eng, fn, r=(), w=()):
        wl = self._waits(eng, r, w)
        s = 'tl_' + eng
        self.cnt[s] += 1
        v = self.cnt[s]
        self.seen[eng][s] = max(self.seen[eng].get(s, 0), 0)
        semh = self.semh

        def emit(e):
            for ws, wv in wl:
                e.wait_ge(semh[ws], wv)
            fn(e).then_inc(semh[s], 1)
        self.streams[eng].append(emit)
        self._mark((s, v), r, w)
        self.n_ops += 1

    def dma(self, q, out, in_, r=(), w=(), sem=None):
        assert sem is not None
        s = 'd_' + str(sem)
        if s not in self.semh:
            self.semh[s] = self.nc.alloc_semaphore(name=s.replace(' ', '').replace(',', '_').replace('(', '').replace(')', '').replace("'", ''))
            self.cnt[s] = 0
        wl = self._waits(q, r, w)
        self.cnt[s] += 16
        v = self.cnt[s]
        semh = self.semh

        def emit(e):
            for ws, wv in wl:
                e.wait_ge(semh[ws], wv)
            e.dma_start(out=out, in_=in_).then_inc(semh[s], 16)
        self.streams[q].append(emit)
        self._mark((s, v), r, w)
        self.n_ops += 1

    def finish(self):
        nc = self.nc
        final = [(s, v) for s, v in self.cnt.items() if v > 0]
        semh = self.semh

        def fin(e):
            for s, v in final:
                e.wait_ge(semh[s], v)
        self.streams['sp'].append(fin)
        streams = self.streams
        with nc.Block() as block:
            @block.sync
            def _(e):
                for f in streams['sp']:
                    f(e)

            @block.scalar
            def _(e):
                for f in streams['act']:
                    f(e)

            @block.vector
            def _(e):
                for f in streams['dve']:
                    f(e)

            @block.gpsimd
            def _(e):
                for f in streams['pool']:
                    f(e)

            @block.tensor
            def _(e):
                for f in streams['pe']:
                    f(e)
        self.es.close()
        return nc
D = 1024; S = 4096; NCH = 8; TC = 512
FFH = 2816
LOG2 = np.log(2.0)


def lay_fm(W):
    K, M = W.shape
    return np.ascontiguousarray(W.reshape(K // 128, 128, M // 128, 128).transpose(1, 2, 0, 3))


def host_prep(inp):
    f32 = np.float32
    W = {}
    w_in = inp['ab_w_in'][0]
    y_w, xr_w = w_in[:, 0:512], w_in[:, 512:1024]
    q_w, k_w, v_w, g_w = (w_in[:, 1024 + i * 512:1024 + (i + 1) * 512] for i in range(4))
    perm = np.arange(512).reshape(8, 64)
    perm = np.concatenate([perm[:, 32:], perm[:, :32]], axis=1).reshape(-1)
    ext0 = np.concatenate([y_w, xr_w, q_w, q_w[:, perm], k_w, k_w[:, perm], g_w, v_w], axis=1)
    W['in0'] = lay_fm(ext0)
    ga = np.zeros((4, 128, 128), f32); gx = np.zeros((4, 128, 128), f32)
    for b in range(8):
        c, o = b // 2, (b % 2) * 64
        ga[c, o:o + 64, o:o + 64] = inp['gate_a_w'][0, b]
        gx[c, o:o + 64, o:o + 64] = inp['gate_x_w'][0, b]
    W['ga'] = np.ascontiguousarray(ga.transpose(1, 0, 2))
    W['gx'] = np.ascontiguousarray(gx.transpose(1, 0, 2))
    W['out0'] = lay_fm(inp['ab_w_out'][0])
    def ffw(l):
        W['w1_%d' % l] = lay_fm(inp['ffn_w1'][l])
        W['w3_%d' % l] = lay_fm(inp['ffn_w3'][l])
        W['w2_%d' % l] = lay_fm(inp['ffn_w2'][l])
    ffw(0)
    wn = inp['nsa_w_in'][0]
    qn = wn[:, 0:1024]
    kvw = wn[:, 1024:1024 + 1536].reshape(1024, 6, 4, 64)
    gw = wn[:, 2560:2608]
    p16 = np.arange(64); p16[0:8] = np.arange(8, 16); p16[8:16] = np.arange(0, 8)
    permq = (np.arange(16)[:, None] * 64 + p16[None, :]).reshape(-1)
    cols = [qn, qn[:, permq]]
    for g in range(4):
        for j in (2, 4):
            kk = kvw[:, j, g]; ks = kk[:, p16]
            cols += [np.concatenate([kk, kk], 1), np.concatenate([ks, ks], 1)]
        cols.append(np.concatenate([kvw[:, 3, g], kvw[:, 5, g]], 1))
        cols.append(np.concatenate([kvw[:, 0, g], kvw[:, 1, g]], 1))
    cols.append(np.concatenate([gw, np.zeros((1024, 80), f32)], 1))
    ext1 = np.concatenate(cols, axis=1)
    W['in1'] = lay_fm(ext1)
    W['out1'] = lay_fm(inp['nsa_w_out'][0])
    w1k = inp['cmp_k_w1'][0].reshape(32, 64, 64).transpose(1, 0, 2)
    w1v = inp['cmp_v_w1'][0].reshape(32, 64, 64).transpose(1, 0, 2)
    W['cw1'] = np.ascontiguousarray(np.concatenate([w1k, w1v], 0))
    w2k = inp['cmp_k_w2'][0]; w2v = inp['cmp_v_w2'][0]
    top = np.zeros((64, 4, 128), f32); bot = np.zeros((64, 4, 128), f32)
    top[:, 0] = np.concatenate([w2k, w2k], 1); top[:, 1] = np.concatenate([w2k[:, p16], w2k[:, p16]], 1)
    bot[:, 2, 0:64] = w2v
    W['cw2'] = np.ascontiguousarray(np.concatenate([top, bot], 0))
    ffw(1)
    offs = {}; parts = []; o = 0
    for n, a in W.items():
        a2 = a.reshape(128, -1).astype(f32)
        offs[n] = (o, a2.shape[1], a.shape[1:]); parts.append(a2); o += a2.shape[1]
    pad = (-o) % 2048
    if pad:
        parts.append(np.zeros((128, pad), f32)); o += pad
    wcat = np.ascontiguousarray(np.concatenate(parts, axis=1))

    def col4(v):
        return np.ascontiguousarray(v.reshape(4, 128).T)

    def col8(v):
        return np.ascontiguousarray(v.reshape(8, 128).T)
    sp = {}
    sp['g_attn0'] = col8(inp['attn_norm'][0]); sp['g_attn1'] = col8(inp['attn_norm'][1])
    sp['g_ffn0'] = col8(inp['ffn_norm'][0]); sp['g_ffn1'] = col8(inp['ffn_norm'][1])
    sp['g_fin'] = col8(inp['final_norm'])
    for i in range(4):
        sp['cw%d' % i] = col4(inp['conv_w'][0, i])
    sp['cb'] = col4(inp['conv_b'][0]); sp['gab'] = col4(inp['gate_a_b'][0]); sp['gxb'] = col4(inp['gate_x_b'][0])
    sp['lam'] = col4(inp['lru_lambda'][0])
    pek = inp['cmp_pe_k'][0].T; pev = inp['cmp_pe_v'][0].T
    sp['pe'] = np.ascontiguousarray(np.concatenate([pek, pev], 0))
    soffs = {}; sparts = []; o = 0
    for n, a in sp.items():
        soffs[n] = (o, a.shape[1]); sparts.append(a.astype(f32)); o += a.shape[1]
    spcat = np.ascontiguousarray(np.concatenate(sparts, 1))
    return wcat, offs, spcat, soffs


def const_tables():
    f32 = np.float32
    C = {}
    pos = np.arange(S, dtype=np.float64)
    inv = 10000.0 ** (-np.arange(0, 64, 2, dtype=np.float64) / 64)
    d = np.arange(128) % 64
    ang = pos[None, :] * inv[d % 32][:, None]
    sgn = np.where(d < 32, -1.0, 1.0)[:, None]
    C['rcos'] = np.cos(ang).astype(f32); C['rsin'] = (np.sin(ang) * sgn).astype(f32)
    C['rcosk'] = (np.cos(ang) * 0.125).astype(f32); C['rsink'] = (np.sin(ang) * sgn * 0.125).astype(f32)
    hh = np.arange(8, dtype=np.float64)
    log_g = np.log1p(-(2.0 ** (-5.0 - hh)))
    ci = np.arange(128, dtype=np.float64)
    qdec = np.zeros((4, 128, 512)); kdec = np.zeros((128, 8)); dm = np.zeros((4, 128, 2, 128)); cdec = np.zeros((128, 4))
    for h in range(8):
        j, o = h // 2, (h % 2) * 64
        qdec[j, o:o + 64, :] = np.tile(np.exp((ci + 1.0) * log_g[h]), 4)[None, :]
        kdec[:, h] = np.exp((127.0 - ci) * log_g[h])
        diff = ci[None, :] - ci[:, None]
        dm[j, :, h % 2, :] = np.where(diff >= 0, np.exp(np.maximum(diff, 0) * log_g[h]), 0.0)
        cdec[o:o + 64, j] = np.exp(128.0 * log_g[h])
    C['qdec'] = qdec.astype(f32); C['kdec'] = kdec.astype(f32); C['dm'] = dm.astype(f32); C['cdec'] = cdec.astype(f32)
    avg = np.zeros((128, 128)); avg[0:64, 0:64] = 1 / 64; avg[64:, 64:] = 1 / 64
    C['avg'] = avg.astype(f32)
    C['ones'] = np.ones((128, 128), f32)
    C['ident'] = np.eye(128, dtype=f32)
    inv2 = 500000.0 ** (-np.arange(0, 16, 2, dtype=np.float64) / 16)
    fi = np.where(d < 16, d % 8, 0)
    ang2 = pos[None, :] * inv2[fi][:, None]
    rot = (d < 16)[:, None]
    sg2 = np.where(d < 8, -1.0, 1.0)[:, None]
    ncos = np.where(rot, np.cos(ang2), 1.0); nsin = np.where(rot, np.sin(ang2) * sg2, 0.0)
    C['ncosq'] = (ncos * 0.125).astype(f32); C['nsinq'] = (nsin * 0.125).astype(f32)
    C['ncos'] = ncos.astype(f32); C['nsin'] = nsin.astype(f32)
    cend = (np.arange(255) * 16 + 31).astype(np.float64)
    angc = cend[None, :] * inv2[fi][:, None]
    cc = np.where(rot, np.cos(angc), 1.0); cs = np.where(rot, np.sin(angc) * sg2, 0.0)
    ccp = np.zeros((128, 256)); csp = np.zeros((128, 256)); ccp[:, :255] = cc; csp[:, :255] = cs
    C['ccos'] = ccp.astype(f32); C['csin'] = csp.astype(f32)
    n = np.arange(256)
    vis = ((n[:, None] * 16 + 31) <= pos[None, :]) & (n[:, None] < 255)
    C['vis'] = np.ascontiguousarray(vis.reshape(2, 128, S).transpose(1, 0, 2)).astype(f32)
    cst = np.arange(255)[:, None] * 16; sst = np.arange(64)[None, :] * 64
    ov = np.clip(np.minimum(cst + 32, sst + 64) - np.maximum(cst, sst), 0, None) / 32.0
    ovp = np.zeros((256, 64)); ovp[:255] = ov
    C['c2s'] = np.ascontiguousarray(ovp.reshape(2, 128, 64).transpose(1, 0, 2)).astype(f32)
    q = np.arange(S)[:, None]; blk = np.arange(64)[None, :]
    cur = q // 64
    causal = (blk * 64 <= q)
    Am = np.where(causal, 0.0, -1e30)
    Am = np.where(blk == 0, 1e30, Am)
    Am = np.where((blk == cur - 1) & (blk != 0), 2e30, Am)
    Am = np.where((blk == cur) & (blk != 0), 3e30, Am)
    Cm = (causal & ~((blk == 0) | (blk == cur) | (blk == cur - 1))).astype(np.float64)
    C['tkC'] = np.ascontiguousarray(Cm.reshape(32, 128, 64).transpose(1, 0, 2)).astype(f32)
    C['tkA'] = np.ascontiguousarray(Am.reshape(32, 128, 64).transpose(1, 0, 2)).astype(f32)
    kk = np.arange(128)[:, None]; qq = np.arange(128)[None, :]
    C['caus'] = (kk <= qq).astype(f32)
    C['wlow'] = (kk > qq).astype(f32)
    sel = np.zeros((128, 16, 3, 64))
    for h in range(16):
        for b in range(3):
            sel[h * 3 + b, h, b, :] = 1.0
    C['gsel'] = sel.astype(f32)
    return C


def const_tables2(C):
    f32 = np.float32
    kk = np.arange(128)[:, None, None]; o = np.arange(8)[None, :, None]; qc = np.arange(512)[None, None, :]
    dist = qc - ((o - 4) * 128 + kk)
    o2 = np.arange(2)[None, :, None]; q2 = np.arange(256)[None, None, :]
    C['cm2'] = ((o2 * 128 + kk) <= q2).astype(f32)
    o6 = np.arange(6)[None, :, None]
    d6 = q2 - ((o6 - 4) * 128 + kk)
    C['wm2'] = ((d6 >= 0) & (d6 < 512)).astype(f32)
    return C
EPS = 1e-6


class Ctx:
    pass


def mm(k, out, lhsT, rhs, start, stop, r, w):
    k.op('pe', lambda e: e.matmul(out, lhsT=lhsT, rhs=rhs, start=start, stop=stop), r=r, w=w)


def act(k, out, in_, func, r, w, bias=None, scale=None):
    kw = {}
    if bias is not None:
        kw['bias'] = bias
    if scale is not None:
        kw['scale'] = scale
    k.op('act', lambda e: e.activation(out=out, in_=in_, func=func, **kw), r=r, w=w)


def tt(k, eng, out, in0, in1, op, r, w):
    k.op(eng, lambda e: e.tensor_tensor(out=out, in0=in0, in1=in1, op=op), r=r, w=w)


def ts(k, eng, out, in0, s1, s2, op0, op1, r, w):
    if op1 is None:
        k.op(eng, lambda e: e.tensor_scalar(out=out, in0=in0, scalar1=s1, scalar2=None, op0=op0), r=r, w=w)
    else:
        k.op(eng, lambda e: e.tensor_scalar(out=out, in0=in0, scalar1=s1, scalar2=s2, op0=op0, op1=op1), r=r, w=w)


def stt(k, out, in0, sc, in1, op0, op1, r, w):
    k.op('dve', lambda e: e.scalar_tensor_tensor(out=out, in0=in0, scalar=sc, in1=in1, op0=op0, op1=op1), r=r, w=w)


def cp(k, eng, out, in_, r, w):
    if eng == 'act':
        k.op('act', lambda e: e.activation(out=out, in_=in_, func=AF.Copy), r=r, w=w)
    else:
        k.op(eng, lambda e: e.tensor_copy(out=out, in_=in_), r=r, w=w)


def barrier(k):
    final = [(s, v) for s, v in k.cnt.items() if v > 0 and not s.startswith("d_('wbs'")]
    semh = k.semh
    for eng in k.ALL:
        wl = [(s, v) for s, v in final if k.seen[eng].get(s, 0) < v and not (s == 'tl_' + eng)]
        for s, v in wl:
            k.seen[eng][s] = v

        def emit(e, wl=wl):
            for s, v in wl:
                e.wait_ge(semh[s], v)
        k.streams[eng].append(emit)
    k.track = {kk: vv for kk, vv in k.track.items() if isinstance(kk, tuple) and kk[0] == 'wb'}


def load_w(k, c, dst, name, lo, n, key, q='sp'):
    o = c.offs[name][0]
    k.dma(q, dst, c.WB[:, o + lo:o + lo + n], r=[('wb', name)], w=[key], sem=key)


def norm_chunk(k, c, src, srckey, gcol, dst_fn, dstkeys, ps, pskey):
    sq = c.sq
    act(k, sq[:].bitcast(F32R), src, AF.Square, r=[srckey], w=['sq'])
    for kc in range(8):
        mm(k, ps[:, :], c.ones32[:].bitcast(F32R), sq[:, kc, :].bitcast(F32R), kc == 0, kc == 7, r=['sq', 'ones32'], w=[pskey])
    act(k, c.rs[:], ps[:, :], AF.Ln, r=[pskey, 'epsc'], w=['rs'], scale=1.0 / 1024, bias=c.epsc[:, 0:1])
    act(k, c.rs[:], c.rs[:], AF.Exp, r=['rs'], w=['rs'], scale=-0.5)
    for kc in range(8):
        stt(k, dst_fn(kc), src[:, kc, :], gcol[:, kc:kc + 1], c.rs[:], ALU.mult, ALU.mult,
            r=[srckey, 'rs', 'sp'], w=[dstkeys[kc] if isinstance(dstkeys, list) else dstkeys])


def phase_weights(k, c, names):
    CH = 4096
    for name in names:
        o, n, _ = c.offs[name]
        i = 0
        for lo in range(0, n, CH):
            hi = min(n, lo + CH)
            k.dma('pool', c.WB[:, o + lo:o + hi], c.WC[:, o + lo:o + hi], w=[('wbc', name, i)], sem=('wbs', name))
            i += 1
        k.track[('wb', name)] = {'w': ("d_" + str(('wbs', name)), k.cnt["d_" + str(('wbs', name))]), 'r': {}}


def phase_consts(k, c):
    k.dma('sp', c.spt[:], c.SPD[:, :], w=['sp'], sem='sp')
    k.dma('sp', c.onesf[:], c.CD['ones'][:, :], w=['onesf'], sem='onesf')
    act(k, c.ones32[:].bitcast(F32R), c.onesf[:], AF.Copy, r=['onesf'], w=['ones32'])
    k.op('dve', lambda e: e.memset(c.epsc[:], EPS), w=['epsc'])


def phase_norm0(k, c, SRC, gname):
    src_v = SRC.rearrange("(kc p) s -> p kc s", p=128)
    g = c.sp(gname)
    for ch in range(NCH):
        sl = ch % 2
        k.dma('sp', c.xc[sl][:], src_v[:, :, ch * TC:(ch + 1) * TC], w=[('xc', sl)], sem=('xc', sl))
        norm_chunk(k, c, c.xc[sl][:], ('xc', sl), g, lambda kc: c.hn[:, kc, ch * TC:(ch + 1) * TC],
                   ('hn', ch), c.pst[6], 'ps6')


def phase_outproj(k, c, wname, RES, HOUT, gname):
    res_v = RES.rearrange("(kc p) s -> p kc s", p=128)
    out_v = HOUT.rearrange("(kc p) s -> p kc s", p=128)
    ot_v = c.OT.rearrange("(kc p) s -> p kc s", p=128)
    load_w(k, c, c.wout[:].rearrange("p a b m -> p (a b m)"), wname, 0, 8 * 8 * 128, 'wout')
    g = c.sp(gname)
    for ch in range(NCH):
        sl = ch % 2
        cs = slice(ch * TC, (ch + 1) * TC)
        k.dma('sp', c.oc[sl][:], ot_v[:, :, cs], w=[('oc', sl)], sem=('oc', sl))
        k.dma('sp', c.xc[sl][:], res_v[:, :, cs], w=[('xc', sl)], sem=('xc', sl))
        for m in range(8):
            ps = c.pst[m % 4]; pk = 'ps%d' % (m % 4)
            for kc in range(8):
                mm(k, ps[:, :], c.wout[:, m, kc, :], c.oc[sl][:, kc, :], kc == 0, kc == 7, r=['wout', ('oc', sl)], w=[pk])
            tt(k, 'dve', c.xc[sl][:, m, :], ps[:, :], c.xc[sl][:, m, :], ALU.add, r=[pk, ('xc', sl)], w=[('xc', sl)])
        k.dma('sp', out_v[:, :, cs], c.xc[sl][:], r=[('xc', sl)], w=[], sem=('xcs', sl))
        norm_chunk(k, c, c.xc[sl][:], ('xc', sl), g, lambda kc: c.hn[:, kc, cs], ('hn', ch), c.pst[6], 'ps6')


def phase_ffn(k, c, l, RES, HOUT, gname, final):
    res_v = RES.rearrange("(kc p) s -> p kc s", p=128)
    out_v = HOUT.rearrange("(kc p) s -> p kc s", p=128)
    g = c.sp(gname)
    TS = 1024
    for sc in range(S // TS):
        for j in range(22):
            sl = j % 2
            load_w(k, c, c.w13[sl][:, 0].rearrange("p kc m -> p (kc m)"), 'w1_%d' % l, j * 1024, 1024, ('w1', sl))
            load_w(k, c, c.w13[sl][:, 1].rearrange("p kc m -> p (kc m)"), 'w3_%d' % l, j * 1024, 1024, ('w3', sl))
            for hf in range(2):
                cs = slice(sc * TS + hf * 512, sc * TS + (hf + 1) * 512)
                ch = (sc * TS + hf * 512) // TC
                pa = c.pst[2 * hf]; pb = c.pst[2 * hf + 1]; ka = 'ps%d' % (2 * hf); kb = 'ps%d' % (2 * hf + 1)
                for kc in range(8):
                    mm(k, pa[:, :], c.w13[sl][:, 0, kc, :], c.hn[:, kc, cs], kc == 0, kc == 7, r=[('w1', sl), ('hn', ch)], w=[ka])
                for kc in range(8):
                    mm(k, pb[:, :], c.w13[sl][:, 1, kc, :], c.hn[:, kc, cs], kc == 0, kc == 7, r=[('w3', sl), ('hn', ch)], w=[kb])
                act(k, c.sil[hf][:], pa[:, :], AF.Silu, r=[ka], w=[('sil', hf)])
                tt(k, 'dve', c.hid[:, j, hf * 512:(hf + 1) * 512], c.sil[hf][:], pb[:, :], ALU.mult, r=[('sil', hf), kb], w=[('hid', j, hf)])
        for hf in range(2):
            sl = hf
            cs = slice(sc * TS + hf * 512, sc * TS + (hf + 1) * 512)
            k.dma('sp', c.xc[sl][:], res_v[:, :, cs], w=[('xc', sl)], sem=('xc', sl))
        for m in range(8):
            sl = m % 2
            load_w(k, c, c.w2t[sl][:].rearrange("p kc m -> p (kc m)"), 'w2_%d' % l, m * 22 * 128, 22 * 128, ('w2', sl))
            for hf in range(2):
                ps = c.pst[4 + hf]; pk = 'ps%d' % (4 + hf)
                for j in range(22):
                    mm(k, ps[:, :], c.w2t[sl][:, j, :], c.hid[:, j, hf * 512:(hf + 1) * 512], j == 0, j == 21,
                       r=[('w2', sl), ('hid', j, hf)], w=[pk])
                tt(k, 'dve', c.xc[hf][:, m, :], ps[:, :], c.xc[hf][:, m, :], ALU.add, r=[pk, ('xc', hf)], w=[('xc', hf)])
        for hf in range(2):
            cs = slice(sc * TS + hf * 512, sc * TS + (hf + 1) * 512)
            ch = (sc * TS + hf * 512) // TC
            if not final:
                k.dma('sp', out_v[:, :, cs], c.xc[hf][:], r=[('xc', hf)], w=[], sem=('xcs', hf))
                norm_chunk(k, c, c.xc[hf][:], ('xc', hf), g, lambda kc: c.hn[:, kc, cs], ('hn', ch), c.pst[6], 'ps6')
            else:
                norm_chunk(k, c, c.xc[hf][:], ('xc', hf), g, lambda kc: c.xc[hf][:, kc, :], ('xc', hf), c.pst[6], 'ps6')
                k.dma('sp', out_v[:, :, cs], c.xc[hf][:], r=[('xc', hf)], w=[], sem=('xcs', hf))


def phase_mix0(k, c):
    hn = c.hn
    T = 1024
    lam = c.sp('lam')
    act(k, c.cv[:, 0:4], lam, AF.Exp, r=['sp'], w=['cv'], scale=-1.0)
    act(k, c.cv[:, 0:4], c.cv[:, 0:4], AF.Ln, r=['cv', 'onec'], w=['cv'], bias=c.onec[:, 0:1], scale=1.0)
    ts(k, 'dve', c.cv[:, 4:8], c.cv[:, 0:4], -16.0, None, ALU.mult, None, r=['cv'], w=['cv2'])
    ts(k, 'dve', c.cv[:, 0:4], c.cv[:, 0:4], -8.0, None, ALU.mult, None, r=['cv', 'cv2'], w=['cv'])
    k.dma('sp', c.avg[:], c.CD['avg'][:, :], w=['avg'], sem='avg')
    k.dma('sp', c.kdec[:], c.CD['kdec'][:, :], w=['kdec'], sem='kdec')
    k.dma('sp', c.cdec[:], c.CD['cdec'][:, :], w=['cdec'], sem='cdec')
    k.dma('sp', c.identf[:], c.CD['ident'][:, :], w=['identf'], sem='identf')
    cp(k, 'dve', c.identb[:], c.identf[:], r=['identf'], w=['identb'])
    load_w(k, c, c.gaw[:].rearrange("p a m -> p (a m)"), 'ga', 0, 512, 'gaw')
    load_w(k, c, c.gxw[:].rearrange("p a m -> p (a m)"), 'gx', 0, 512, 'gxw')
    k.op('pool', lambda e: e.memset(c.KRA[:], 0.0), w=['KRA'])
    k.op('pool', lambda e: e.memset(c.KRB[:], 0.0), w=['KRB'])
    WY, WX, WQ, WQS, WK, WKS, WG, WV = range(8)
    for j in range(4):
        for typ in range(8):
            load_w(k, c, c.gw[:, typ].rearrange("p kc m -> p (kc m)"), 'in0', (typ * 4 + j) * 1024, 1024, ('gw', typ))
        k.dma('sp', c.qdec[:], c.CD['qdec'][j], w=['qdec'], sem='qdec')
        k.dma('sp', c.dm[:], c.CD['dm'][j], w=['dm'], sem='dm')

        def proj(typ, ch, ps, pk):
            for kc in range(8):
                mm(k, ps[:, :], c.gw[:, typ, kc, :], hn[:, kc, ch * TC:(ch + 1) * TC], kc == 0, kc == 7,
                   r=[('gw', typ), ('hn', ch)], w=[pk])
        def lru_stages(tb):
            def s0():
                if tb == 0:
                    k.op('dve', lambda e: e.memset(c.XR[:, 0:3], 0.0), w=['XR'])
                else:
                    cp(k, 'dve', c.XR[:, 0:3], c.XR[:, T:T + 3], r=['XR'], w=['XR'])
                for cc in range(2):
                    ch = tb * 2 + cc
                    proj(WY, ch, c.pst[0], 'ps0')
                    cp(k, 'act', c.Y[:, cc * 512:(cc + 1) * 512], c.pst[0][:, :], r=['ps0'], w=['Y'])
                    proj(WX, ch, c.pst[1], 'ps1')
                    cp(k, 'act', c.XR[:, 3 + cc * 512:3 + (cc + 1) * 512], c.pst[1][:, :], r=['ps1'], w=['XR'])

            def s1():
                ts(k, 'dve', c.XC[:], c.XR[:, 3:3 + T], c.sp('cw3')[:, j:j + 1], c.sp('cb')[:, j:j + 1], ALU.mult, ALU.add,
                   r=['XR', 'sp'], w=['XC'])
                for i in range(3):
                    stt(k, c.XC[:], c.XR[:, i:i + T], c.sp('cw%d' % i)[:, j:j + 1], c.XC[:], ALU.mult, ALU.add,
                        r=['XR', 'XC', 'sp'], w=['XC'])
                cp(k, 'act', c.XCb[:], c.XC[:], r=['XC'], w=['XCb'])

            def s2():
                for cc in range(2):
                    cs = slice(cc * 512, (cc + 1) * 512)
                    mm(k, c.pst[2][:, :], c.gaw[:, j, :], c.XCb[:, cs], True, True, r=['gaw', 'XCb'], w=['ps2'])
                    act(k, c.R[:, cs], c.pst[2][:, :], AF.Sigmoid, r=['ps2', 'sp'], w=['R'], bias=c.sp('gab')[:, j:j + 1], scale=1.0)
                    mm(k, c.pst[3][:, :], c.gxw[:, j, :], c.XCb[:, cs], True, True, r=['gxw', 'XCb'], w=['ps3'])
                    act(k, c.I[:, cs], c.pst[3][:, :], AF.Sigmoid, r=['ps3', 'sp'], w=['I'], bias=c.sp('gxb')[:, j:j + 1], scale=1.0)

            def s3():
                act(k, c.A[:], c.R[:], AF.Exp, r=['R', 'cv'], w=['A'], scale=c.cv[:, j:j + 1])
                act(k, c.R[:], c.R[:], AF.Exp, r=['R', 'cv2'], w=['R'], scale=c.cv[:, 4 + j:5 + j])
                act(k, c.R[:], c.R[:], AF.Sqrt, r=['R', 'onec'], w=['R'], scale=-1.0, bias=c.onec[:, 0:1])

            def s4():
                tt(k, 'dve', c.I[:], c.I[:], c.R[:], ALU.mult, r=['I', 'R'], w=['I'])
                tt(k, 'dve', c.I[:], c.I[:], c.XC[:], ALU.mult, r=['I', 'XC'], w=['I'])
                if tb == 0:
                    k.op('dve', lambda e: e.tensor_tensor_scan(out=c.XC[:], data0=c.A[:], data1=c.I[:], initial=0.0,
                                                               op0=ALU.mult, op1=ALU.add), r=['A', 'I', 'XC'], w=['XC'])
                else:
                    k.op('dve', lambda e: e.tensor_tensor_scan(out=c.XC[:], data0=c.A[:], data1=c.I[:], initial=c.hl[:, 0:1],
                                                               op0=ALU.mult, op1=ALU.add), r=['A', 'I', 'XC', 'hl'], w=['XC'])
                cp(k, 'dve', c.hl[:, 0:1], c.XC[:, T - 1:T], r=['XC'], w=['hl'])

            def s5():
                act(k, c.Y[:], c.Y[:], AF.Gelu_apprx_tanh, r=['Y'], w=['Y'])
                tt(k, 'dve', c.ob[:], c.XC[:], c.Y[:], ALU.mult, r=['XC', 'Y'], w=['ob'])
                k.dma('sp', c.OT[j * 128:(j + 1) * 128, tb * T:(tb + 1) * T], c.ob[:], r=['ob'], w=[], sem='ob')
            return [s0, s1, s2, s3, s4, s5]
        k.op('dve', lambda e: e.memset(c.ST32[:], 0.0), w=['ST32'])
        k.op('dve', lambda e: e.memset(c.STp[:], 0.0), w=['STp'])

        def ret_stages(ch):
            cs = slice(ch * TC, (ch + 1) * TC)

            def r0():
                def ld_tabs(chx):
                    rsl = (chx % 2) * 4
                    csx = slice(chx * TC, (chx + 1) * TC)
                    for ti, nm in enumerate(('rcos', 'rsin', 'rcosk', 'rsink')):
                        k.dma('act', c.rt[rsl + ti][:], c.CD[nm][:, csx], w=[('rt', rsl + ti)], sem=('rt', rsl + ti))
                if ch == 0:
                    ld_tabs(0)
                if ch + 1 < NCH:
                    ld_tabs(ch + 1)
                proj(WQ, ch, c.pst[0], 'ps0'); proj(WQS, ch, c.pst[1], 'ps1')
                tt(k, 'dve', c.t1[:], c.pst[0][:, :], c.rt[(ch % 2) * 4 + 0][:], ALU.mult, r=['ps0', ('rt', (ch % 2) * 4 + 0)], w=['t1'])
                tt(k, 'dve', c.t2[:], c.pst[1][:, :], c.rt[(ch % 2) * 4 + 1][:], ALU.mult, r=['ps1', ('rt', (ch % 2) * 4 + 1)], w=['t2'])
                tt(k, 'pool', c.QR[:], c.t1[:], c.t2[:], ALU.add, r=['t1', 't2'], w=['QR'])
                tt(k, 'pool', c.QD[:], c.QR[:], c.qdec[:], ALU.mult, r=['QR', 'qdec'], w=['QD'])

            def r1():
                proj(WK, ch, c.pst[0], 'ps0'); proj(WKS, ch, c.pst[1], 'ps1')
                tt(k, 'dve', c.t1[:], c.pst[0][:, :], c.rt[(ch % 2) * 4 + 2][:], ALU.mult, r=['ps0', ('rt', (ch % 2) * 4 + 2)], w=['t1'])
                tt(k, 'dve', c.t2[:], c.pst[1][:, :], c.rt[(ch % 2) * 4 + 3][:], ALU.mult, r=['ps1', ('rt', (ch % 2) * 4 + 3)], w=['t2'])
                tt(k, 'pool', c.KR[:], c.t1[:], c.t2[:], ALU.add, r=['t1', 't2'], w=['KR'])
                cp(k, 'act', c.KRA[0:64, :], c.KR[0:64, :], r=['KR'], w=['KRA'])
                cp(k, 'act', c.KRB[64:128, :], c.KR[64:128, :], r=['KR'], w=['KRB'])

            def r2():
                proj(WG, ch, c.pst[0], 'ps0')
                act(k, c.SG[:], c.pst[0][:, :], AF.Silu, r=['ps0'], w=['SG'])
                for s in range(4):
                    tk = slice(ch * TC + s * 128, ch * TC + (s + 1) * 128)
                    for kc in range(8):
                        mm(k, c.pst[2][:, s * 128:(s + 1) * 128], hn[:, kc, tk], c.gw[:, WV, kc, :], kc == 0, kc == 7,
                           r=[('gw', WV), ('hn', ch)], w=['ps2'])
                cp(k, 'act', c.VT[:].rearrange("p s m -> p (s m)"), c.pst[2][:, :], r=['ps2'], w=['VT'])
                for s in range(4):
                    k.op('pe', lambda e, s=s: e.transpose(c.psb[:, s * 128:(s + 1) * 128], c.KR[:, s * 128:(s + 1) * 128], c.identb[:]),
                         r=['KR', 'identb'], w=['psb'])
                for hh in range(2):
                    ts(k, 'dve', c.KD[:, :, hh * 64:(hh + 1) * 64],
                       c.psb[:, 0:512].rearrange("p (s m) -> p s m", s=4)[:, :, hh * 64:(hh + 1) * 64],
                       c.kdec[:, 2 * j + hh:2 * j + hh + 1], None, ALU.mult, None, r=['psb', 'kdec'], w=['KD'])

            def sub(s):
                def f():
                    sc_ = slice(s * 128, (s + 1) * 128)
                    for hh in range(2):
                        KRp = c.KRA if hh == 0 else c.KRB
                        ps_s = c.pst[3 + hh]; pks = 'ps%d' % (3 + hh)
                        mm(k, ps_s[:, 0:128], KRp[:, sc_], c.QR[:, sc_], True, True, r=['KRA', 'KRB', 'QR'], w=[pks])
                    for hh in range(2):
                        ps_s = c.pst[3 + hh]; pks = 'ps%d' % (3 + hh)
                        tt(k, 'dve', c.SD[:, hh, :], ps_s[:, 0:128], c.dm[:, hh, :], ALU.mult, r=[pks, 'dm'], w=[('SD', hh)])
                    for hh in range(2):
                        hs = slice(hh * 64, (hh + 1) * 64)
                        mm(k, c.pst[5][hs, sc_], c.VT[:, s, hs], c.SD[:, hh, :], True, False, r=['VT', ('SD', hh)], w=['ps5'])
                        mm(k, c.pst[5][hs, sc_], c.STp[:, hh, :], c.QD[:, sc_], False, True, r=['STp', 'QD'], w=['ps5'])
                    for hh in range(2):
                        hs = slice(hh * 64, (hh + 1) * 64)
                        mm(k, c.pst[2][hs, 0:64], c.KD[:, s, hs], c.VT[:, s, hs], True, True, r=['KD', 'VT'], w=['ps2'])
                    stt(k, c.ST32[:], c.ST32[:], c.cdec[:, j:j + 1], c.pst[2][:, 0:64], ALU.mult, ALU.add,
                        r=['ST32', 'ps2', 'cdec'], w=['ST32'])
                    cp(k, 'dve', c.STp[0:64, 0, :], c.ST32[0:64, :], r=['ST32'], w=['STp'])
                    cp(k, 'dve', c.STp[64:128, 1, :], c.ST32[64:128, :], r=['ST32'], w=['STp'])
                return f

            def r7():
                cp(k, 'act', c.O32[:], c.pst[5][:, :], r=['ps5'], w=['O32'])
                mm(k, c.pst[6][:, :], c.avg[:], c.O32[:], True, True, r=['avg', 'O32'], w=['ps6'])
                tt(k, 'dve', c.O32[:], c.O32[:], c.pst[6][:, :], ALU.subtract, r=['O32', 'ps6'], w=['O32'])
                act(k, c.SQr[:], c.O32[:], AF.Square, r=['O32'], w=['SQr'])
                mm(k, c.pst[6][:, :], c.avg[:], c.SQr[:], True, True, r=['avg', 'SQr'], w=['ps6'])
                act(k, c.SQr[:], c.pst[6][:, :], AF.Ln, r=['ps6', 'epsc'], w=['SQr'], scale=1.0, bias=c.epsc[:, 0:1])
                act(k, c.SQr[:], c.SQr[:], AF.Exp, r=['SQr'], w=['SQr'], scale=-0.5)
                tt(k, 'dve', c.O32[:], c.O32[:], c.SQr[:], ALU.mult, r=['O32', 'SQr'], w=['O32'])
                tt(k, 'dve', c.ob2[:], c.O32[:], c.SG[:], ALU.mult, r=['O32', 'SG'], w=['ob2'])
                k.dma('sp', c.OT[512 + j * 128:512 + (j + 1) * 128, cs], c.ob2[:], r=['ob2'], w=[], sem='ob2')
            return [r0, r1, r2, sub(0), sub(1), sub(2), sub(3), r7]

        Lq = [st for tb in range(S // T) for st in lru_stages(tb)]
        Rq = [st for ch in range(NCH) for st in ret_stages(ch)]
        li = ri = 0
        while li < len(Lq) or ri < len(Rq):
            if ri < len(Rq) and (li >= len(Lq) or ri * len(Lq) <= li * len(Rq)):
                Rq[ri](); ri += 1
            else:
                Lq[li](); li += 1


def phase_nsa_a(k, c):
    hn = c.hn
    cnt = [0]

    def ldw(ci):
        sl = cnt[0] % 4; cnt[0] += 1
        load_w(k, c, c.gw4[sl][:].rearrange("p kc m -> p (kc m)"), 'in1', ci * 1024, 1024, ('gw4', sl), q='act')
        return sl

    def proj(sl, ch, ps, pk):
        for kc in range(8):
            mm(k, ps[:, :], c.gw4[sl][:, kc, :], hn[:, kc, ch * TC:(ch + 1) * TC], kc == 0, kc == 7,
               r=[('gw4', sl), ('hn', ch)], w=[pk])
    ob_i = [0]

    def store(dst, src_fn, view=None):
        sl = ob_i[0] % 2; ob_i[0] += 1
        src_fn(c.obn[sl], ('obn', sl))
        src = c.obn[sl][:] if view is None else view(c.obn[sl])
        k.dma('sp', dst, src, r=[('obn', sl)], w=[], sem=('obn', sl))

    def rot_job(ci, cis, cosn, sinn, dst_rows):
        sa = ldw(ci); sb_ = ldw(cis)
        for ch in range(NCH):
            cs = slice(ch * TC, (ch + 1) * TC)
            tsl = ch % 2
            k.dma('act', c.rt[tsl][:], c.CD[cosn][:, cs], w=[('rt', tsl)], sem=('rt', tsl))
            k.dma('act', c.rt[2 + tsl][:], c.CD[sinn][:, cs], w=[('rt', 2 + tsl)], sem=('rt', 2 + tsl))
            pa, ka = c.pst[2 * tsl], 'ps%d' % (2 * tsl)
            pb, kb = c.pst[2 * tsl + 1], 'ps%d' % (2 * tsl + 1)
            proj(sa, ch, pa, ka); proj(sb_, ch, pb, kb)
            tt(k, 'dve', c.t1[:], pa[:, :], c.rt[tsl][:], ALU.mult, r=[ka, ('rt', tsl)], w=['t1'])
            tt(k, 'dve', c.t2[:], pb[:, :], c.rt[2 + tsl][:], ALU.mult, r=[kb, ('rt', 2 + tsl)], w=['t2'])
            store(dst_rows[:, cs], lambda o, ok: tt(k, 'pool', o[:], c.t1[:], c.t2[:], ALU.add, r=['t1', 't2'], w=[ok]))
    for c2 in range(8):
        rot_job(c2, 8 + c2, 'ncosq', 'nsinq', c.QS[c2 * 128:(c2 + 1) * 128, :])
    for g in range(4):
        base = 16 + 6 * g
        rot_job(base + 0, base + 1, 'ncos', 'nsin', c.KSD[g * 128:(g + 1) * 128, :])
        rot_job(base + 2, base + 3, 'ncos', 'nsin', c.KWD[g * 128:(g + 1) * 128, :])
        sa = ldw(base + 5)
        for ch in range(NCH):
            cs = slice(ch * TC, (ch + 1) * TC)
            pa, ka = c.pst[ch % 4], 'ps%d' % (ch % 4)
            proj(sa, ch, pa, ka)
            store(c.KVCD[g * 128:(g + 1) * 128, cs], lambda o, ok: cp(k, 'act', o[:], pa[:, :], r=[ka], w=[ok]))
        sa = ldw(base + 4)
        for ch in range(NCH):
            pa, ka = c.pst[4 + ch % 2], 'ps%d' % (4 + ch % 2)
            for s in range(4):
                tk = slice(ch * TC + s * 128, ch * TC + (s + 1) * 128)
                for kc in range(8):
                    mm(k, pa[:, s * 128:(s + 1) * 128], hn[:, kc, tk], c.gw4[sa][:, kc, :], kc == 0, kc == 7,
                       r=[('gw4', sa), ('hn', ch)], w=[ka])
            dst = c.VD[g, ch * TC:(ch + 1) * TC, :].rearrange("(s p) d -> p s d", p=128)
            store(dst, lambda o, ok: cp(k, 'act', o[:], pa[:, :], r=[ka], w=[ok]),
                  view=lambda o: o[:].rearrange("p (s d) -> p s d", s=4))
    sa = ldw(40)
    for ch in range(NCH):
        cs = slice(ch * TC, (ch + 1) * TC)
        pa, ka = c.pst[ch % 4], 'ps%d' % (ch % 4)
        proj(sa, ch, pa, ka)
        store(c.GSD[:, cs], lambda o, ok: act(k, o[:], pa[:, :], AF.Sigmoid, r=[ka], w=[ok]))


def phase_nsa_b(k, c):
    QB = 256
    NQB = S // QB
    def ldc(dst, src, key, tmp=None):
        k.dma('sp', dst, src, w=[key], sem=key)
    k.dma('sp', c.identfN[:], c.CD['ident'][:, :], w=['identf'], sem='identf')
    cp(k, 'dve', c.identbN[:], c.identfN[:], r=['identf'], w=['identb'])
    def ldcb(dst2d, src2d, n, key):
        k.dma('sp', c.cst[:, 0:n], src2d, w=['cst'], sem='cst')
        cp(k, 'dve', dst2d, c.cst[:, 0:n], r=['cst'], w=[key])
    ldcb(c.c2s[:].rearrange("p a b -> p (a b)"), c.CD['c2s'].rearrange("p a b -> p (a b)"), 128, 'c2s')
    ldcb(c.cm2[:].rearrange("p a b -> p (a b)"), c.CD['cm2'].rearrange("p a b -> p (a b)"), 512, 'cm2')
    ldcb(c.wm2[:].rearrange("p a b -> p (a b)"), c.CD['wm2'].rearrange("p a b -> p (a b)"), 1536, 'wm2')
    cp(k, 'dve', c.peb[:], c.sp('pe'), r=['sp'], w=['peb'])
    k.dma('sp', c.ccos[:], c.CD['ccos'][:, :], w=['ccos'], sem='ccos')
    k.dma('sp', c.csin[:], c.CD['csin'][:, :], w=['csin'], sem='csin')
    load_w(k, c, c.cw1[:].rearrange("p l o -> p (l o)"), 'cw1', 0, 2048, 'cw1')
    load_w(k, c, c.cw2[:].rearrange("p a m -> p (a m)"), 'cw2', 0, 512, 'cw2')
    k.op('pool', lambda e: e.memset(c.VS[:, :, 64:128], 1.0), w=['VSo'])
    k.op('pool', lambda e: e.memset(c.VW[:, :, 64:128], 1.0), w=['VWo'])
    k.op('pool', lambda e: e.memset(c.VC[:, :, 64:128], 1.0), w=['VCo'])
    k.op('pool', lambda e: e.memset(c.KC2[:], 0.0), w=['KC2'])
    k.op('pool', lambda e: e.memset(c.GH[:], 0.0), w=['GH'])
    k.op('pool', lambda e: e.memset(c.tiny[:], 1e-30), w=['tiny'])
    for g in range(4):
        for qc in range(2):
            k.dma('sp', c.Q[:, qc, :], c.QS[(2 * g + qc) * 128:(2 * g + qc + 1) * 128, :], w=['Q'], sem='Q')
        k.dma('sp', c.KS2[:], c.KSD[g * 128:(g + 1) * 128, :], w=['KS2'], sem='KS2')
        k.dma('sp', c.KW2[:], c.KWD[g * 128:(g + 1) * 128, :], w=['KW2'], sem='KW2')
        k.dma('sp', c.KVC[:], c.KVCD[g * 128:(g + 1) * 128, :], w=['KVC'], sem='KVC')
        vd = c.VD[g].rearrange("(kt p) d -> p kt d", p=128)
        k.dma('sp', c.VS[:, :, 0:64], vd[:, :, 0:64], w=['VS'], sem='VS')
        k.dma('sp', c.VW[:, :, 0:64], vd[:, :, 64:128], w=['VW'], sem='VW')
        for l in range(32):
            mm(k, c.pst[0][0:64, 0:1], c.cw1[0:64, l, :], c.peb[0:64, l:l + 1], l == 0, l == 31, r=['cw1', 'peb'], w=['pss0'])
        for l in range(32):
            mm(k, c.pst[1][64:128, 0:1], c.cw1[64:128, l, :], c.peb[64:128, l:l + 1], l == 0, l == 31, r=['cw1', 'peb'], w=['pss0'])
        cp(k, 'dve', c.cbias[0:64, :], c.pst[0][0:64, 0:1], r=['pss0'], w=['cbias'])
        cp(k, 'dve', c.cbias[64:128, :], c.pst[1][64:128, 0:1], r=['pss0'], w=['cbias'])
        for l in range(32):
            mm(k, c.pst[2][0:64, 0:255], c.cw1[0:64, l, :], c.KVC[0:64, l:l + 4065:16], l == 0, l == 31, r=['cw1', 'KVC'], w=['pss1'])
        for l in range(32):
            mm(k, c.pst[3][64:128, 0:255], c.cw1[64:128, l, :], c.KVC[64:128, l:l + 4065:16], l == 0, l == 31, r=['cw1', 'KVC'], w=['pss1'])
        act(k, c.GH[0:64, 0:255], c.pst[2][0:64, 0:255], AF.Gelu_apprx_tanh, r=['pss1', 'cbias'], w=['GH'], bias=c.cbias[0:64, 0:1], scale=1.0)
        act(k, c.GH[64:128, 0:255], c.pst[3][64:128, 0:255], AF.Gelu_apprx_tanh, r=['pss1', 'cbias'], w=['GH'], bias=c.cbias[64:128, 0:1], scale=1.0)
        mm(k, c.pst[0][:, 0:256], c.cw2[0:64, 0, :], c.GH[0:64, :], True, True, r=['cw2', 'GH'], w=['pss0'])
        mm(k, c.pst[1][:, 0:256], c.cw2[0:64, 1, :], c.GH[0:64, :], True, True, r=['cw2', 'GH'], w=['pss0'])
        tt(k, 'dve', c.t1[:, 0:256], c.pst[0][:, 0:256], c.ccos[:], ALU.mult, r=['pss0', 'ccos'], w=['t1'])
        tt(k, 'dve', c.t2[:, 0:256], c.pst[1][:, 0:256], c.csin[:], ALU.mult, r=['pss0', 'csin'], w=['t2'])
        tt(k, 'pool', c.KC2[:], c.t1[:, 0:256], c.t2[:, 0:256], ALU.add, r=['t1', 't2'], w=['KC2'])
        for nt in range(2):
            mm(k, c.pst[2 + nt][:, 0:64], c.GH[64:128, nt * 128:(nt + 1) * 128], c.cw2[64:128, 2, 0:64], True, True,
               r=['GH', 'cw2'], w=['pss1'])
            cp(k, 'dve', c.VC[:, nt, 0:64], c.pst[2 + nt][:, 0:64], r=['pss1'], w=['VC'])
        LA = 3
        scnt = [0]
        ecnt = [0]

        def prefetch_tabs(qb):
            sl = qb % 2
            qs_ = slice(qb * QB, (qb + 1) * QB)
            for nt in range(2):
                k.dma('sp', c.visf[sl][:, nt, :], c.CD['vis'][:, nt, qs_], w=[('visf', sl, nt)], sem=('visf', sl, nt))
            k.dma('sp', c.tkc[sl][:], c.CD['tkC'][:, 2 * qb:2 * qb + 2, :], w=[('tkc', sl)], sem=('tkc', sl))
            k.dma('sp', c.tka[sl][:], c.CD['tkA'][:, 2 * qb:2 * qb + 2, :], w=[('tka', sl)], sem=('tka', sl))

        def prefetch_gbt(qb):
            sl = qb % 2
            qs_ = slice(qb * QB, (qb + 1) * QB)
            gv = c.GSD[12 * g:12 * g + 12, qs_].rearrange("(hp b br) q -> br b hp q", hp=2, b=2, br=3)
            for br in range(3):
                for b in range(2):
                    k.dma('sp', c.GBT[sl][:, br, b], gv[br:br + 1, b].broadcast_to([64, 2, QB]), w=[('GBT', sl, br)], sem=('GBT', sl, br))

        def epilogue(br, qbx):
            osl = qbx % 2
            first = (br == 0)
            act(k, c.rec[64:128, :], c.osb[64:128, :], AF.Ln, r=['osb', 'tiny'], w=['rec'], bias=c.tiny[64:128, 0:1], scale=1.0)
            act(k, c.rec[0:64, :], c.rec[64:128, :], AF.Exp, r=['rec'], w=['rec'], scale=-1.0)
            gt = c.GBT[osl][:, br].rearrange("p b h q -> p (b h q)")
            tt(k, 'dve', c.fg[0:64, :], c.rec[0:64, :], gt, ALU.mult, r=['rec', ('GBT', osl, br)], w=['fg'])
            og = c.OG[osl][0:64, :]
            if first:
                tt(k, 'dve', og, c.osb[0:64, :], c.fg[0:64, :], ALU.mult, r=['osb', 'fg'], w=[('OG', osl)])
            else:
                tt(k, 'dve', c.tmp[0:64, :], c.osb[0:64, :], c.fg[0:64, :], ALU.mult, r=['osb', 'fg'], w=['tmp'])
                tt(k, 'pool', og, og, c.tmp[0:64, :], ALU.add, r=[('OG', osl), 'tmp'], w=[('OG', osl)])

        def mk_job(qbx, br, K2, kcols, Vt, mask, first, last, nt=None):
            st = {}
            qs = slice(qbx * QB, (qbx + 1) * QB)
            osl = qbx % 2

            def front():
                slot = scnt[0] % 2; scnt[0] += 1
                ps, pk = c.pss[slot], 'pss%d' % slot
                for b in range(2):
                    for hp in range(2):
                        mm(k, ps[:, b * 512 + hp * QB:b * 512 + (hp + 1) * QB], K2[b * 64:(b + 1) * 64, kcols],
                           c.Q[b * 64:(b + 1) * 64, hp, qs], True, True, r=['KS2', 'KW2', 'KC2', 'Q'], w=[pk])
                if br == 0:
                    E, ek = c.EC[nt], ('EC', nt)
                else:
                    es = ecnt[0] % 4; ecnt[0] += 1
                    E, ek = c.EB[es], ('EB', es)
                act(k, E[:], ps[:, :], AF.Exp, r=[pk], w=[ek])
                if mask is not None:
                    mt, mkey = mask
                    mb = mt.unsqueeze(1).broadcast_to([128, 4, QB])
                    ev = E[:].rearrange("p (h q) -> p h q", h=4)
                    tt(k, 'dve', ev, ev, mb, ALU.mult, r=[ek] + (list(mkey) if isinstance(mkey, list) else [mkey]), w=[ek])
                st['E'] = (E, ek)

            def back():
                E, ek = st['E']
                for b in range(2):
                    mm(k, c.pso[:, b * 512:(b + 1) * 512], Vt, E[:, b * 512:(b + 1) * 512], first, last,
                       r=['VS', 'VSo', 'VW', 'VWo', 'VC', 'VCo', ek], w=['pso%d' % b])
                if last:
                    cp(k, 'act', c.osb[:], c.pso[:, :], r=['pso0', 'pso1'], w=['osb'])
                    epilogue(br, qbx)
                    if br == 0:
                        cp(k, 'pool', c.recD[64:128, :], c.rec[0:64, :], r=['rec'], w=['recD'])
                        cp(k, 'pool', c.recD[0:64, :], c.rec[0:64, :], r=['rec'], w=['recD'])
                        for nt2 in range(2):
                            tt(k, 'dve' if nt2 == 0 else 'pool', c.PN[nt2][:], c.EC[nt2][:], c.recD[:], ALU.mult,
                               r=[('EC', nt2), 'recD'], w=[('PN', nt2)])
            return front, back

        def topk_q(qbx, qt):
            tsl = qbx % 2
            pi = c.pst[6][:, 0:64]; pik = 'ps6'
            i = 0
            for b in range(2):
                for hp in range(2):
                    for nt in range(2):
                        o_ = b * 512 + hp * QB + qt * 128
                        mm(k, pi, c.PN[nt][:, o_:o_ + 128], c.c2s[:, nt, :], i == 0, i == 7, r=[('PN', nt), 'c2s'], w=[pik])
                        i += 1
            IM, IM2, m8 = c.IMs[qt], c.IM2s[qt], c.m8s[qt]
            tt(k, 'dve', IM[:], pi, c.tkc[tsl][:, qt, :], ALU.mult, r=[pik, ('tkc', tsl)], w=[('IM', qt)])
            tt(k, 'dve', IM[:], IM[:], c.tka[tsl][:, qt, :], ALU.add, r=[('IM', qt), ('tka', tsl)], w=[('IM', qt)])
            k.op('dve', lambda e, IM=IM, m8=m8: e.max(out=m8[:, 0:8], in_=IM[:]), r=[('IM', qt)], w=[('m8', qt)])
            k.op('dve', lambda e, IM=IM, IM2=IM2, m8=m8: e.match_replace(out=IM2[:], in_to_replace=m8[:, 0:8], in_values=IM[:], imm_value=-3e38),
                 r=[('IM', qt), ('m8', qt)], w=[('IM2', qt)])
            k.op('dve', lambda e, IM2=IM2, m8=m8: e.max(out=m8[:, 8:16], in_=IM2[:]), r=[('IM2', qt)], w=[('m8', qt)])
            ts(k, 'dve', c.SELMs[qt][:], IM[:], m8[:, 15:16], None, ALU.is_ge, None, r=[('IM', qt), ('m8', qt)], w=[('SELM', qt)])

        def topk_b(qbx):
            msl = qbx % 2
            qs = slice(qbx * QB, (qbx + 1) * QB)
            for qt in range(2):
                k.op('pe', lambda e, qt=qt: e.transpose(c.psb[0:64, qt * 128:(qt + 1) * 128], c.SELMs[qt][:], c.identbN[:]),
                     r=[('SELM', qt), 'identb'], w=['psb'])
            cp(k, 'dve', c.SELT[0:64, :], c.psb[0:64, 0:256], r=['psb'], w=['SELT'])
            k.dma('sp', c.SELD[g, :, qs], c.SELT[0:64, :], r=['SELT'], w=[('seld', g, qbx)], sem='SELT')
            nkt_ = 2 * qbx + 2
            seld_v = c.SELD[g].rearrange("(kt two) q -> two kt q", two=2)
            for two in range(2):
                k.dma('sp', c.MB[msl][two * 64:(two + 1) * 64, 0:nkt_, :],
                      seld_v[two:two + 1, 0:nkt_, qs].broadcast_to([64, nkt_, QB]),
                      r=[('seld', g, qbx)], w=[('MB', msl, two)], sem=('MB', msl, two))

        def diag_mask(qbx):
            msl = qbx % 2
            for o in range(2):
                tt(k, 'pool', c.MB[msl][:, 2 * qbx + o, :], c.MB[msl][:, 2 * qbx + o, :], c.cm2[:, o, :], ALU.mult,
                   r=[('MB', msl, 0), ('MB', msl, 1), 'cm2'], w=[('MB', msl, 0), ('MB', msl, 1)])

        def cmp_jobs(qbx):
            tsl = qbx % 2
            cp(k, 'pool', c.visb[:].rearrange("p a b -> p (a b)"), c.visf[tsl][:].rearrange("p a b -> p (a b)"),
               r=[('visf', tsl, 0), ('visf', tsl, 1)], w=['visb'])
            return [mk_job(qbx, 0, c.KC2, slice(nt * 128, (nt + 1) * 128), c.VC[:, nt, :], (c.visb[:, nt, :], 'visb'),
                           nt == 0, nt == 1, nt=nt) for nt in range(2)]

        def run_jobs(jobs, hooks):
            nj = len(jobs)
            for i in range(nj + LA):
                for h in hooks.pop(i, []):
                    h()
                if i < nj:
                    jobs[i][0]()
                if i >= LA:
                    jobs[i - LA][1]()
            for i in sorted(hooks):
                for h in hooks[i]:
                    h()

        prefetch_tabs(0)
        prefetch_tabs(1)
        prefetch_gbt(0)
        run_jobs(cmp_jobs(0), {})
        topk_q(0, 0)
        topk_q(0, 1)
        topk_b(0)
        for qb in range(NQB):
            qs = slice(qb * QB, (qb + 1) * QB)
            if qb + 2 < NQB:
                prefetch_tabs(qb + 2)
            if qb + 1 < NQB:
                prefetch_gbt(qb + 1)
            jobs = []
            kts = [(o, 2 * qb - 4 + o) for o in range(6) if 2 * qb - 4 + o >= 0]
            for idx, (o, kt) in enumerate(kts):
                jobs.append(mk_job(qb, 2, c.KW2, slice(kt * 128, (kt + 1) * 128), c.VW[:, kt, :],
                                   None if o in (2, 3) else (c.wm2[:, o, :], 'wm2'), idx == 0, idx == len(kts) - 1))
            hooks = {}
            if qb + 1 < NQB:
                jobs += cmp_jobs(qb + 1)
                hooks.setdefault(len(jobs) + LA + 5, []).append(lambda q1=qb + 1: topk_q(q1, 0))
                hooks.setdefault(len(jobs) + LA + 8, []).append(lambda q1=qb + 1: topk_q(q1, 1))
                hooks.setdefault(len(jobs) + LA + 11, []).append(lambda q1=qb + 1: topk_b(q1))
            msl = qb % 2
            nkt = 2 * qb + 2
            hooks.setdefault(len(jobs), []).append(lambda q0=qb: diag_mask(q0))
            for kt in range(nkt):
                jobs.append(mk_job(qb, 1, c.KS2, slice(kt * 128, (kt + 1) * 128), c.VS[:, kt, :],
                                   (c.MB[msl][:, kt, :], [('MB', msl, 0), ('MB', msl, 1)]), kt == 0, kt == nkt - 1))
            run_jobs(jobs, hooks)
            osl = qb % 2
            for b in range(2):
                cp(k, 'pool', c.OGb[b * 64:(b + 1) * 64, :], c.OG[osl][0:64, b * 512:(b + 1) * 512], r=[('OG', osl)], w=['OGb'])
            for hp in range(2):
                k.dma('sp', c.OT[(2 * g + hp) * 128:(2 * g + hp + 1) * 128, qs], c.OGb[:, hp * QB:(hp + 1) * QB], r=['OGb'], w=[], sem='OGb')


def alloc_global(k, c):
    c.sq = k.sb('sq', [128, 8, TC], F32)
    c.rs = k.sb('rs', [128, TC], F32)
    c.spt = k.sb('spt', [128, c.NSP], F32)
    c.ones32 = k.sb('ones32', [128, 128], F32)
    c.onesf = k.sb('onesf', [128, 128], F32)
    c.epsc = k.sb('epsc', [128, 1], F32)
    c.onec = k.sb('onec', [128, 1], F32)
    c.psbig = k.ps('psbig', [128, 3072], F32)
    c.pst = [c.psbig[:, i * 512:(i + 1) * 512] for i in range(6)] + [k.ps('ps6', [128, 512], F32)]
    c.pss = [c.psbig[:, 0:1024], c.psbig[:, 1024:2048]]
    c.pso = c.psbig[:, 2048:3072]
    c.psb = k.ps('psb', [128, 1024], BF16)
    c.sp = lambda n: c.spt[:, c.soffs[n][0]:c.soffs[n][0] + c.soffs[n][1]]


class Scope:
    CNT = [0]

    def __init__(self, k):
        self.k = k
        self.es = ExitStack()
        Scope.CNT[0] += 1
        self.id = Scope.CNT[0]

    def sb(self, name, shape, dtype):
        return self.es.enter_context(self.k.nc.sbuf_tensor('%s_s%d' % (name, self.id), list(shape), dtype))

    def close(self):
        self.es.close()


def build(offs, XTOT, soffs, NSP, cshapes, upto=99, dbg=False):
    k = Prog(); c = Ctx()
    c.offs, c.XTOT, c.soffs, c.NSP = offs, XTOT, soffs, NSP
    kind_s = 'ExternalOutput' if dbg else 'Internal'
    c.XT = k.dram('xT', [D, S], F32, 'ExternalInput')
    c.WC = k.dram('wcat', [128, XTOT], F32, 'ExternalInput')
    c.SPD = k.dram('spcat', [128, NSP], F32, 'ExternalInput')
    c.CD = {n: k.dram('c_' + n, list(shp), F32, 'ExternalInput') for n, shp in cshapes.items()}
    c.OUT = k.dram('outT', [D, S], F32, 'ExternalOutput')
    c.WB = k.dram('wb', [128, XTOT], BF16, 'Internal')
    c.H1 = k.dram('h1', [D, S], F32, kind_s)
    c.H2 = k.dram('h2', [D, S], F32, kind_s)
    c.H3 = k.dram('h3', [D, S], F32, kind_s)
    c.OT = k.dram('oT', [D, S], BF16, 'Internal')
    alloc_global(k, c)
    k.op('dve', lambda e: e.memset(c.onec[:], 1.0), w=['onec'])
    phase_consts(k, c)
    phase_weights(k, c, ['in0', 'ga', 'gx', 'out0', 'w1_0', 'w3_0', 'w2_0'])
    hsc = Scope(k)
    c.hn = hsc.sb('hn', [128, 8, S], BF16)
    sc = Scope(k)
    c.xc = [sc.sb('xc%d' % i, [128, 8, TC], F32) for i in range(2)]
    phase_norm0(k, c, c.XT, 'g_attn0')
    barrier(k); sc.close()
    if upto >= 1:
        sc = Scope(k)
        T = 1024
        c.gw = sc.sb('gw', [128, 8, 8, 128], BF16)
        c.gaw = sc.sb('gaw', [128, 4, 128], BF16); c.gxw = sc.sb('gxw', [128, 4, 128], BF16)
        c.cv = sc.sb('cv', [128, 8], F32)
        c.avg = sc.sb('avg', [128, 128], F32); c.kdec = sc.sb('kdec', [128, 8], F32); c.cdec = sc.sb('cdec', [128, 4], F32)
        c.identf = sc.sb('identf', [128, 128], F32); c.identb = sc.sb('identb', [128, 128], BF16)
        c.qdec = sc.sb('qdec', [128, 512], F32); c.dm = sc.sb('dm', [128, 2, 128], F32)
        c.XR = sc.sb('XR', [128, T + 4], F32); c.Y = sc.sb('Y', [128, T], F32); c.XC = sc.sb('XC', [128, T], F32)
        c.XCb = sc.sb('XCb', [128, T], BF16); c.R = sc.sb('R', [128, T], F32); c.I = sc.sb('I', [128, T], F32)
        c.A = sc.sb('A', [128, T], F32); c.hl = sc.sb('hl', [128, 1], F32); c.ob = sc.sb('ob', [128, T], BF16)
        c.rt = [sc.sb('rt%d' % i, [128, TC], F32) for i in range(8)]
        c.t1 = sc.sb('t1', [128, TC], F32); c.t2 = sc.sb('t2', [128, TC], F32)
        c.QR = sc.sb('QR', [128, TC], BF16); c.QD = sc.sb('QD', [128, TC], BF16); c.KR = sc.sb('KR', [128, TC], BF16)
        c.KRA = sc.sb('KRA', [128, TC], BF16); c.KRB = sc.sb('KRB', [128, TC], BF16)
        c.SG = sc.sb('SG', [128, TC], F32); c.VT = sc.sb('VT', [128, 4, 128], BF16); c.KD = sc.sb('KD', [128, 4, 128], BF16)
        c.SD = sc.sb('SD', [128, 2, 128], BF16); c.ST32 = sc.sb('ST32', [128, 64], F32); c.STp = sc.sb('STp', [128, 2, 64], BF16)
        c.O32 = sc.sb('O32', [128, TC], F32); c.SQr = sc.sb('SQr', [128, TC], F32); c.ob2 = sc.sb('ob2', [128, TC], BF16)
        phase_mix0(k, c)
        barrier(k); sc.close()
    if upto >= 2:
        sc = Scope(k)
        c.xc = [sc.sb('xc%d' % i, [128, 8, TC], F32) for i in range(2)]
        c.oc = [sc.sb('oc%d' % i, [128, 8, TC], BF16) for i in range(2)]
        c.wout = sc.sb('wout', [128, 8, 8, 128], BF16)
        phase_weights(k, c, ['in1', 'out1', 'cw1', 'cw2', 'w1_1', 'w3_1', 'w2_1'])
        phase_outproj(k, c, 'out0', c.XT, c.H1, 'g_ffn0')
        barrier(k); sc.close()
    if upto >= 3:
        sc = Scope(k)
        c.xc = [sc.sb('xc%d' % i, [128, 8, TC], F32) for i in range(2)]
        c.w13 = [sc.sb('w13_%d' % i, [128, 2, 8, 128], BF16) for i in range(2)]
        c.w2t = [sc.sb('w2t%d' % i, [128, 22, 128], BF16) for i in range(2)]
        c.sil = [sc.sb('sil%d' % i, [128, TC], F32) for i in range(2)]
        c.hid = sc.sb('hid', [128, 22, 1024], BF16)
        phase_ffn(k, c, 0, c.H1, c.H2, 'g_attn1', final=False)
        barrier(k); sc.close()
    if upto >= 4:
        c.QS = k.dram('qs', [D, S], BF16, 'Internal')
        c.KSD = k.dram('ksd', [512, S], BF16, 'Internal')
        c.KWD = k.dram('kwd', [512, S], BF16, 'Internal')
        c.KVCD = k.dram('kvcd', [512, S], BF16, 'Internal')
        c.VD = k.dram('vd', [4, S, 128], BF16, 'Internal')
        c.GSD = k.dram('gsd', [128, S], BF16, 'Internal')
        c.SELD = k.dram('seld', [4, 64, S], BF16, 'Internal')
        sc = Scope(k)
        c.gw4 = [sc.sb('gw4_%d' % i, [128, 8, 128], BF16) for i in range(4)]
        c.rt = [sc.sb('rt%d' % i, [128, TC], F32) for i in range(4)]
        c.t1 = sc.sb('t1', [128, TC], F32); c.t2 = sc.sb('t2', [128, TC], F32)
        c.obn = [sc.sb('obn%d' % i, [128, TC], BF16) for i in range(2)]
        phase_nsa_a(k, c)
        barrier(k); sc.close(); hsc.close()
        sc = Scope(k)
        QB = 256
        c.Q = sc.sb('Q', [128, 2, S], BF16); c.KS2 = sc.sb('KS2', [128, S], BF16); c.KW2 = sc.sb('KW2', [128, S], BF16)
        c.KVC = sc.sb('KVC', [128, S], BF16); c.VS = sc.sb('VS', [128, 32, 128], BF16); c.VW = sc.sb('VW', [128, 32, 128], BF16)
        c.MB = [sc.sb('MB%d' % i, [128, 32, QB], BF16) for i in range(2)]
        c.EB = [sc.sb('EB%d' % i, [128, 1024], BF16) for i in range(4)]
        c.EC = [sc.sb('EC%d' % i, [128, 1024], BF16) for i in range(2)]
        c.PN = [sc.sb('PN%d' % i, [128, 1024], BF16) for i in range(2)]
        c.rec = sc.sb('rec', [128, 1024], F32); c.recD = sc.sb('recD', [128, 1024], F32); c.fg = sc.sb('fg', [128, 1024], F32)
        c.tmp = sc.sb('tmp', [128, 1024], F32); c.OG = [sc.sb('OG%d' % i, [128, 1024], F32) for i in range(2)]; c.GBT = [sc.sb('GBT%d' % i, [64, 3, 2, 2, QB], BF16) for i in range(2)]; c.tiny = sc.sb('tiny', [128, 1], F32); c.osb = sc.sb('osb', [128, 1024], F32); c.OGb = sc.sb('OGb', [128, 512], BF16)
        c.visf = [sc.sb('visf%d' % i, [128, 2, QB], F32) for i in range(2)]; c.visb = sc.sb('visb', [128, 2, QB], BF16)
        c.tkc = [sc.sb('tkc%d' % i, [128, 2, 64], F32) for i in range(2)]; c.tka = [sc.sb('tka%d' % i, [128, 2, 64], F32) for i in range(2)]
        c.IMs = [sc.sb('IM%d' % i, [128, 64], F32) for i in range(2)]; c.IM2s = [sc.sb('IM2%d' % i, [128, 64], F32) for i in range(2)]; c.m8s = [sc.sb('m8%d' % i, [128, 16], F32) for i in range(2)]
        c.SELMs = [sc.sb('SELM%d' % i, [128, 64], BF16) for i in range(2)]; c.SELT = sc.sb('SELT', [128, QB], BF16)
        c.identfN = sc.sb('identf', [128, 128], F32); c.identbN = sc.sb('identb', [128, 128], BF16)
        c.cst = sc.sb('cst', [128, 1536], F32)
        c.c2s = sc.sb('c2s', [128, 2, 64], BF16); c.cm2 = sc.sb('cm2', [128, 2, QB], BF16); c.wm2 = sc.sb('wm2', [128, 6, QB], BF16)
        c.peb = sc.sb('peb', [128, 32], BF16); c.ccos = sc.sb('ccos', [128, 256], F32); c.csin = sc.sb('csin', [128, 256], F32)
        c.cw1 = sc.sb('cw1', [128, 32, 64], BF16); c.cw2 = sc.sb('cw2', [128, 4, 128], BF16)
        c.cbias = sc.sb('cbias', [128, 1], F32); c.GH = sc.sb('GH', [128, 256], BF16); c.KC2 = sc.sb('KC2', [128, 256], BF16)
        c.VC = sc.sb('VC', [128, 2, 128], BF16)
        c.t1 = sc.sb('t1', [128, TC], F32); c.t2 = sc.sb('t2', [128, TC], F32)
        phase_nsa_b(k, c)
        barrier(k); sc.close()
    if upto >= 5:
        hsc = Scope(k)
        c.hn = hsc.sb('hn', [128, 8, S], BF16)
        sc = Scope(k)
        c.xc = [sc.sb('xc%d' % i, [128, 8, TC], F32) for i in range(2)]
        c.oc = [sc.sb('oc%d' % i, [128, 8, TC], BF16) for i in range(2)]
        c.wout = sc.sb('wout', [128, 8, 8, 128], BF16)
        phase_outproj(k, c, 'out1', c.H2, c.H3, 'g_ffn1')
        barrier(k); sc.close()
    if upto >= 6:
        sc = Scope(k)
        c.xc = [sc.sb('xc%d' % i, [128, 8, TC], F32) for i in range(2)]
        c.w13 = [sc.sb('w13_%d' % i, [128, 2, 8, 128], BF16) for i in range(2)]
        c.w2t = [sc.sb('w2t%d' % i, [128, 22, 128], BF16) for i in range(2)]
        c.sil = [sc.sb('sil%d' % i, [128, TC], F32) for i in range(2)]
        c.hid = sc.sb('hid', [128, 22, 1024], BF16)
        phase_ffn(k, c, 1, c.H3, c.OUT, 'g_fin', final=True)
        barrier(k); sc.close()
    c.upto = upto
    try:
        hsc.close()
    except Exception:
        pass
    return k, c


_CACHE = {}


def kernel(**inputs):
    inp = {n: np.asarray(v) for n, v in inputs.items()}
    wcat, offs, spcat, soffs = host_prep(inp)
    C = const_tables2(const_tables())
    k, c = build(offs, wcat.shape[1], soffs, spcat.shape[1], {n: a.shape for n, a in C.items()}, upto=6, dbg=False)
    nc = k.finish()
    x = inp['x']
    B = x.shape[0]
    shared = {"wcat": wcat, "spcat": spcat}
    for n, a in C.items():
        shared['c_' + n] = np.ascontiguousarray(a)
    in_maps = []
    for b in range(B):
        m = dict(shared)
        m["xT"] = np.ascontiguousarray(x[b].T)
        in_maps.append(m)
    res = run_bass_kernel_spmd(nc, in_maps, core_ids=list(range(B)))
    out = np.stack([np.ascontiguousarray(r["outT"].T) for r in res.results], axis=0)
    return out.astype(np.float32)
```
